# Optimizing a Trainium2 kernel written in Bass

```python
import jax, jax.numpy as jnp
from jax import lax
import numpy as np

D_MODEL = 1024
BATCH = 16
SEQ = 256
DEPTH = 4
DEC_BATCH = 4
DEC_SEQ = 1024
PAST_LEN = 512

GRID_W = 64
HEAD_DIM = 64
RET_HEADS = 4
RET_DK = 64
RET_DV = 64
RET_W = RET_HEADS * RET_DV
RET_CHUNK = 128
GQA_HEADS = 6
GQA_KV_HEADS = 2
GQA_GROUP = GQA_HEADS // GQA_KV_HEADS
GQA_W = GQA_HEADS * HEAD_DIM
MLA_HEADS = 6
MLA_NOPE = 64
MLA_ROPE = 32
MLA_V = 64
MLA_Q_RANK = 256
MLA_KV_RANK = 128
MLA_W = MLA_HEADS * MLA_V
MIX_W = RET_W + GQA_W + MLA_W
Q_BLOCK = 128
ROPE_THETA = 10000.0
EPS = 1e-6
IN_SPLITS = (RET_W, RET_W, RET_W,
             GQA_HEADS * HEAD_DIM, GQA_KV_HEADS * HEAD_DIM, GQA_KV_HEADS * HEAD_DIM,
             MLA_Q_RANK, MLA_KV_RANK, MLA_ROPE,
             MIX_W)
IN_W = sum(IN_SPLITS)

kernel_name = 'hybrid_retention_gqa_mla_prefix_dit'

f32 = jnp.float32


def rms_norm(x, g=None):
    xf = x.astype(f32)
    y = xf * lax.rsqrt(jnp.mean(xf * xf, axis=-1, keepdims=True) + EPS)
    if g is not None:
        y = y * g.astype(f32)
    return y.astype(x.dtype)


def axial_rope(n_tok, dim):
    rows = n_tok // GRID_W
    row = jnp.repeat(jnp.arange(rows), GRID_W).astype(f32)
    col = (jnp.arange(n_tok) % GRID_W).astype(f32)
    half = dim // 2
    inv = ROPE_THETA ** (-jnp.arange(0, half, 2, dtype=f32) / half)
    ar = row[:, None] * inv[None, :]
    ac = col[:, None] * inv[None, :]
    ang = jnp.concatenate([ar, ar, ac, ac], axis=-1)
    return jnp.cos(ang), jnp.sin(ang)


def apply_rope(x, cos, sin):
    d = x.shape[-1]
    half = d // 2
    qd = half // 2
    xr, xc = x[..., :half], x[..., half:]
    rot = jnp.concatenate([-xr[..., qd:], xr[..., :qd], -xc[..., qd:], xc[..., :qd]], axis=-1)
    out = x.astype(f32) * cos[:, None, :] + rot.astype(f32) * sin[:, None, :]
    return out.astype(x.dtype)


def retention_chunkwise(q, k, v, log_g, s0):
    B, L, H, _ = q.shape
    dv = v.shape[-1]
    n = L // RET_CHUNK
    q, k, v = q.astype(f32), k.astype(f32), v.astype(f32)
    pos = jnp.arange(RET_CHUNK, dtype=f32)
    rel = pos[:, None] - pos[None, :]
    intra = jnp.where(rel >= 0, jnp.exp(jnp.maximum(rel, 0.0)[None] * log_g[:, None, None]), 0.0)
    q_dec = jnp.exp((pos + 1.0)[:, None] * log_g[None, :])
    k_dec = jnp.exp((RET_CHUNK - 1.0 - pos)[:, None] * log_g[None, :])
    c_dec = jnp.exp(RET_CHUNK * log_g)

    def to_chunks(a):
        return a.reshape(B, n, RET_CHUNK, H, a.shape[-1]).transpose(1, 0, 2, 3, 4)

    def step(s, inp):
        qi, ki, vi = inp
        a = jnp.einsum('bihd,bjhd->bhij', qi, ki) * intra
        o = (jnp.einsum('bhij,bjhe->bihe', a, vi)
             + jnp.einsum('bihd,bhde->bihe', qi, s) * q_dec[None, :, :, None])
        s = s * c_dec[None, :, None, None] + jnp.einsum('bjhd,bjhe->bhde', ki * k_dec[None, :, :, None], vi)
        return s, o

    s, o = lax.scan(step, s0.astype(f32), (to_chunks(q), to_chunks(k), to_chunks(v)))
    return o.transpose(1, 0, 2, 3, 4).reshape(B, L, H, dv), s


def bidir_retention(q, k, v, log_g, s0_f, s0_b):
    B, L = q.shape[:2]
    o_f, s_f = retention_chunkwise(q, k, v, log_g[0], s0_f)
    o_b, s_b = retention_chunkwise(jnp.flip(q, 1), jnp.flip(k, 1), jnp.flip(v, 1), log_g[1], s0_b)
    o = rms_norm(o_f + jnp.flip(o_b, 1))
    return o.reshape(B, L, RET_W).astype(v.dtype), jnp.stack([s_f, s_b], axis=1).astype(v.dtype)


def block_attention(q, k, v, scale):
    B, Lq, KH, G, dq = q.shape
    dv = v.shape[-1]
    nb = Lq // Q_BLOCK
    qb = q.reshape(B, nb, Q_BLOCK, KH, G, dq).transpose(1, 0, 2, 3, 4, 5)

    def one_block(qi):
        s = jnp.einsum('bqkgd,bskd->bkgqs', qi, k).astype(f32) * scale
        p = jax.nn.softmax(s, axis=-1).astype(v.dtype)
        return jnp.einsum('bkgqs,bskd->bqkgd', p, v)

    o = lax.map(one_block, qb)
    return o.transpose(1, 0, 2, 3, 4, 5).reshape(B, Lq, KH, G, dv)


def gqa_attend(q, k, v):
    B, L = q.shape[:2]
    qg = q.reshape(B, L, GQA_KV_HEADS, GQA_GROUP, HEAD_DIM)
    return block_attention(qg, k, v, HEAD_DIM ** -0.5).reshape(B, L, GQA_W)


def mla_queries(cq, p):
    B, L = cq.shape[:2]
    qf = (rms_norm(cq, p['mla_q_norm']) @ p['w_uq']).reshape(B, L, MLA_HEADS, MLA_NOPE + MLA_ROPE)
    return qf[..., :MLA_NOPE], qf[..., MLA_NOPE:]


def mla_attend(q_nope, q_pe, ckv, kpe, w_ukv):
    B, Lq = q_nope.shape[:2]
    Lk = ckv.shape[1]
    kv = (ckv @ w_ukv).reshape(B, Lk, MLA_HEADS, MLA_NOPE + MLA_V)
    k = jnp.concatenate([kv[..., :MLA_NOPE],
                         jnp.broadcast_to(kpe[:, :, None, :], (B, Lk, MLA_HEADS, MLA_ROPE))], axis=-1)
    v = kv[..., MLA_NOPE:]
    q = jnp.concatenate([q_nope, q_pe], axis=-1)[:, :, :, None, :]
    o = block_attention(q, k, v, (MLA_NOPE + MLA_ROPE) ** -0.5)
    return o.reshape(B, Lq, MLA_W)


def _pre(x, cond, p):
    mod = (jax.nn.silu(cond) @ p['w_ada'] + p['b_ada'])[:, None, :]
    shift, scale, gate = jnp.split(mod, 3, axis=-1)
    h = rms_norm(x, p['norm_g']) * (1 + scale) + shift
    idx = np.cumsum(IN_SPLITS)[:-1].tolist()
    parts = jnp.split(h @ p['w_in'], idx, axis=-1)
    return parts, gate


def _post(x, o_ret, o_gqa, o_mla, z_gate, g_res, w_out):
    y = jnp.concatenate([o_ret, o_gqa, o_mla], axis=-1) * jax.nn.silu(z_gate)
    return x + g_res * (y @ w_out)


def context_layer(x, c_ctx, p):
    B, L, _ = x.shape
    (rq, rk, rv, gq, gk, gv, cq, ckv, kpe, z_gate), g_res = _pre(x, c_ctx[None, :], p)
    q = rq.reshape(B, L, RET_HEADS, RET_DK)
    k = rk.reshape(B, L, RET_HEADS, RET_DK) * (RET_DK ** -0.5)
    v = rv.reshape(B, L, RET_HEADS, RET_DV)
    log_g = jax.nn.log_sigmoid(p['ret_decay_logit'].astype(f32))
    s0 = jnp.zeros((B, RET_HEADS, RET_DK, RET_DV), f32)
    o_ret, ret_state = bidir_retention(q, k, v, log_g, s0, s0)
    gq = rms_norm(gq.reshape(B, L, GQA_HEADS, HEAD_DIM), p['gqa_q_norm'])
    gk = rms_norm(gk.reshape(B, L, GQA_KV_HEADS, HEAD_DIM), p['gqa_k_norm'])
    gv = gv.reshape(B, L, GQA_KV_HEADS, HEAD_DIM)
    o_gqa = gqa_attend(gq, gk, gv)
    ckv = rms_norm(ckv, p['mla_kv_norm'])
    q_nope, q_pe = mla_queries(cq, p)
    o_mla = mla_attend(q_nope, q_pe, ckv, kpe, p['w_ukv'])
    x = _post(x, o_ret, o_gqa, o_mla, z_gate, g_res, p['w_out'])
    return x, (ret_state, gk, gv, ckv, kpe)


def latent_layer(x, c, p, cache, rope_hd, rope_mla):
    state, ck, cv, cckv, ckpe = cache
    B, L, _ = x.shape
    (rq, rk, rv, gq, gk, gv, cq, ckv, kpe, z_gate), g_res = _pre(x, c, p)
    cos, sin = rope_hd
    q = apply_rope(rq.reshape(B, L, RET_HEADS, RET_DK), cos, sin)
    k = apply_rope(rk.reshape(B, L, RET_HEADS, RET_DK), cos, sin) * (RET_DK ** -0.5)
    v = rv.reshape(B, L, RET_HEADS, RET_DV)
    log_g = jax.nn.log_sigmoid(p['ret_decay_logit'].astype(f32))
    o_ret, _ = bidir_retention(q, k, v, log_g, state[:, 0], state[:, 1])
    gq = apply_rope(rms_norm(gq.reshape(B, L, GQA_HEADS, HEAD_DIM), p['gqa_q_norm']), cos, sin)
    gk = apply_rope(rms_norm(gk.reshape(B, L, GQA_KV_HEADS, HEAD_DIM), p['gqa_k_norm']), cos, sin)
    gv = gv.reshape(B, L, GQA_KV_HEADS, HEAD_DIM)
    o_gqa = gqa_attend(gq, jnp.concatenate([ck, gk], axis=1), jnp.concatenate([cv, gv], axis=1))
    mcos, msin = rope_mla
    ckv = rms_norm(ckv, p['mla_kv_norm'])
    q_nope, q_pe = mla_queries(cq, p)
    q_pe = apply_rope(q_pe, mcos, msin)
    kpe = apply_rope(kpe[:, :, None, :], mcos, msin)[:, :, 0, :]
    o_mla = mla_attend(q_nope, q_pe, jnp.concatenate([cckv, ckv], axis=1),
                       jnp.concatenate([ckpe, kpe], axis=1), p['w_ukv'])
    return _post(x, o_ret, o_gqa, o_mla, z_gate, g_res, p['w_out'])


def setup_inputs(seed: int = 0) -> dict:
    key = jax.random.key(seed)
    ks = jax.random.split(key, 24)

    def nrm(k, shape, s=1.0):
        return jax.random.normal(k, shape, f32) * s

    expo = (5.0 + jnp.arange(RET_HEADS, dtype=f32))[None, None, :] \
        + jnp.array([0.0, 0.5], f32)[None, :, None] + nrm(ks[10], (DEPTH, 2, RET_HEADS), 0.1)
    one_minus = 2.0 ** (-expo)
    ret_decay_logit = jnp.log((1.0 - one_minus) / one_minus)
    return {
        'x_prompt': nrm(ks[0], (BATCH, SEQ, D_MODEL)),
        'x_sample': nrm(ks[1], (DEC_BATCH, DEC_SEQ, D_MODEL)),
        'state_ret': nrm(ks[2], (DEC_BATCH, DEPTH, 2, RET_HEADS, RET_DK, RET_DV)),
        'cache_gqa_k': nrm(ks[3], (DEC_BATCH, DEPTH, PAST_LEN, GQA_KV_HEADS, HEAD_DIM)),
        'cache_gqa_v': nrm(ks[4], (DEC_BATCH, DEPTH, PAST_LEN, GQA_KV_HEADS, HEAD_DIM)),
        'cache_mla_ckv': nrm(ks[5], (DEC_BATCH, DEPTH, PAST_LEN, MLA_KV_RANK)),
        'cache_mla_kpe': nrm(ks[6], (DEC_BATCH, DEPTH, PAST_LEN, MLA_ROPE)),
        'c': nrm(ks[7], (DEC_BATCH, D_MODEL)),
        'c_ctx': nrm(ks[8], (D_MODEL,)),
        'norm_g': 1.0 + nrm(ks[9], (DEPTH, D_MODEL), 0.02),
        'w_ada': nrm(ks[11], (DEPTH, D_MODEL, 3 * D_MODEL), 0.5 * D_MODEL ** -0.5),
        'b_ada': nrm(ks[12], (DEPTH, 3 * D_MODEL), 0.02),
        'w_in': nrm(ks[13], (DEPTH, D_MODEL, IN_W), D_MODEL ** -0.5),
        'ret_decay_logit': ret_decay_logit,
        'gqa_q_norm': 1.0 + nrm(ks[14], (DEPTH, HEAD_DIM), 0.02),
        'gqa_k_norm': 1.0 + nrm(ks[15], (DEPTH, HEAD_DIM), 0.02),
        'mla_q_norm': 1.0 + nrm(ks[16], (DEPTH, MLA_Q_RANK), 0.02),
        'mla_kv_norm': 1.0 + nrm(ks[17], (DEPTH, MLA_KV_RANK), 0.02),
        'w_uq': nrm(ks[18], (DEPTH, MLA_Q_RANK, MLA_HEADS * (MLA_NOPE + MLA_ROPE)), MLA_Q_RANK ** -0.5),
        'w_ukv': nrm(ks[19], (DEPTH, MLA_KV_RANK, MLA_HEADS * (MLA_NOPE + MLA_V)), MLA_KV_RANK ** -0.5),
        'w_out': nrm(ks[20], (DEPTH, MIX_W, D_MODEL), MIX_W ** -0.5),
        'final_norm': 1.0 + nrm(ks[21], (D_MODEL,), 0.02),
    }


def reference(x_prompt, x_sample, state_ret, cache_gqa_k, cache_gqa_v, cache_mla_ckv, cache_mla_kpe,
              c, c_ctx, norm_g, w_ada, b_ada, w_in, ret_decay_logit, gqa_q_norm, gqa_k_norm,
              mla_q_norm, mla_kv_norm, w_uq, w_ukv, w_out, final_norm):
    n_lat = x_sample.shape[1]
    rope_hd = axial_rope(n_lat, HEAD_DIM)
    rope_mla = axial_rope(n_lat, MLA_ROPE)
    xp, xs = x_prompt, x_sample
    st_l, gk_l, gv_l, ckv_l, kpe_l = [], [], [], [], []
    for l in range(DEPTH):
        p = dict(norm_g=norm_g[l], w_ada=w_ada[l], b_ada=b_ada[l], w_in=w_in[l],
                 ret_decay_logit=ret_decay_logit[l], gqa_q_norm=gqa_q_norm[l], gqa_k_norm=gqa_k_norm[l],
                 mla_q_norm=mla_q_norm[l], mla_kv_norm=mla_kv_norm[l], w_uq=w_uq[l], w_ukv=w_ukv[l],
                 w_out=w_out[l])
        xp, (st, gk, gv, ckv, kpe) = context_layer(xp, c_ctx, p)
        st_l.append(st)
        gk_l.append(gk)
        gv_l.append(gv)
        ckv_l.append(ckv)
        kpe_l.append(kpe)
        cache_l = (state_ret[:, l], cache_gqa_k[:, l], cache_gqa_v[:, l], cache_mla_ckv[:, l], cache_mla_kpe[:, l])
        xs = latent_layer(xs, c, p, cache_l, rope_hd, rope_mla)
    y_prompt = rms_norm(xp, final_norm)
    y_sample = rms_norm(xs, final_norm)
    new_state_ret = jnp.stack(st_l, axis=1)
    new_gqa_k = jnp.stack(gk_l, axis=1)
    new_gqa_v = jnp.stack(gv_l, axis=1)
    new_mla_ckv = jnp.stack(ckv_l, axis=1)
    new_mla_kpe = jnp.stack(kpe_l, axis=1)
    return (y_prompt, y_sample, new_state_ret, new_gqa_k, new_gqa_v, new_mla_ckv, new_mla_kpe)
```

```python
import contextlib
import numpy as np
import concourse.bass as bass
import concourse.mybir as mybir
from concourse.bass_utils import run_bass_kernel_spmd

F32 = mybir.dt.float32
BF16 = mybir.dt.bfloat16
F32R = mybir.dt.float32r
AF = mybir.ActivationFunctionType
ALU = mybir.AluOpType
AX = mybir.AxisListType

ENGS = ("pe", "act", "dve", "pool", "sp")
EPS = 1e-6
NEG_BIG = -1.0e4


class Res:
    __slots__ = ("name", "lw", "rd", "excl")

    def __init__(self, name, excl=False):
        self.name = name
        self.lw = None
        self.rd = []
        self.excl = excl


class Op:
    __slots__ = ("eng", "fn", "waits", "idx", "milestone", "dma_sem", "dma_val", "ms_val")

    def __init__(self, eng, fn):
        self.eng = eng
        self.fn = fn
        self.waits = {}
        self.idx = None
        self.milestone = False
        self.dma_sem = None
        self.dma_val = None
        self.ms_val = None


class Prog:
    def __init__(self):
        self.ops = {e: [] for e in ENGS}
        self.known = {e: {} for e in ENGS}
        self.dma_count = {}
        self.dma_sems = []

    def _dep(self, op, key_val, same_ok=False):
        if key_val is None:
            return
        key, val = key_val
        if key == op.eng and (key == "pe" or same_ok):
            return
        if self.known[op.eng].get(key, -1) >= val:
            return
        if op.waits.get(key, -1) < val:
            op.waits[key] = val

    def add(self, eng, fn, reads=(), writes=(), dma_sem=None):
        op = Op(eng, fn)
        op.idx = len(self.ops[eng])
        for r in reads:
            self._dep(op, r.lw)
            if r.excl:
                for rd in r.rd:
                    self._dep(op, rd, same_ok=True)
        for w in writes:
            self._dep(op, w.lw)
            for rd in w.rd:
                self._dep(op, rd)
        kn = self.known[eng]
        for k, v in op.waits.items():
            if kn.get(k, -1) < v:
                kn[k] = v
        if dma_sem is not None:
            if dma_sem not in self.dma_count:
                self.dma_count[dma_sem] = 0
                self.dma_sems.append(dma_sem)
            self.dma_count[dma_sem] += 16
            op.dma_sem = dma_sem
            op.dma_val = self.dma_count[dma_sem]
            me = ("dma:" + dma_sem, op.dma_val)
        else:
            me = (eng, op.idx)
        for r in reads:
            r.rd.append(me)
        for w in writes:
            w.lw = me
            w.rd = []
        self.ops[eng].append(op)
        return op

    def emit(self, nc):
        for e in ENGS:
            for op in self.ops[e]:
                for k, v in op.waits.items():
                    if not k.startswith("dma:"):
                        self.ops[k][v].milestone = True
        for e in ENGS:
            c = 0
            for op in self.ops[e]:
                if op.milestone:
                    c += 1
                    op.ms_val = c
        with contextlib.ExitStack() as st:
            sems = {}
            for e in ENGS:
                sems[e] = st.enter_context(nc.semaphore("s_" + e))
            for d in self.dma_sems:
                sems["dma:" + d] = st.enter_context(nc.semaphore("d_" + d))
            block = st.enter_context(nc.Block())
            prog = self

            def run(eng_name, eng):
                for op in prog.ops[eng_name]:
                    for k, v in op.waits.items():
                        if k.startswith("dma:"):
                            eng.wait_ge(sems[k], v)
                        else:
                            eng.wait_ge(sems[k], prog.ops[k][v].ms_val)
                    ins = op.fn(eng)
                    if op.dma_sem is not None:
                        ins.then_inc(sems["dma:" + op.dma_sem], 16)
                    elif op.milestone:
                        ins.then_inc(sems[eng_name], 1)

            @block.tensor
            def _(eng):
                run("pe", eng)

            @block.scalar
            def _(eng):
                run("act", eng)

            @block.vector
            def _(eng):
                run("dve", eng)

            @block.gpsimd
            def _(eng):
                run("pool", eng)

            @block.sync
            def _(eng):
                run("sp", eng)
                for d in prog.dma_sems:
                    eng.wait_ge(sems["dma:" + d], prog.dma_count[d])


T = 1024
NT = 8
NK = 12
IN_W = 2848


def build(nlayers=4, taps=(), upto=99):
    nc = bass.Bass("TRN2", target_bir_lowering=False)
    P = Prog()
    taps = set(taps)
    tap_out = {}

    def din(name, shape):
        return nc.dram_tensor(name, list(shape), F32, kind="ExternalInput").ap()

    def dout(name, shape):
        return nc.dram_tensor(name, list(shape), F32, kind="ExternalOutput").ap()

    d_x = din("xin", [T, 1024])
    d_cond = din("cond", [8, 128])
    d_cgk = din("c_gk", [4, 512, 128])
    d_cgv = din("c_gv", [4, 512, 128])
    d_cckv = din("c_ckv", [4, 512, 128])
    d_ckpe = din("c_kpe", [4, 512, 32])
    d_st0 = din("state0", [4, 2, 4, 64, 64])
    d_wada = din("w_ada", [4, 1024, 3072])
    d_bada = din("b_ada", [96, 128])
    d_win = din("w_in", [4, 1024, IN_W])
    d_wuq = din("w_uq", [4, 256, 576])
    d_wukv = din("w_ukv", [4, 128, 768])
    d_wout = din("w_out", [4, 1024, 1024])
    d_ng = din("norm_g", [32, 128])
    d_fn = din("final_norm", [8, 128])
    d_logit = din("ret_logit", [32])
    d_gqg = din("gq_g", [256])
    d_gkg = din("gk_g", [256])
    d_mqg = din("mq_g", [1024])
    d_mkvg = din("mkv_g", [512])
    d_ropehd = din("rope_hd", [T, 2, 64])
    d_ropem = din("rope_m", [T, 2, 32])
    d_qmask = din("qmask", [5, T])
    d_kmask = din("kmask", [5, 1536])
    d_carry = din("carry", [8])
    d_nef = din("ne_f", [128, 128])
    d_neb = din("ne_b", [128, 128])
    d_neq = din("ne_q", [256])
    d_nek = din("ne_k", [128, 2])

    o_y = dout("y", [T, 1024])
    o_st = dout("st_out", [4, 4, 64, 512])
    o_gk = dout("gk_out", [4, T, 128])
    o_gv = dout("gv_out", [4, T, 128])
    o_ckv = dout("ckv_out", [4, T, 128])
    o_kpe = dout("kpe_out", [4, T, 32])

    with contextlib.ExitStack() as st:
        cnt = [0]

        def sbt(name, shape, dt):
            return st.enter_context(nc.sbuf_tensor(name, list(shape), dt))

        def R(name, excl=False):
            return Res(name, excl)

        def _flat(x):
            out = []
            for it in x:
                if isinstance(it, (list, tuple)):
                    out.extend(_flat(it))
                else:
                    out.append(it)
            return out

        def A(eng, fn, r=(), w=(), dma=None):
            return P.add(eng, fn, reads=_flat(r), writes=_flat(w), dma_sem=dma)

        def uniq(prefix):
            cnt[0] += 1
            return "%s%d" % (prefix, cnt[0])

        PP = [st.enter_context(nc.psum_tensor("pp%d" % i, [128, 1024], F32)) for i in range(4)]
        PR = [[R("pp%d_%d" % (i, j), excl=True) for j in range(2)] for i in range(4)]

        def bank(i):
            return PP[i // 2][:, (i % 2) * 512:(i % 2) * 512 + 512], PR[i // 2][i % 2]

        class Region:
            def __init__(self, name, nbytes, cell):
                self.t = sbt(name, [128, nbytes // 2], BF16)
                self.cell = cell
                self.cells = [R("%s_c%d" % (name, i)) for i in range((nbytes + cell - 1) // cell)]
                self.nbytes = nbytes

            def view(self, off, nbytes, dt, p0=0, p1=128):
                assert off % 4 == 0 and off + nbytes <= self.nbytes, (off, nbytes, self.nbytes)
                ap = self.t[p0:p1, off // 2:(off + nbytes) // 2]
                if dt != BF16:
                    ap = ap.bitcast(dt)
                res = self.cells[off // self.cell:(off + nbytes - 1) // self.cell + 1]
                return ap, res

        xT = sbt("xT", [128, 8, T], F32)
        r_xT = [R("xT%d" % c) for c in range(8)]
        hyT = sbt("hyT", [128, 8, T], BF16)
        r_hy = [R("hy%d" % c) for c in range(8)]
        szT = sbt("szT", [128, 8, T], BF16)
        r_sz = [R("sz%d" % c) for c in range(8)]

        cbias = sbt("cbias", [128, 8], F32); r_cb = R("cb")
        ident_f = sbt("ident_f", [128, 128], F32); r_idf = R("idf")
        ident_b = sbt("ident_b", [128, 128], BF16); r_idb = R("idb")
        ones_mean = sbt("ones_mean", [128, 128], BF16); r_om = R("om")
        ones64 = sbt("ones64", [128, 64], BF16); r_o64 = R("o64")
        ones_r = sbt("ones_r", [128, 64], F32); r_onr = R("onr")

        NSLOT = 3
        ws_t = [sbt("ws%d" % i, [128, 8, 512], BF16) for i in range(NSLOT)]
        ws_r = [R("ws%d" % i) for i in range(NSLOT)]

        wuq = sbt("wuq", [128, 2, 576], BF16); r_wuq = R("wuq")
        wukv = sbt("wukvV", [128, 384], BF16); r_wukv = R("wukvV")
        wukp = sbt("wukp", [128, 6, 96], BF16); r_wukp = R("wukp")

        vT = sbt("vT", [128, 144], F32); r_vT = R("vT")
        scond = sbt("scond", [128, 8], BF16); r_scond = R("scond")
        modl = [sbt("modl%d" % l, [128, 32], F32) for l in range(4)]
        r_modl = [R("modl%d" % l) for l in range(4)]

        Gq8 = sbt("Gq8", [128, 64], F32); Gk8 = sbt("Gk8", [128, 64], F32)
        Gmq = sbt("Gmq", [128, 256], F32); Gmkv = sbt("Gmkv", [128, 128], F32)
        r_G = R("G")
        RD = Region("RD", 4096, 4096)
        ropeG_all, r_ropeG = RD.view(0, 4096, F32); ropeG = ropeG_all.rearrange("p (t a d) -> p t a d", t=8, a=2)
        ropeM = sbt("ropeM", [128, 8, 2, 32], F32); r_ropeM = R("ropeM")
        nlg = sbt("nlg", [128, 32], F32); r_nlg = R("nlg")
        NEf = sbt("NEf", [128, 128], BF16); NEb = sbt("NEb", [128, 128], BF16)
        NEq = sbt("NEq", [128, 2, 128], F32); NEk = sbt("NEk", [128, 2], F32)
        mcar = sbt("mcar", [128, 8], F32)
        r_NE = R("NE")

        qdec = sbt("qdec", [128, 2, 4, 128], BF16); r_qdec = R("qdec")
        kdec = sbt("kdec", [128, 8], F32); r_kdec = R("kdec")
        mix1 = sbt("mix1", [128, 1024], F32)
        cdec = mix1[0:64, 0:512].rearrange("d (a e) -> d a e", a=8); r_cdec = R("cdec")
        S_out = mix1[0:64, 512:1024]; r_Sout = R("Sout")
        kpeT = mix1[64:96, 0:768].bitcast(BF16); r_kpeT = R("kpeT")
        isel = mix1[64:96, 768:816].bitcast(BF16); r_isel = R("isel")
        mix2 = sbt("mix2", [128, 1024], F32)
        S_aft = mix2[0:64, 0:512]; r_Saft = R("Saft")
        S_tmp = mix2[0:64, 512:1024]; r_Stmp = R("Stmp")
        rowr = sbt("rowr", [65, 512], F32); r_rowr = R("rowr")

        V_ret = sbt("V_ret", [128, 8, 256], BF16); r_vret = [R("vret%d" % t) for t in range(8)]
        ckvn = sbt("ckvn", [128, 12, 128], BF16); r_ckvn = [R("ckvn%d" % t) for t in range(12)]
        kpeb = sbt("kpeb", [128, 12, 32], BF16); r_kpeb = [R("kpeb%d" % t) for t in range(12)]

        Vg = sbt("Vg", [128, 12, 3, 64], BF16); r_Vg = [R("Vg%d" % k) for k in range(12)]
        QTg = sbt("QTg", [128, 2, T], BF16); r_QTg = [R("QTg%d" % s_) for s_ in range(2)]
        KTg = sbt("KTg", [128, 2, 1536], BF16); r_KTg = [R("KTg%d" % g) for g in range(2)]
        cqnT = sbt("cqnT", [128, 2, T], BF16); r_cqnT = R("cqnT")
        ckvT = sbt("ckvT", [128, 1536], BF16); r_ckvT = R("ckvT")
        small = [sbt("small%d" % i, [128, 16], F32) for i in range(4)]; r_small = [R("small%d" % i) for i in range(4)]
        stage0 = sbt("stage0", [128, 416], F32)
        r_stg = [R("stg_gk"), R("stg_gv"), R("stg_ckv"), R("stg_kpe")]

        RA = Region("RA", 8192, 1024)
        PT = []; r_PT = []
        for i in range(4):
            ap_, rs_ = RA.view(i * 2048, 2048, BF16); PT.append(ap_); r_PT.append(rs_)
        xst = []; r_xst = []
        for i in range(2):
            ap_, rs_ = RA.view(i * 4096, 4096, F32); xst.append(ap_); r_xst.append(rs_)
        qk_all, _ = RA.view(0, 8192, BF16)
        qk_tm = qk_all.rearrange("p (t c) -> p t c", t=8)
        r_qk = [[RA.cells[t]] for t in range(8)]
        gkc_all, r_gkc = RA.view(4096, 1024, BF16); gkc = gkc_all.rearrange("p (k d) -> p k d", k=4)
        cqn_all, _ = RA.view(0, 4096, BF16); cqn = cqn_all.rearrange("p (t c) -> p t c", t=8)
        r_cqn = [[RA.cells[t // 2]] for t in range(8)]
        intra_all, r_intra = RD.view(0, 4096, F32); intra = intra_all.rearrange("p (a h i) -> p a h i", a=2, h=4)

        RB = Region("RB", 40960, 1024)
        g_all, _ = RB.view(0, 8192, BF16); g_tm = g_all.rearrange("p (t c) -> p t c", t=8)
        r_g = [[RB.cells[t]] for t in range(8)]
        kd_all, _ = RB.view(8192, 8192, BF16); kd_tm = kd_all.rearrange("p (t a h d) -> p t a h d", t=8, a=2, h=4)
        r_kd = [[RB.cells[8 + t]] for t in range(8)]
        qkT_all, r_qkT_all = RB.view(16384, 8192, BF16); qkT = qkT_all.rearrange("p (r k t) -> p r k t", r=2, k=2)
        r_qkT = [RB.cells[16 + 4 * (h // 2):16 + 4 * (h // 2) + 4] for h in range(4)]
        Sin_all, _ = RB.view(24576, 8192, BF16); S_in = Sin_all.rearrange("p (t c) -> p t c", t=8)
        r_Sin = [[RB.cells[24 + t]] for t in range(8)]
        qd_all, r_qd_all = RB.view(32768, 4096, BF16); qd = qd_all.rearrange("p (a t) -> p a t", a=2)
        r_qd = r_qd_all
        AT = []; r_AT = []
        for i in range(2):
            ap_, rs_ = RB.view(36864 + i * 2048, 2048, BF16); AT.append(ap_.rearrange("p (a c) -> p a c", a=2)); r_AT.append(rs_)
        qm_all, _ = RB.view(0, 9216, BF16); qm_tm = qm_all.rearrange("p (t h x) -> p t h x", t=8, h=6)
        r_qm = [RB.cells[(t * 1152) // 1024:(t * 1152 + 1151) // 1024 + 1] for t in range(8)]
        Vm_all, r_Vm_all = RB.view(9216, 13824, BF16); Vm = Vm_all.rearrange("p (k r x d) -> p k r x d", k=12, r=3, x=3)
        r_Vm = [RB.cells[(9216 + k * 1152) // 1024:(9216 + k * 1152 + 1151) // 1024 + 1] for k in range(12)]
        KTm = []; r_KTm = []
        QTm = []; r_QTm = []
        for i in range(2):
            ap_, rs_ = RB.view(23040 + i * 3072, 3072, BF16, 0, 101); KTm.append(ap_); r_KTm.append(rs_)
        for i in range(2):
            ap_, rs_ = RB.view(29184 + i * 2048, 2048, BF16, 0, 101); QTm.append(ap_); r_QTm.append(rs_)

        RC = Region("RC", 12288, 2048)
        Xs = []; r_Xs = []; T1 = []; r_T1 = []; T2 = []; r_T2 = []; sqb = []; r_sqb = []
        for i in range(2):
            ap_, rs_ = RC.view(i * 2048, 2048, F32); Xs.append(ap_); r_Xs.append(rs_)
            ap_, rs_ = RC.view(4096 + i * 2048, 2048, F32); T1.append(ap_); r_T1.append(rs_)
            ap_, rs_ = RC.view(8192 + i * 2048, 2048, F32); T2.append(ap_); r_T2.append(rs_)
            ap_, rs_ = RC.view(8192 + i * 2048, 2048, BF16); sqb.append(ap_); r_sqb.append(rs_)
        vecA_all, r_vecA = RC.view(8192, 512, F32); vecA = vecA_all[0:48, :]
        vecB_all, r_vecB = RC.view(10240, 512, F32); vecB = vecB_all[0:96, :]
        big1, r_big1 = RC.view(0, 4096, F32)
        big2, r_big2 = RC.view(4096, 4096, F32)
        for i in range(2):
            ap_, rs_ = RB.view(24576 + i * 6144, 2048, F32); Xs.append(ap_); r_Xs.append(rs_)
            ap_, rs_ = RB.view(24576 + i * 6144 + 2048, 2048, F32); T1.append(ap_); r_T1.append(rs_)
            ap_, rs_ = RB.view(24576 + i * 6144 + 4096, 2048, F32); T2.append(ap_); r_T2.append(rs_)
        NTMP = 4

        def dma_in(eng, out_ap, in_ap, w, nonctg=False):
            name = uniq("su")
            if nonctg:
                return A(eng, lambda e: e.dma_start(out=out_ap, in_=in_ap, allow_slow_non_contiguous=True), w=w, dma=name)
            return A(eng, lambda e: e.dma_start(out=out_ap, in_=in_ap), w=w, dma=name)

        def x_dma(tt):
            s_ = tt % 2
            A("sp", lambda e: e.dma_start(out=xst[s_][:], in_=d_x[tt * 128:(tt + 1) * 128, :]), w=[r_xst[s_]], dma="xst%d" % s_)
        x_dma(0)
        x_dma(1)

        A("dve", lambda e: e.memset(cbias[:, 0:1], EPS), w=[r_cb])
        A("dve", lambda e: e.memset(cbias[:, 1:2], float(np.log(0.125))), w=[r_cb])
        A("dve", lambda e: e.memset(cbias[:, 2:3], 1.0), w=[r_cb])
        A("dve", lambda e: e.memset(cbias[:, 3:4], 64.0 * EPS), w=[r_cb])
        A("dve", lambda e: e.memset(cbias[:, 4:5], 128.0 * EPS), w=[r_cb])
        A("dve", lambda e: e.memset(cbias[:, 5:6], 256.0 * EPS), w=[r_cb])
        A("dve", lambda e: e.memset(cbias[:, 6:7], -1.0), w=[r_cb])
        A("dve", lambda e: e.memset(ident_f[:], 0.0), w=[r_idf])
        A("pool", lambda e: e.affine_select(out=ident_f[:], in_=ident_f[:], pattern=[[-1, 128]], compare_op=ALU.not_equal,
                                            fill=1.0, base=0, channel_multiplier=1), r=[r_idf], w=[r_idf])
        A("dve", lambda e: e.tensor_copy(ident_b[:], ident_f[:]), r=[r_idf], w=[r_idb])
        A("dve", lambda e: e.memset(ones_mean[:], 1.0 / 1024.0), w=[r_om])
        A("dve", lambda e: e.memset(ones64[:], 1.0 / 64.0), w=[r_o64])
        A("dve", lambda e: e.memset(big2[:, 0:64], 1.0), w=[r_big2])
        A("dve", lambda e: e.tensor_copy(ones_r[:].bitcast(F32R), big2[:, 0:64]), r=[r_big2], w=[r_onr])
        A("dve", lambda e: e.memset(isel, 0.0), w=[r_isel])
        A("dve", lambda e: e.tensor_copy(isel[:, 64:96], ident_f[0:32, 0:32]), r=[r_idf, r_isel], w=[r_isel])
        A("dve", lambda e: e.memset(wukp[:], 0.0), w=[r_wukp])
        A("dve", lambda e: e.memset(Vg[:, :, 1, :], 1.0), w=r_Vg)

        dma_in("sp", vecA[0:8, :], d_cond, [r_vecA])
        dma_in("sp", vecA[8:16, :], d_fn, [r_vecA])
        dma_in("sp", vecA[16:48, :], d_ng, [r_vecA])
        dma_in("sp", vecB[:], d_bada, [r_vecB])
        dma_in("sp", ropeM[:].rearrange("p t a d -> p t (a d)"), d_ropem.rearrange("(t p) a d -> p t (a d)", p=128), [r_ropeM])
        dma_in("sp", big1[:, 0:32], d_logit.partition_broadcast(128), [r_big1])
        dma_in("sp", NEq[:].rearrange("p a i -> p (a i)"), d_neq.partition_broadcast(128), [r_NE])
        dma_in("sp", NEk[:], d_nek, [r_NE])
        dma_in("sp", mcar[:], d_carry.partition_broadcast(128), [r_NE])
        A("dve", lambda e: e.memset(QTg[:], 0.0), w=r_QTg)
        A("dve", lambda e: e.memset(KTg[:], 0.0), w=r_KTg)
        def late_pool_setup():
            dma_in("pool", NEf[:], d_nef, [r_NE])
            dma_in("pool", NEb[:], d_neb, [r_NE])
            for s_ in range(2):
                dma_in("pool", QTg[64:69, s_, :], d_qmask, [r_QTg[s_]])
            for g in range(2):
                dma_in("pool", KTg[64:69, g, :], d_kmask, [r_KTg[g]])

        A("act", lambda e: e.activation(big1[:, 32:64], big1[:, 0:32], AF.Exp, scale=-1.0), r=[r_big1], w=[r_big1])
        A("act", lambda e: e.activation(nlg[:], big1[:, 32:64], AF.Ln, bias=cbias[:, 2:3], scale=1.0), r=[r_big1, r_cb], w=[r_nlg])

        bA, rA = bank(6)
        A("pe", lambda e: e.transpose(bA[:, 0:48], vecA[0:48, :], ident_f[0:48, 0:48]), r=[r_vecA, r_idf], w=[rA])
        A("pe", lambda e: e.transpose(bA[:, 48:144], vecB[0:96, :], ident_f[0:96, 0:96]), r=[r_vecB, r_idf], w=[rA])
        A("dve", lambda e: e.tensor_copy(vT[:], bA[:, 0:144]), r=[rA], w=[r_vT])
        A("act", lambda e: e.activation(scond[:], vT[:, 0:8], AF.Silu), r=[r_vT], w=[r_scond])

        def x_load(cb):
            for tt in range(NT):
                s = tt % 2
                if tt >= 2:
                    x_dma(tt)
                for hf in range(2):
                    b, rb = bank(2 * s + hf)
                    for j in range(4):
                        c = hf * 4 + j
                        A("pe", lambda e, b=b, j=j, c=c, s=s: e.transpose(b[:, j * 128:(j + 1) * 128], xst[s][:, c * 128:(c + 1) * 128], ident_f[:]),
                          r=[r_xst[s], r_idf], w=[rb])
                    eng = "dve" if hf == 0 else "act"
                    if eng == "dve":
                        A("dve", lambda e, b=b, hf=hf, tt=tt: e.tensor_copy(xT[:, hf * 4:hf * 4 + 4, tt * 128:(tt + 1) * 128],
                                                                            b.rearrange("p (j t) -> p j t", j=4)),
                          r=[rb], w=r_xT[hf * 4:hf * 4 + 4])
                    else:
                        A("act", lambda e, b=b, hf=hf, tt=tt: e.copy(xT[:, hf * 4:hf * 4 + 4, tt * 128:(tt + 1) * 128],
                                                                      b.rearrange("p (j t) -> p j t", j=4)),
                          r=[rb], w=r_xT[hf * 4:hf * 4 + 4])
                cb()

        pieces = []

        def wview(d, l):
            return d[l].rearrange("(c p) n -> p c n", p=128)

        def add_ada(l):
            for pc in range(6):
                pieces.append([(0, 512, wview(d_wada, l)[:, :, pc * 512:(pc + 1) * 512])])

        def add_in(l):
            v = wview(d_win, l)
            pieces.append([(0, 512, v[:, :, 0:512])])
            pieces.append([(0, 512, v[:, :, 768:1280])])
            pieces.append([(0, 256, v[:, :, 512:768]), (256, 128, v[:, :, 1280:1408]), (384, 128, v[:, :, 1664:1792])])
            pieces.append([(0, 256, v[:, :, 1408:1664]), (256, 32, v[:, :, 1792:1824])])
            pieces.append([(0, 512, v[:, :, 1824:2336])])
            pieces.append([(0, 512, v[:, :, 2336:2848])])

        def add_out(l):
            for pc in range(2):
                pieces.append([(0, 512, wview(d_wout, l)[:, :, pc * 512:(pc + 1) * 512])])

        add_ada(0)
        for l in range(nlayers):
            add_in(l)
            if l + 1 < nlayers:
                add_ada(l + 1)
            add_out(l)
        ws_state = {"loaded": 0, "used": 0}

        def ws_issue():
            i = ws_state["loaded"]
            if i >= len(pieces):
                return
            s = i % NSLOT
            for (off, wd, src) in pieces[i]:
                A("pool", lambda e, s=s, off=off, wd=wd, src=src: e.dma_start(out=ws_t[s][:, :, off:off + wd], in_=src),
                  w=[ws_r[s]], dma="ws%d" % s)
            ws_state["loaded"] += 1

        def ws_get():
            i = ws_state["used"]
            ws_state["used"] += 1
            return ws_t[i % NSLOT], ws_r[i % NSLOT]

        for _ in range(NSLOT):
            ws_issue()
        late_pool_setup()

        tmp_i = [0]

        def nxt():
            tmp_i[0] = (tmp_i[0] + 1) % NTMP
            return tmp_i[0]

        bank_rr = [0]

        def rope_ops(src, r_src, cos, sin, r_tab, dst, r_dst, nh, hd, ti, c0=0):
            w = nh * hd
            t1 = T1[ti][:, c0:c0 + w]; t2 = T2[ti][:, c0:c0 + w]
            A("dve", lambda e: e.tensor_tensor(t1.rearrange("p (h d) -> p h d", h=nh), src.rearrange("p (h d) -> p h d", h=nh),
                                               cos.unsqueeze(1).broadcast_to([128, nh, hd]), ALU.mult), r=[r_src, r_tab], w=[r_T1[ti]])
            sv = src.rearrange("p (h a q s) -> p h a q s", h=nh, a=2, q=2)
            tv = t2.rearrange("p (h a q s) -> p h a q s", h=nh, a=2, q=2)
            sn = sin.rearrange("p (a q s) -> p a q s", a=2, q=2)
            for q in range(2):
                snq = sn[:, :, q, :].unsqueeze(1).broadcast_to([128, nh, 2, hd // 4])
                A("dve", lambda e, q=q, snq=snq: e.tensor_tensor(tv[:, :, :, q, :], sv[:, :, :, 1 - q, :], snq, ALU.mult),
                  r=[r_src, r_tab], w=[r_T2[ti]])
            yield
            A("dve", lambda e: e.tensor_tensor(dst, t1, t2, ALU.add), r=[r_T1[ti], r_T2[ti]], w=[r_dst])

        def transposes_gen(items, banks=(4, 5, 6, 7), evac=None):
            k = 0
            for (src_fn, rs_fn, w, dst_fn, r_dst, ntile, t0) in items:
                for half in range((ntile + 3) // 4):
                    bi = banks[bank_rr[0] % len(banks)]
                    bank_rr[0] += 1
                    b, rb = bank(bi)
                    bb = b.bitcast(BF16)
                    n = min(4, ntile - half * 4)
                    for j in range(n):
                        tt = half * 4 + j
                        A("pe", lambda e, bb=bb, j=j, tt=tt, src_fn=src_fn, w=w: e.transpose(bb[0:w, j * 128:(j + 1) * 128], src_fn(tt), ident_b[:]),
                          r=[rs_fn(tt), r_idb], w=[rb])
                    eng = evac if evac is not None else ("act" if (k % 2 == 0) else "dve")
                    k += 1
                    dst = dst_fn(half)
                    if eng == "act":
                        A("act", lambda e, bb=bb, w=w, n=n, dst=dst: e.copy(dst, bb[0:w, 0:n * 128]), r=[rb], w=[r_dst])
                    else:
                        A("dve", lambda e, bb=bb, w=w, n=n, dst=dst: e.tensor_copy(dst, bb[0:w, 0:n * 128]), r=[rb], w=[r_dst])
                    yield

        def transposes(items, banks=(4, 5, 6, 7), evac=None):
            for _ in transposes_gen(items, banks, evac):
                pass

        def tap(name, ap, shape, reads):
            if name not in taps:
                return
            d = dout("tap_" + name, shape)
            tap_out[name] = shape
            A("pool", lambda e: e.dma_start(out=d, in_=ap), r=reads, dma=uniq("tap"))

        def mod_gen(l):
            pm, rpm = bank(7)
            for pc in range(6):
                wt, wr = ws_get()
                for j in range(4):
                    cc = pc * 4 + j
                    for kc in range(8):
                        A("pe", lambda e, wt=wt, j=j, kc=kc, cc=cc: e.matmul(pm[:, cc:cc + 1], lhsT=wt[:, kc, j * 128:(j + 1) * 128],
                                                                              rhs=scond[:, kc:kc + 1], start=(kc == 0), stop=(kc == 7)),
                          r=[wr, r_scond], w=[rpm])
                    yield
                ws_issue()
            m = modl[l]
            A("dve", lambda e: e.tensor_tensor(m[:, 0:24], pm[:, 0:24], vT[:, 48 + 24 * l:72 + 24 * l], ALU.add), r=[rpm, r_vT], w=[r_modl[l]])
            A("dve", lambda e: e.scalar_tensor_tensor(m[:, 24:32], m[:, 8:16], 1.0, vT[:, 16 + 8 * l:24 + 8 * l], ALU.add, ALU.mult),
              r=[r_modl[l], r_vT], w=[r_modl[l]])
            yield

        def phase_mod(l):
            for _ in mod_gen(l):
                pass

        def sumsq_bc(dst_bc, r_dst, src_chunks, r_src, nchunk, ones, r_ones, np_, eps_n, PPi):
            for c in range(nchunk):
                s = c % 2
                A("act", lambda e, c=c, s=s: e.activation(sqb[s][0:np_, :], src_chunks(c), AF.Square), r=[r_src[c]], w=[r_sqb[s]])
                for hf in range(2):
                    A("pe", lambda e, c=c, s=s, hf=hf: e.matmul(PP[PPi][0:np_, hf * 512:(hf + 1) * 512], lhsT=ones[0:np_, 0:np_],
                                                                 rhs=sqb[s][0:np_, hf * 512:(hf + 1) * 512], start=(c == 0), stop=(c == nchunk - 1)),
                      r=[r_sqb[s], r_ones], w=[PR[PPi][hf]])
            A("act", lambda e: e.activation(dst_bc, PP[PPi][0:np_, :], AF.Ln, bias=cbias[0:np_, 0:1], scale=1.0), r=PR[PPi] + [r_cb], w=[r_dst])
            A("act", lambda e: e.activation(dst_bc, dst_bc, AF.Exp, scale=-0.5), r=[r_dst], w=[r_dst])

        def phase_norm(l):
            sumsq_bc(big1[:], r_big1, lambda c: xT[:, c, :], r_xT, 8, ones_mean, r_om, 128, EPS, 0)
            m = modl[l]
            big3, r_big3 = RC.view(8192, 4096, F32)
            for c in range(8):
                bb_, rbb_ = (big2, r_big2) if c % 2 == 0 else (big3, r_big3)
                A("dve", lambda e, c=c, bb_=bb_: e.tensor_tensor(bb_, xT[:, c, :], big1[:], ALU.mult), r=[r_xT[c], r_big1], w=[rbb_])
                A("act", lambda e, c=c, bb_=bb_: e.activation(hyT[:, c, :], bb_, AF.Identity, bias=m[:, c:c + 1], scale=m[:, 24 + c:25 + c]),
                  r=[rbb_, r_modl[l]], w=[r_hy[c]])

        def load_small(l, gkc_only=False, skip_gkc=False):
            if gkc_only:
                A("pool", lambda e: e.dma_start(out=gkc, in_=d_cgk[l].rearrange("(k p) d -> p k d", p=128)), w=[r_gkc], dma="gkc")
                return
            A("sp", lambda e: e.dma_start(out=ropeG.rearrange("p t a d -> p t (a d)"), in_=d_ropehd.rearrange("(t p) a d -> p t (a d)", p=128)),
              w=[r_ropeG], dma="ropeG")
            A("sp", lambda e: e.dma_start(out=Gq8[:], in_=d_gqg[l * 64:(l + 1) * 64].partition_broadcast(128)), w=[r_G], dma="g1")
            A("sp", lambda e: e.dma_start(out=Gk8[:], in_=d_gkg[l * 64:(l + 1) * 64].partition_broadcast(128)), w=[r_G], dma="g2")
            A("sp", lambda e: e.dma_start(out=Gmq[:], in_=d_mqg[l * 256:(l + 1) * 256].partition_broadcast(128)), w=[r_G], dma="g3")
            A("sp", lambda e: e.dma_start(out=Gmkv[:], in_=d_mkvg[l * 128:(l + 1) * 128].partition_broadcast(128)), w=[r_G], dma="g4")
            A("dve", lambda e: e.tensor_scalar(Gq8[:], Gq8[:], 8.0, None, ALU.mult), r=[r_G], w=[r_G])
            A("dve", lambda e: e.tensor_scalar(Gk8[:], Gk8[:], 8.0, None, ALU.mult), r=[r_G], w=[r_G])
            A("dve", lambda e: e.tensor_scalar(Gmq[:], Gmq[:], 16.0, None, ALU.mult), r=[r_G], w=[r_G])
            A("dve", lambda e: e.tensor_scalar(Gmkv[:], Gmkv[:], float(np.sqrt(128.0)), None, ALU.mult), r=[r_G], w=[r_G])
            A("pool", lambda e: e.dma_start(out=wuq[:], in_=d_wuq[l].rearrange("(c p) n -> p c n", p=128)), w=[r_wuq], dma="wuq")
            A("pool", lambda e: e.dma_start(out=wukv[:].rearrange("p (h d) -> p h d", h=6), in_=d_wukv[l].rearrange("p (h x) -> p h x", h=6)[:, :, 64:128]),
              w=[r_wukv], dma="wukv")
            A("pool", lambda e: e.dma_start(out=wukp[:, :, 0:64], in_=d_wukv[l].rearrange("p (h x) -> p h x", h=6)[:, :, 0:64]), w=[r_wukp], dma="wukp")
            if not skip_gkc:
                A("pool", lambda e: e.dma_start(out=gkc, in_=d_cgk[l].rearrange("(k p) d -> p k d", p=128)), w=[r_gkc], dma="gkc")
            for g in range(2):
                A("pool", lambda e, g=g: e.dma_start(out=Vg[:, 0:4, 2 * g, :], in_=d_cgv[l].rearrange("(k p) (g d) -> p k g d", p=128, g=2)[:, :, g, :]),
                  w=r_Vg[0:4], dma="vgc")
            A("pool", lambda e: e.dma_start(out=ckvn[:, 0:4, :], in_=d_cckv[l].rearrange("(k p) d -> p k d", p=128)), w=r_ckvn[0:4], dma="ckvc")
            A("pool", lambda e: e.dma_start(out=kpeb[:, 0:4, :], in_=d_ckpe[l].rearrange("(k p) d -> p k d", p=128)), w=r_kpeb[0:4], dma="kpec")
            A("sp", lambda e: e.dma_start(out=S_aft.rearrange("d (a e) -> d a e", a=8), in_=d_st0[l].rearrange("a h d e -> d (a h) e")),
              w=[r_Saft], dma="st0")

        def phase_consts(l):
            LN8 = float(np.log(0.125))
            for a in range(2):
                NE = NEf if a == 0 else NEb
                for h in range(4):
                    col = l * 8 + a * 4 + h
                    p0 = 64 * (h % 2)
                    A("act", lambda e, a=a, h=h, col=col, p0=p0: e.activation(qdec[p0:p0 + 64, a, h, :], NEq[p0:p0 + 64, a, :], AF.Exp,
                                                                              scale=nlg[p0:p0 + 64, col:col + 1]),
                      r=[r_NE, r_nlg], w=[r_qdec])
            sm = small[0]
            A("dve", lambda e: e.tensor_tensor(sm[:, 0:8].rearrange("p (a h) -> p a h", a=2), nlg[:, l * 8:l * 8 + 8].rearrange("p (a h) -> p a h", a=2),
                                               NEk[:].unsqueeze(2).broadcast_to([128, 2, 4]), ALU.mult), r=[r_nlg, r_NE], w=[r_small[0]])
            A("act", lambda e: e.activation(kdec[:], sm[:, 0:8], AF.Exp, bias=cbias[:, 1:2]), r=[r_small[0], r_cb], w=[r_kdec])
            A("act", lambda e: e.activation(sm[0:64, 8:16], nlg[0:64, l * 8:l * 8 + 8], AF.Exp, scale=-128.0), r=[r_nlg, r_small[0]], w=[r_small[0]])
            A("dve", lambda e: e.tensor_copy(cdec, sm[0:64, 8:16].unsqueeze(2).broadcast_to([64, 8, 64])), r=[r_small[0]], w=[r_cdec])

        def phase_consts_b(l):
            LN8 = float(np.log(0.125))
            for a in range(2):
                NE = NEf if a == 0 else NEb
                for h in range(4):
                    col = l * 8 + a * 4 + h
                    A("act", lambda e, a=a, h=h, col=col, NE=NE: e.activation(intra[:, a, h, :], NE[:], AF.Exp, bias=cbias[:, 1:2], scale=nlg[:, col:col + 1]),
                      r=[r_NE, r_nlg, r_cb], w=[r_intra])

        def phase_inproj(l):
            m = modl[l]
            transposes([
                (lambda tt: gkc[:, tt, 0:64], lambda tt: r_gkc, 64, lambda half: KTg[0:64, 0, 0:512], r_KTg[0], 4, 0),
                (lambda tt: gkc[:, tt, 64:128], lambda tt: r_gkc, 64, lambda half: KTg[0:64, 1, 0:512], r_KTg[1], 4, 0),
                (lambda tt: ckvn[:, tt, :], lambda tt: r_ckvn[tt], 128, lambda half: ckvT[:, 0:512], r_ckvT, 4, 0),
                (lambda tt: kpeb[:, tt, :], lambda tt: r_kpeb[tt], 32, lambda half: kpeT[:, 0:512], r_kpeT, 4, 0),
            ])
            widths = [512, 512, 512, 288]
            def mm_group(g):
                wt, wr = ws_get()
                wd = widths[g]
                def prep_gen(tt, b, rb, ti):
                    X = Xs[ti]
                    A("act", lambda e, X=X, b=b, wd=wd: e.copy(X[:, 0:wd], b[:, 0:wd]), r=[rb], w=[r_Xs[ti]])
                    yield
                    sg = stage0
                    if g == 0:
                        yield from rope_ops(X[:, 0:512], r_Xs[ti], ropeG[:, tt, 0, :], ropeG[:, tt, 1, :], r_ropeG,
                                 qk_tm[:, tt, :], r_qk[tt], 8, 64, ti, 0)
                        A("pool", lambda e, tt=tt: e.tensor_tensor(kd_tm[:, tt, :, :, :],
                                                                   qk_tm[:, tt, 256:512].rearrange("p (h d) -> p h d", h=4).unsqueeze(1).broadcast_to([128, 2, 4, 64]),
                                                                   kdec[:].rearrange("p (a h) -> p a h", a=2).unsqueeze(3).broadcast_to([128, 2, 4, 64]), ALU.mult),
                          r=[r_qk[tt], r_kdec], w=[r_kd[tt]])
                    elif g == 1:
                        sq = T1[ti]; sm = small[ti]
                        A("act", lambda e, X=X, sq=sq: e.activation(sq[:], X[:], AF.Square), r=[r_Xs[ti]], w=[r_T1[ti]])
                        yield
                        A("dve", lambda e, sq=sq, sm=sm: e.tensor_reduce(sm[:, 0:8], sq[:].rearrange("p (h d) -> p h d", h=8), AX.X, ALU.add),
                          r=[r_T1[ti]], w=[r_small[ti]])
                        yield
                        A("act", lambda e, sm=sm: e.activation(sm[:, 8:16], sm[:, 0:8], AF.Ln, bias=cbias[:, 3:4], scale=1.0), r=[r_small[ti], r_cb], w=[r_small[ti]])
                        yield
                        A("act", lambda e, sm=sm: e.activation(sm[:, 8:16], sm[:, 8:16], AF.Exp, scale=-0.5), r=[r_small[ti]], w=[r_small[ti]])
                        yield
                        Tn = T2[ti]
                        A("dve", lambda e, X=X, sm=sm, Tn=Tn: e.tensor_tensor(Tn[:].rearrange("p (h d) -> p h d", h=8), X[:].rearrange("p (h d) -> p h d", h=8),
                                                                              sm[:, 8:16].unsqueeze(2).broadcast_to([128, 8, 64]), ALU.mult),
                          r=[r_Xs[ti], r_small[ti]], w=[r_T2[ti]])
                        yield
                        A("dve", lambda e, X=X, Tn=Tn: e.tensor_tensor(X[:, 0:384].rearrange("p (h d) -> p h d", h=6), Tn[:, 0:384].rearrange("p (h d) -> p h d", h=6),
                                                                        Gq8[:].unsqueeze(1).broadcast_to([128, 6, 64]), ALU.mult),
                          r=[r_T2[ti], r_G], w=[r_Xs[ti]])
                        A("dve", lambda e, X=X, Tn=Tn: e.tensor_tensor(X[:, 384:512].rearrange("p (h d) -> p h d", h=2), Tn[:, 384:512].rearrange("p (h d) -> p h d", h=2),
                                                                        Gk8[:].unsqueeze(1).broadcast_to([128, 2, 64]), ALU.mult),
                          r=[r_T2[ti], r_G], w=[r_Xs[ti]])
                        yield
                        A("act", lambda e, X=X, sg=sg: e.copy(sg[:, 0:128], X[:, 384:512]), r=[r_Xs[ti]], w=[r_stg[0]])
                        A("sp", lambda e, sg=sg, tt=tt: e.dma_start(out=o_gk[l, tt * 128:(tt + 1) * 128, :], in_=sg[:, 0:128]), r=[r_stg[0]], dma="stg0")
                        yield from rope_ops(X[:, 0:512], r_Xs[ti], ropeG[:, tt, 0, :], ropeG[:, tt, 1, :], r_ropeG, g_tm[:, tt, :], r_g[tt], 8, 64, ti)
                    elif g == 2:
                        A("dve", lambda e, X=X, tt=tt: e.tensor_copy(V_ret[:, tt, :], X[:, 0:256]), r=[r_Xs[ti]], w=[r_vret[tt]])
                        A("dve", lambda e, X=X, tt=tt: e.tensor_copy(Vg[:, 4 + tt, 0:3:2, :], X[:, 256:384].rearrange("p (g d) -> p g d", g=2)),
                          r=[r_Xs[ti]], w=[r_Vg[4 + tt]])
                        A("act", lambda e, X=X, sg=sg: e.copy(sg[:, 128:256], X[:, 256:384]), r=[r_Xs[ti]], w=[r_stg[1]])
                        A("sp", lambda e, sg=sg, tt=tt: e.dma_start(out=o_gv[l, tt * 128:(tt + 1) * 128, :], in_=sg[:, 128:256]), r=[r_stg[1]], dma="stg1")
                        sm = small[ti]; junk = T1[ti]
                        A("dve", lambda e, X=X, sm=sm, junk=junk: e.scalar_tensor_tensor(junk[:, 0:128], X[:, 384:512], 1.0, X[:, 384:512], ALU.mult, ALU.mult,
                                                                                        accum_out=sm[:, 0:1]), r=[r_Xs[ti]], w=[r_T1[ti], r_small[ti]])
                        yield
                        A("act", lambda e, sm=sm: e.activation(sm[:, 1:2], sm[:, 0:1], AF.Ln, bias=cbias[:, 4:5], scale=1.0), r=[r_small[ti], r_cb], w=[r_small[ti]])
                        yield
                        A("act", lambda e, sm=sm: e.activation(sm[:, 1:2], sm[:, 1:2], AF.Exp, scale=-0.5), r=[r_small[ti]], w=[r_small[ti]])
                        yield
                        A("dve", lambda e, X=X, sm=sm, sg=sg: e.scalar_tensor_tensor(sg[:, 256:384], X[:, 384:512], sm[:, 1:2], Gmkv[:], ALU.mult, ALU.mult),
                          r=[r_Xs[ti], r_small[ti], r_G], w=[r_stg[2]])
                        A("dve", lambda e, sg=sg, tt=tt: e.tensor_copy(ckvn[:, 4 + tt, :], sg[:, 256:384]), r=[r_stg[2]], w=[r_ckvn[4 + tt]])
                        A("sp", lambda e, sg=sg, tt=tt: e.dma_start(out=o_ckv[l, tt * 128:(tt + 1) * 128, :], in_=sg[:, 256:384]), r=[r_stg[2]], dma="stg2")
                    else:
                        sm = small[ti]; junk = T1[ti]
                        A("dve", lambda e, X=X, sm=sm, junk=junk: e.scalar_tensor_tensor(junk[:, 0:256], X[:, 0:256], 1.0, X[:, 0:256], ALU.mult, ALU.mult,
                                                                                        accum_out=sm[:, 0:1]), r=[r_Xs[ti]], w=[r_T1[ti], r_small[ti]])
                        yield
                        A("act", lambda e, sm=sm: e.activation(sm[:, 1:2], sm[:, 0:1], AF.Ln, bias=cbias[:, 5:6], scale=1.0), r=[r_small[ti], r_cb], w=[r_small[ti]])
                        yield
                        A("act", lambda e, sm=sm: e.activation(sm[:, 1:2], sm[:, 1:2], AF.Exp, scale=-0.5), r=[r_small[ti]], w=[r_small[ti]])
                        yield
                        A("dve", lambda e, X=X, sm=sm, tt=tt: e.scalar_tensor_tensor(cqn[:, tt, :], X[:, 0:256], sm[:, 1:2], Gmq[:], ALU.mult, ALU.mult),
                          r=[r_Xs[ti], r_small[ti], r_G], w=[r_cqn[tt]])
                        yield from rope_ops(X[:, 256:288], r_Xs[ti], ropeM[:, tt, 0, :], ropeM[:, tt, 1, :], r_ropeM, sg[:, 384:416], r_stg[3], 1, 32, ti, 256)
                        A("dve", lambda e, sg=sg, tt=tt: e.tensor_copy(kpeb[:, 4 + tt, :], sg[:, 384:416]), r=[r_stg[3]], w=[r_kpeb[4 + tt]])
                        A("sp", lambda e, sg=sg, tt=tt: e.dma_start(out=o_kpe[l, tt * 128:(tt + 1) * 128, :], in_=sg[:, 384:416]), r=[r_stg[3]], dma="stg3")
                    yield
                for t0 in (0, 4):
                    gens = []
                    for tt in range(t0, t0 + 4):
                        bi = (g * NT + tt) % 4
                        b, rb = bank(bi)
                        for kc in range(8):
                            A("pe", lambda e, b=b, wd=wd, kc=kc, tt=tt, wt=wt: e.matmul(b[:, 0:wd], lhsT=hyT[:, kc, tt * 128:(tt + 1) * 128], rhs=wt[:, kc, 0:wd],
                                                                                         start=(kc == 0), stop=(kc == 7)),
                              r=[r_hy[kc], wr], w=[rb])
                        gens.append(prep_gen(tt, b, rb, nxt()))
                    while gens:
                        alive = []
                        for gn in gens:
                            try:
                                next(gn)
                                alive.append(gn)
                            except StopIteration:
                                pass
                        gens = alive
                ws_issue()
            def tr_group(g):
                if g == 0:
                    items = []
                    for h in range(4):
                        p0 = 64 * (h % 2); pr = h // 2
                        items.append((lambda tt, h=h: qk_tm[:, tt, h * 64:(h + 1) * 64], lambda tt: r_qk[tt], 64,
                                      lambda half, p0=p0, pr=pr: qkT[p0:p0 + 64, pr, 0, half * 512:(half + 1) * 512], r_qkT[h], 8, 0))
                        items.append((lambda tt, h=h: qk_tm[:, tt, 256 + h * 64:256 + (h + 1) * 64], lambda tt: r_qk[tt], 64,
                                      lambda half, p0=p0, pr=pr: qkT[p0:p0 + 64, pr, 1, half * 512:(half + 1) * 512], r_qkT[h], 8, 0))
                    transposes(items, evac="act")
                    tap("qk%d" % l, qk_tm, [128, 8, 512], r_qk)
                elif g == 1:
                    items = []
                    for gg in range(2):
                        items.append((lambda tt, gg=gg: g_tm[:, tt, 384 + gg * 64:384 + (gg + 1) * 64], lambda tt: r_g[tt], 64,
                                      lambda half, gg=gg: KTg[0:64, gg, 512 + half * 512:512 + (half + 1) * 512], r_KTg[gg], 8, 0))
                    transposes(items, evac="act")
                elif g == 2:
                    transposes([(lambda tt: ckvn[:, 4 + tt, :], lambda tt: r_ckvn[4 + tt], 128,
                                 lambda half: ckvT[:, 512 + half * 512:512 + (half + 1) * 512], r_ckvT, 8, 0)])
                else:
                    transposes([
                        (lambda tt: cqn[:, tt, 0:128], lambda tt: r_cqn[tt], 128, lambda half: cqnT[:, 0, half * 512:(half + 1) * 512], r_cqnT, 8, 0),
                        (lambda tt: cqn[:, tt, 128:256], lambda tt: r_cqn[tt], 128, lambda half: cqnT[:, 1, half * 512:(half + 1) * 512], r_cqnT, 8, 0),
                        (lambda tt: kpeb[:, 4 + tt, :], lambda tt: r_kpeb[4 + tt], 32, lambda half: kpeT[:, 512 + half * 512:512 + (half + 1) * 512], r_kpeT, 8, 0),
                    ])
            def z_group(zp):
                wt, wr = ws_get()
                for j in range(4):
                    cc = zp * 4 + j
                    for hf in range(2):
                        b, rb = bank((j * 2 + hf) % 4)
                        for kc in range(8):
                            A("pe", lambda e, b=b, wt=wt, j=j, kc=kc, hf=hf: e.matmul(b[:], lhsT=wt[:, kc, j * 128:(j + 1) * 128],
                                                                                        rhs=hyT[:, kc, hf * 512:(hf + 1) * 512], start=(kc == 0), stop=(kc == 7)),
                              r=[wr, r_hy[kc]], w=[rb])
                        A("act", lambda e, b=b, cc=cc, hf=hf: e.activation(szT[:, cc, hf * 512:(hf + 1) * 512], b[:], AF.Silu), r=[rb], w=[r_sz[cc]])
                ws_issue()
            mm_group(0)
            mm_group(1)
            tr_group(0)
            mm_group(2)
            tr_group(1)
            mm_group(3)
            tr_group(2)
            z_group(0)
            tr_group(3)
            z_group(1)

        pend = [None]
        bg = [None]

        def bg_step(n=1):
            for _ in range(n):
                if bg[0] is None:
                    return
                try:
                    next(bg[0])
                except StopIteration:
                    bg[0] = None

        def bg_flush():
            while bg[0] is not None:
                bg_step()

        prep = [None]

        def prep_step():
            if prep[0] is None:
                return
            try:
                next(prep[0])
            except StopIteration:
                prep[0] = None

        def prep_flush():
            while prep[0] is not None:
                prep_step()

        pass_ctr = [0]

        def attend(QT_fn, r_Q, KT_fn, r_K, V_fn, r_V_fn, krows, scale, ymix_row0, par):
            po = 64 * par
            pd = 64 - po
            c = ymix_row0 // 128
            p0 = ymix_row0 % 128
            units = [(hf, kc) for hf in range(2) for kc in range(NK)]
            NU = len(units)
            obank = {}
            for hf in range(2):
                obank[hf] = bank(4 + (pass_ctr[0] % 2))
                pass_ctr[0] += 1

            def QK(u):
                hf, kc = units[u]
                pS, rS = bank(u % 4)
                A("pe", lambda e, pS=pS, kc=kc, hf=hf: e.matmul(pS, lhsT=KT_fn(kc), rhs=QT_fn(hf), start=True, stop=True),
                  r=[r_K, r_Q], w=[rS])
                pi = u % 8
                ptv = PT[pi // 2][:, (pi % 2) * 512:(pi % 2) * 512 + 512]
                A("act", lambda e, pS=pS, ptv=ptv: e.activation(ptv, pS, AF.Exp, scale=scale), r=[rS], w=[RA.cells[pi]])

            def PV(u):
                hf, kc = units[u]
                pi = u % 8
                ptv = PT[pi // 2][:, (pi % 2) * 512:(pi % 2) * 512 + 512]
                pOb, rOb = obank[hf]
                A("pe", lambda e, kc=kc, ptv=ptv, pOb=pOb: e.matmul(pOb, lhsT=V_fn(kc), rhs=ptv, start=(kc == 0), stop=(kc == NK - 1)),
                  r=[r_V_fn(kc), RA.cells[pi]], w=[rOb])

            def finish(hf):
                pOb, rOb = obank[hf]
                cs = slice(hf * 512, (hf + 1) * 512)
                A("act", lambda e: e.activation(big1[po:po + 64, cs], pOb[pd:pd + 64, :], AF.Ln), r=[rOb], w=[r_big1[hf]])
                A("dve", lambda e: e.tensor_copy(big2[po:po + 64, cs], pOb[po:po + 64, :]), r=[rOb], w=[r_big2[hf]])
                A("act", lambda e: e.activation(big1[po:po + 64, cs], big1[po:po + 64, cs], AF.Exp, scale=-1.0), r=[r_big1[hf]], w=[r_big1[hf]])

                def norm():
                    A("dve", lambda e: e.tensor_tensor(big2[p0:p0 + 64, cs], big2[po:po + 64, cs], big1[po:po + 64, cs], ALU.mult),
                      r=[r_big2[hf], r_big1[hf]], w=[r_big2[hf]])
                    A("pool", lambda e: e.tensor_tensor(hyT[p0:p0 + 64, c, cs], big2[p0:p0 + 64, cs], szT[p0:p0 + 64, c, cs], ALU.mult),
                      r=[r_big2[hf], r_sz[c]], w=[r_hy[c]])
                pend.append(norm)

            for u in range(min(3, NU)):
                QK(u)
            for u in range(NU):
                if u + 3 < NU:
                    QK(u + 3)
                PV(u)
                hf, kc = units[u]
                if kc == 2 and len(pend) > 1:
                    pend.pop(1)()
                if u % 4 == 1:
                    prep_step()
                if u % 2 == 1:
                    bg_step()
                if kc == NK - 1:
                    if hf == 1:
                        prep_flush()
                    finish(hf)

        def attend_flush():
            while len(pend) > 1:
                pend.pop(1)()

        def gqa_head_prep(h):
            s_ = h % 2
            yield from transposes_gen([(lambda tt, h=h: g_tm[:, tt, h * 64:(h + 1) * 64], lambda tt: r_g[tt], 64,
                                        lambda half, s_=s_: QTg[0:64, s_, half * 512:(half + 1) * 512], r_QTg[s_], 8, 0)], banks=(6,), evac="dve")

        def phase_gqa(l):
            bg[0] = ret_gen(l)
            for _ in gqa_head_prep(0):
                pass
            for h in range(6):
                g = h // 3
                s_ = h % 2
                if h + 1 < 6:
                    prep[0] = gqa_head_prep(h + 1)
                attend(lambda hf, s_=s_: QTg[:, s_, hf * 512:(hf + 1) * 512], r_QTg[s_],
                       lambda kc, g=g: KTg[:, g, kc * 128:(kc + 1) * 128], r_KTg[g],
                       lambda kc, g=g: Vg[:, kc, g:g + 2, :].rearrange("p x d -> p (x d)"), lambda kc: r_Vg[kc], 69, 0.125, 256 + 64 * h, g)
            attend_flush()
            bg_flush()

        def phase_mla_proj(l):
            def tile_gen(tt):
                b0, rb0 = bank(0 + 2 * (tt % 2)); b1, rb1 = bank(1 + 2 * (tt % 2))
                for hh, (b, rb) in enumerate(((b0, rb0), (b1, rb1))):
                    for kc in range(2):
                        A("pe", lambda e, b=b, kc=kc, tt=tt, hh=hh: e.matmul(b[:, 0:288], lhsT=cqnT[:, kc, tt * 128:(tt + 1) * 128],
                                                                              rhs=wuq[:, kc, hh * 288:(hh + 1) * 288], start=(kc == 0), stop=(kc == 1)),
                          r=[r_cqnT, r_wuq], w=[rb])
                yield
                ti = nxt()
                Xa = Xs[ti]; Xb = T1[ti]
                A("act", lambda e, Xa=Xa, b0=b0: e.copy(Xa[:, 0:288], b0[:, 0:288]), r=[rb0], w=[r_Xs[ti]])
                A("act", lambda e, Xb=Xb, b1=b1: e.copy(Xb[:, 0:288], b1[:, 0:288]), r=[rb1], w=[r_T1[ti]])
                yield
                fin = []
                for hh, (Xh, rX) in enumerate(((Xa, r_Xs[ti]), (Xb, r_T1[ti]))):
                    xv = Xh[:, 0:288].rearrange("p (h x) -> p h x", h=3)
                    dv = qm_tm[:, tt, hh * 3:(hh + 1) * 3, :]
                    A("dve", lambda e, xv=xv, dv=dv: e.tensor_copy(dv[:, :, 0:64], xv[:, :, 0:64]), r=[rX], w=[r_qm[tt]])
                    cosb = ropeM[:, tt, 0, :].unsqueeze(1).broadcast_to([128, 3, 32])
                    sinb = ropeM[:, tt, 1, :].unsqueeze(1).broadcast_to([128, 3, 32])
                    t1 = T2[ti][:, hh * 96:(hh + 1) * 96].rearrange("p (h x) -> p h x", h=3)
                    t2 = T2[ti][:, 192 + hh * 96:192 + (hh + 1) * 96].rearrange("p (h x) -> p h x", h=3)
                    A("dve", lambda e, t1=t1, xv=xv, cosb=cosb: e.tensor_tensor(t1, xv[:, :, 64:96], cosb, ALU.mult), r=[rX, r_ropeM], w=[r_T2[ti]])
                    x5 = xv[:, :, 64:96].rearrange("p h (a q s) -> p h a q s", a=2, q=2)
                    t5 = t2.rearrange("p h (a q s) -> p h a q s", a=2, q=2)
                    s5 = sinb.rearrange("p h (a q s) -> p h a q s", a=2, q=2)
                    A("dve", lambda e, t5=t5, x5=x5, s5=s5: e.tensor_tensor(t5[:, :, :, 0, :], x5[:, :, :, 1, :], s5[:, :, :, 0, :], ALU.mult),
                      r=[rX, r_ropeM], w=[r_T2[ti]])
                    A("dve", lambda e, t5=t5, x5=x5, s5=s5: e.tensor_tensor(t5[:, :, :, 1, :], x5[:, :, :, 0, :], s5[:, :, :, 1, :], ALU.mult),
                      r=[rX, r_ropeM], w=[r_T2[ti]])
                    fin.append((dv, t1, t2))
                yield
                for (dv, t1, t2) in fin:
                    A("dve", lambda e, dv=dv, t1=t1, t2=t2: e.tensor_tensor(dv[:, :, 64:96], t1, t2, ALU.add), r=[r_T2[ti]], w=[r_qm[tt]])
                yield

            def v_gen():
                A("dve", lambda e: e.memset(Vm[:, :, :, 1, :], 1.0), w=[r_Vm_all])
                for kc in range(NK):
                    b, rb = bank(4 + kc % 4)
                    A("pe", lambda e, b=b, kc=kc: e.matmul(b[:, 0:384], lhsT=ckvT[:, kc * 128:(kc + 1) * 128], rhs=wukv[:], start=True, stop=True),
                      r=[r_ckvT, r_wukv], w=[rb])
                    yield
                    if kc % 2 == 0:
                        A("act", lambda e, b=b, kc=kc: e.copy(Vm[:, kc, :, 0:3:2, :], b[:, 0:384].rearrange("p (r x d) -> p r x d", r=3, x=2)), r=[rb], w=[r_Vm[kc]])
                    else:
                        A("dve", lambda e, b=b, kc=kc: e.tensor_copy(Vm[:, kc, :, 0:3:2, :], b[:, 0:384].rearrange("p (r x d) -> p r x d", r=3, x=2)), r=[rb], w=[r_Vm[kc]])
                    yield

            vg = [v_gen()]

            def step(gn):
                try:
                    next(gn)
                    return True
                except StopIteration:
                    return False

            for t0 in range(0, NT, 2):
                gens = [tile_gen(t0), tile_gen(t0 + 1)]
                while gens:
                    gens = [gn for gn in gens if step(gn)]
                    if vg[0] is not None and not step(vg[0]):
                        vg[0] = None
            while vg[0] is not None:
                if not step(vg[0]):
                    vg[0] = None

        def mla_head_prep(h):
            s = h % 2
            yield from transposes_gen([(lambda tt, h=h: qm_tm[:, tt, h, :], lambda tt: r_qm[tt], 96,
                                        lambda half, s=s: QTm[s][0:96, half * 512:(half + 1) * 512], r_QTm[s], 8, 0)], banks=(6,), evac="dve")
            for kb in range(3):
                b, rb = bank(6)
                A("pe", lambda e, b=b, kb=kb, h=h: e.matmul(b[0:96, :], lhsT=wukp[:, h, :], rhs=ckvT[:, kb * 512:(kb + 1) * 512], start=True, stop=False),
                  r=[r_wukp, r_ckvT], w=[rb])
                A("pe", lambda e, b=b, kb=kb: e.matmul(b[0:96, :], lhsT=isel, rhs=kpeT[:, kb * 512:(kb + 1) * 512], start=False, stop=True),
                  r=[r_isel, r_kpeT], w=[rb])
                A("dve", lambda e, b=b, kb=kb, s=s: e.tensor_copy(KTm[s][0:96, kb * 512:(kb + 1) * 512], b[0:96, :]), r=[rb], w=[r_KTm[s]])
                yield

        def phase_mla(l):
            for s_ in range(2):
                A("pool", lambda e, s_=s_: e.dma_start(out=QTm[s_][96:101, :], in_=d_qmask), w=[r_QTm[s_]], dma="mq%d" % s_)
                A("pool", lambda e, s_=s_: e.dma_start(out=KTm[s_][96:101, :], in_=d_kmask), w=[r_KTm[s_]], dma="mk%d" % s_)
            if l + 1 < nlayers:
                bg[0] = mod_gen(l + 1)
            for _ in mla_head_prep(0):
                pass
            for h in range(6):
                s = h % 2
                if h + 1 < 6:
                    prep[0] = mla_head_prep(h + 1)
                attend(lambda hf, s=s: QTm[s][:, hf * 512:(hf + 1) * 512], r_QTm[s],
                       lambda kc, s=s: KTm[s][:, kc * 128:(kc + 1) * 128], r_KTm[s],
                       lambda kc, h=h: Vm[:, kc, h // 2, (h % 2):(h % 2) + 2, :].rearrange("p x d -> p (x d)"), lambda kc: r_Vm[kc], 101, float(96.0 ** -0.5), 640 + 64 * h, h % 2)
            attend_flush()
            bg_flush()

        def ret_gen(l):
            for a in range(2):
                NE = NEf if a == 0 else NEb
                for h in range(4):
                    col = l * 8 + a * 4 + h
                    A("act", lambda e, a=a, h=h, col=col, NE=NE: e.activation(intra[:, a, h, :], NE[:], AF.Exp, bias=cbias[:, 1:2], scale=nlg[:, col:col + 1]),
                      r=[r_NE, r_nlg, r_cb], w=[r_intra])
                yield
            A("dve", lambda e: e.tensor_tensor(intra[:, 0, :, :], intra[:, 0, :, :], intra[:, 1, :, :], ALU.add), r=[r_intra], w=[r_intra])
            yield
            cdf = cdec.rearrange("d a e -> d (a e)")
            for t in range(8):
                pk, rk = bank(7)
                for a in range(2):
                    ch = t if a == 0 else 7 - t
                    for h in range(4):
                        A("pe", lambda e, pk=pk, a=a, h=h, ch=ch: e.matmul(pk[0:64, (a * 4 + h) * 64:(a * 4 + h + 1) * 64], lhsT=kd_tm[:, ch, a, h, :],
                                                                           rhs=V_ret[:, ch, h * 64:(h + 1) * 64], start=True, stop=True),
                          r=[r_kd[ch], r_vret[ch]], w=[rk])
                if t == 0:
                    A("dve", lambda e: e.tensor_copy(S_in[0:64, 0, :], S_aft), r=[r_Saft], w=[r_Sin[0]])
                    A("dve", lambda e: e.tensor_copy(S_in[64:128, 0, :], S_aft), r=[r_Saft], w=[r_Sin[0]])
                    A("dve", lambda e: e.tensor_tensor(S_tmp, S_aft, cdf, ALU.mult), r=[r_Saft, r_cdec], w=[r_Stmp])
                else:
                    A("dve", lambda e, t=t: e.tensor_scalar(S_in[0:64, t, :], S_aft, mcar[0:64, t:t + 1], None, ALU.mult), r=[r_Saft, r_NE], w=[r_Sin[t]])
                    A("dve", lambda e, t=t: e.tensor_scalar(S_in[64:128, t, :], S_aft, mcar[0:64, t:t + 1], None, ALU.mult), r=[r_Saft, r_NE], w=[r_Sin[t]])
                    A("dve", lambda e, t=t: e.scalar_tensor_tensor(S_tmp, S_aft, mcar[0:64, t:t + 1], cdf, ALU.mult, ALU.mult),
                      r=[r_Saft, r_cdec, r_NE], w=[r_Stmp])
                yield
                A("dve", lambda e, pk=pk: e.tensor_tensor(S_aft, S_tmp, pk[0:64, :], ALU.add), r=[r_Stmp, rk], w=[r_Saft])
                if t % 2 == 1:
                    A("dve", lambda e: e.tensor_copy(S_out, S_aft), r=[r_Saft], w=[r_Sout])
                    A("sp", lambda e, t=t: e.dma_start(out=o_st[l, t // 2], in_=S_out), r=[r_Sout], dma="sto")
                yield
            for h in range(4):
                p0 = 64 * (h % 2); pr = h // 2
                qTh = qkT[p0:p0 + 64, pr, 0, :]; kTh = qkT[p0:p0 + 64, pr, 1, :]
                for a in range(2):
                    A("pool", lambda e, a=a, h=h, p0=p0, qTh=qTh: e.tensor_tensor(qd[p0:p0 + 64, a, :].rearrange("d (c i) -> d c i", c=8), qTh.rearrange("d (c i) -> d c i", c=8),
                                                                                 qdec[p0:p0 + 64, a, h, :].unsqueeze(1).broadcast_to([64, 8, 128]), ALU.mult),
                      r=[r_qkT[h], r_qdec], w=[r_qd])
                yield
                c = (64 * h) // 128
                Oh = T2[0][p0:p0 + 64, :]; rOh = r_T2[0]
                W2 = T2[1][p0:p0 + 64, :]; rW2 = r_T2[1]
                W2b = sqb[1][p0:p0 + 64, 0:512]
                for half in range(2):
                    pq, rq = bank(7)
                    for j in range(4):
                        ch = half * 4 + j
                        A("pe", lambda e, pq=pq, j=j, ch=ch, kTh=kTh, qTh=qTh: e.matmul(pq[:, j * 128:(j + 1) * 128], lhsT=kTh[:, ch * 128:(ch + 1) * 128],
                                                                                         rhs=qTh[:, ch * 128:(ch + 1) * 128], start=True, stop=True),
                          r=[r_qkT[h]], w=[rq])
                    at = AT[half]; rat = r_AT[half]
                    A("dve", lambda e, at=at, pq=pq, h=h: e.tensor_tensor(at[:, 0, :].rearrange("j (c i) -> j c i", c=4), pq.rearrange("j (c i) -> j c i", c=4),
                                                                          intra[:, 0, h, :].unsqueeze(1).broadcast_to([128, 4, 128]), ALU.mult),
                      r=[rq, r_intra], w=[rat])
                    yield
                    pO, rO = bank(7)
                    for j in range(4):
                        ch = half * 4 + j
                        cols = slice(j * 128, (j + 1) * 128)
                        A("pe", lambda e, cols=cols, ch=ch, h=h, at=at, j=j, pO=pO: e.matmul(pO[0:64, cols], lhsT=V_ret[:, ch, h * 64:(h + 1) * 64],
                                                                                              rhs=at[:, 0, j * 128:(j + 1) * 128], start=True, stop=False),
                          r=[r_vret[ch], rat], w=[rO])
                        A("pe", lambda e, cols=cols, ch=ch, h=h, p0=p0, pO=pO: e.matmul(pO[0:64, cols], lhsT=S_in[p0:p0 + 64, ch, h * 64:(h + 1) * 64],
                                                                                         rhs=qd[p0:p0 + 64, 0, ch * 128:(ch + 1) * 128], start=False, stop=False),
                          r=[r_Sin[ch], r_qd], w=[rO])
                        A("pe", lambda e, cols=cols, ch=ch, h=h, p0=p0, pO=pO: e.matmul(pO[0:64, cols], lhsT=S_in[p0:p0 + 64, 7 - ch, (4 + h) * 64:(5 + h) * 64],
                                                                                         rhs=qd[p0:p0 + 64, 1, ch * 128:(ch + 1) * 128], start=False, stop=True),
                          r=[r_Sin[7 - ch], r_qd], w=[rO])
                    yield
                    A("dve", lambda e, Oh=Oh, pO=pO: e.tensor_copy(Oh, pO[0:64, :]), r=[rO], w=[rOh])
                    A("pool", lambda e, Oh=Oh, W2b=W2b: e.tensor_tensor(W2b, Oh, Oh, ALU.mult), r=[rOh], w=[rW2])
                    yield
                    pss, rss = bank(7)
                    A("pe", lambda e, pss=pss, W2b=W2b, p0=p0: e.matmul(pss[0:64, :], lhsT=ones64[p0:p0 + 64, :], rhs=W2b, start=True, stop=True),
                      r=[rW2, r_o64], w=[rss])
                    A("act", lambda e, W2=W2, pss=pss, p0=p0: e.activation(W2, pss[0:64, :], AF.Ln, bias=cbias[p0:p0 + 64, 0:1], scale=1.0), r=[rss, r_cb], w=[rW2])
                    A("act", lambda e, W2=W2: e.activation(W2, W2, AF.Exp, scale=-0.5), r=[rW2], w=[rW2])
                    yield
                    A("dve", lambda e, Oh=Oh, W2=W2: e.tensor_tensor(Oh, Oh, W2, ALU.mult), r=[rOh, rW2], w=[rOh])
                    A("pool", lambda e, Oh=Oh, p0=p0, c=c, half=half: e.tensor_tensor(hyT[p0:p0 + 64, c, half * 512:(half + 1) * 512], Oh,
                                                                                    szT[p0:p0 + 64, c, half * 512:(half + 1) * 512], ALU.mult),
                      r=[rOh, r_sz[c]], w=[r_hy[c]])
                    yield

        def phase_ret(l):
            for _ in ret_gen(l):
                pass

        def phase_outproj(l):
            m = modl[l]
            for pc in range(2):
                wt, wr = ws_get()
                for j in range(4):
                    cc = pc * 4 + j
                    for hf in range(2):
                        b, rb = bank((j * 2 + hf) % 8)
                        for kc in range(8):
                            A("pe", lambda e, b=b, wt=wt, j=j, kc=kc, hf=hf: e.matmul(b[:], lhsT=wt[:, kc, j * 128:(j + 1) * 128],
                                                                                        rhs=hyT[:, kc, hf * 512:(hf + 1) * 512], start=(kc == 0), stop=(kc == 7)),
                              r=[wr, r_hy[kc]], w=[rb])
                        A("dve", lambda e, b=b, cc=cc, hf=hf: e.scalar_tensor_tensor(xT[:, cc, hf * 512:(hf + 1) * 512], b[:], m[:, 16 + cc:17 + cc],
                                                                                       xT[:, cc, hf * 512:(hf + 1) * 512], ALU.mult, ALU.add),
                          r=[rb, r_modl[l], r_xT[cc]], w=[r_xT[cc]])
                ws_issue()

        def phase_final():
            sumsq_bc(big1[:], r_big1, lambda c: xT[:, c, :], r_xT, 8, ones_mean, r_om, 128, EPS, 0)
            tap("rstdF", big1, [128, T], [r_big1])
            for c in range(8):
                A("dve", lambda e, c=c: e.scalar_tensor_tensor(xT[:, c, :], xT[:, c, :], vT[:, 8 + c:9 + c], big1[:], ALU.mult, ALU.mult),
                  r=[r_xT[c], r_vT, r_big1], w=[r_xT[c]])
            for tt in range(NT):
                s = tt % 2
                for hf in range(2):
                    b, rb = bank(2 * s + hf + 4 * 0)
                    for j in range(4):
                        c = hf * 4 + j
                        A("pe", lambda e, b=b, j=j, c=c, tt=tt: e.transpose(b[:, j * 128:(j + 1) * 128], xT[:, c, tt * 128:(tt + 1) * 128], ident_f[:]),
                          r=[r_xT[c], r_idf], w=[rb])
                    if hf == 0:
                        A("dve", lambda e, b=b, s=s, hf=hf: e.tensor_copy(xst[s][:, hf * 512:(hf + 1) * 512], b[:]), r=[rb], w=[r_xst[s]])
                    else:
                        A("act", lambda e, b=b, s=s, hf=hf: e.copy(xst[s][:, hf * 512:(hf + 1) * 512], b[:]), r=[rb], w=[r_xst[s]])
                A("sp", lambda e, s=s, tt=tt: e.dma_start(out=o_y[tt * 128:(tt + 1) * 128, :], in_=xst[s][:]), r=[r_xst[s]], dma="xst%d" % s)

        tap("xin", xT[:], [128, 8, T], r_xT)
        tap("vT", vT[:], [128, 144], [r_vT])
        if upto >= 1:
            load_small(0, skip_gkc=True)
            mg0 = mod_gen(0)

            def _cb():
                for _ in range(3):
                    try:
                        next(mg0)
                    except StopIteration:
                        pass
            x_load(_cb)
            load_small(0, gkc_only=True)
            for _ in mg0:
                pass
        else:
            x_load(lambda: None)
        for l in range(nlayers):
            if upto < 1:
                break
            if l > 0:
                load_small(l)
            phase_consts(l)
            phase_norm(l)
            tap("hT%d" % l, hyT[:], [128, 8, T], r_hy)
            if upto < 2:
                break
            phase_inproj(l)
            tap("g%d" % l, g_tm, [128, 8, 512], r_g)
            tap("sz%d" % l, szT[:], [128, 8, T], r_sz)
            if upto < 3:
                break
            if upto < 4:
                phase_ret(l)
                tap("yT%d" % l, hyT[:], [128, 8, T], r_hy)
                break
            phase_gqa(l)
            if upto < 5:
                tap("yT%d" % l, hyT[:], [128, 8, T], r_hy)
                break
            phase_mla_proj(l)
            tap("qm%d" % l, qm_all, [128, 4608], r_qm)
            tap("vm%d" % l, Vm_all, [128, 6912], r_Vm)
            tap("ckvT%d" % l, ckvT[:], [128, 1536], [r_ckvT])
            tap("kpeT%d" % l, kpeT, [32, 1536], [r_kpeT])
            phase_mla(l)
            tap("ktm%d" % l, KTm[1], [101, 1536], r_KTm[1])
            tap("qtm%d" % l, QTm[1], [101, 1024], r_QTm[1])
            tap("yT%d" % l, hyT[:], [128, 8, T], r_hy)
            if upto < 6:
                break
            phase_outproj(l)
            tap("xT%d" % l, xT[:], [128, 8, T], r_xT)
        phase_final()
        P.emit(nc)
    return nc, P, tap_out


def _rope_tables(n_tok, dim, grid_w=64, theta=10000.0):
    rows = n_tok // grid_w
    row = np.repeat(np.arange(rows), grid_w).astype(np.float32)
    col = (np.arange(n_tok) % grid_w).astype(np.float32)
    half = dim // 2
    inv = (theta ** (-np.arange(0, half, 2, dtype=np.float32) / half)).astype(np.float32)
    ar = row[:, None] * inv[None]
    ac = col[:, None] * inv[None]
    ang = np.concatenate([ar, ar, ac, ac], -1)
    cos = np.cos(ang).astype(np.float32)
    sin = np.sin(ang).astype(np.float32)
    qd = half // 2
    sgn = np.concatenate([-np.ones(qd), np.ones(qd), -np.ones(qd), np.ones(qd)]).astype(np.float32)
    return np.stack([cos, sin * sgn[None]], 1)


def _const_tables():
    i = np.arange(128, dtype=np.float32)
    rel = i[None, :] - i[:, None]
    ne_f = np.where(rel >= 0, -rel, -1.0e7).astype(np.float32)
    ne_b = np.where(rel <= 0, rel, -1.0e7).astype(np.float32)
    ne_q = np.concatenate([-(i + 1.0), -(128.0 - i)]).astype(np.float32)
    ne_k = np.stack([-(127.0 - i), -i], 1).astype(np.float32)
    return ne_f, ne_b, ne_q, ne_k


def make_in_maps(inputs):
    f = lambda a: np.ascontiguousarray(np.asarray(a, dtype=np.float32))
    xp = f(inputs["x_prompt"]); xs = f(inputs["x_sample"])
    ne_f, ne_b, ne_q, ne_k = _const_tables()
    shared = {
        "w_ada": f(inputs["w_ada"]), "b_ada": f(inputs["b_ada"]).reshape(96, 128), "w_in": f(inputs["w_in"]),
        "w_uq": f(inputs["w_uq"]), "w_ukv": f(inputs["w_ukv"]), "w_out": f(inputs["w_out"]),
        "norm_g": f(inputs["norm_g"]).reshape(32, 128), "final_norm": f(inputs["final_norm"]).reshape(8, 128),
        "ret_logit": f(inputs["ret_decay_logit"]).reshape(32), "gq_g": f(inputs["gqa_q_norm"]).reshape(256),
        "gk_g": f(inputs["gqa_k_norm"]).reshape(256), "mq_g": f(inputs["mla_q_norm"]).reshape(1024),
        "mkv_g": f(inputs["mla_kv_norm"]).reshape(512),
        "ne_f": ne_f, "ne_b": ne_b, "ne_q": ne_q, "ne_k": ne_k,
    }
    kmask = np.zeros((5, 1536), np.float32)
    kmask[4, 0:512] = 1.0
    for s in range(4):
        kmask[s, 512 + 256 * s:512 + 256 * (s + 1)] = 1.0
    rope_s_hd = _rope_tables(1024, 64); rope_s_m = _rope_tables(1024, 32)
    rope_p_hd = np.zeros((1024, 2, 64), np.float32); rope_p_hd[:, 0] = 1.0
    rope_p_m = np.zeros((1024, 2, 32), np.float32); rope_p_m[:, 0] = 1.0
    maps = []
    for core in range(8):
        m = dict(shared)
        m["kmask"] = kmask
        if core < 4:
            b = core
            m["xin"] = xs[b]
            m["cond"] = f(inputs["c"])[b].reshape(8, 128)
            m["c_gk"] = f(inputs["cache_gqa_k"])[b].reshape(4, 512, 128)
            m["c_gv"] = f(inputs["cache_gqa_v"])[b].reshape(4, 512, 128)
            m["c_ckv"] = f(inputs["cache_mla_ckv"])[b]
            m["c_kpe"] = f(inputs["cache_mla_kpe"])[b]
            m["state0"] = f(inputs["state_ret"])[b]
            m["rope_hd"] = rope_s_hd; m["rope_m"] = rope_s_m
            m["qmask"] = np.zeros((5, 1024), np.float32)
            m["carry"] = np.ones(8, np.float32)
        else:
            b0 = 4 * (core - 4)
            m["xin"] = xp[b0:b0 + 4].reshape(1024, 1024)
            m["cond"] = f(inputs["c_ctx"]).reshape(8, 128)
            m["c_gk"] = np.zeros((4, 512, 128), np.float32)
            m["c_gv"] = np.zeros((4, 512, 128), np.float32)
            m["c_ckv"] = np.zeros((4, 512, 128), np.float32)
            m["c_kpe"] = np.zeros((4, 512, 32), np.float32)
            m["state0"] = np.zeros((4, 2, 4, 64, 64), np.float32)
            m["rope_hd"] = rope_p_hd; m["rope_m"] = rope_p_m
            qm = np.full((5, 1024), NEG_BIG, np.float32)
            for s in range(4):
                qm[s, 256 * s:256 * (s + 1)] = 0.0
            m["qmask"] = qm
            m["carry"] = np.array([1, 1, 0, 1, 0, 1, 0, 1], np.float32)
        maps.append({k: np.ascontiguousarray(v) for k, v in m.items()})
    return maps


def assemble(results):
    y_prompt = np.zeros((16, 256, 1024), np.float32)
    y_sample = np.zeros((4, 1024, 1024), np.float32)
    st = np.zeros((16, 4, 2, 4, 64, 64), np.float32)
    gk = np.zeros((16, 4, 256, 2, 64), np.float32)
    gv = np.zeros((16, 4, 256, 2, 64), np.float32)
    ckv = np.zeros((16, 4, 256, 128), np.float32)
    kpe = np.zeros((16, 4, 256, 32), np.float32)
    for core in range(8):
        r = results[core]
        if core < 4:
            y_sample[core] = r["y"]
            continue
        b0 = 4 * (core - 4)
        y_prompt[b0:b0 + 4] = r["y"].reshape(4, 256, 1024)
        so = r["st_out"].reshape(4, 4, 64, 2, 4, 64)
        for s in range(4):
            st[b0 + s, :, 0] = so[:, s, :, 0].transpose(0, 2, 1, 3)
            st[b0 + s, :, 1] = so[:, 3 - s, :, 1].transpose(0, 2, 1, 3)
        gk[b0:b0 + 4] = r["gk_out"].reshape(4, 4, 256, 2, 64).transpose(1, 0, 2, 3, 4)
        gv[b0:b0 + 4] = r["gv_out"].reshape(4, 4, 256, 2, 64).transpose(1, 0, 2, 3, 4)
        ckv[b0:b0 + 4] = r["ckv_out"].reshape(4, 4, 256, 128).transpose(1, 0, 2, 3)
        kpe[b0:b0 + 4] = r["kpe_out"].reshape(4, 4, 256, 32).transpose(1, 0, 2, 3)
    return (y_prompt, y_sample, st, gk, gv, ckv, kpe)


_CACHE = {}


def kernel(**inputs):
    if "nc" not in _CACHE:
        _CACHE["nc"] = build()[0]
    nc = _CACHE["nc"]
    maps = make_in_maps(inputs)
    res = run_bass_kernel_spmd(nc, maps, core_ids=list(range(8)))
    return assemble(res.results)
```

```python
import contextlib
import numpy as np
import concourse.bass as bass
import concourse.mybir as mybir
from concourse.bass_utils import run_bass_kernel_spmd

F32 = mybir.dt.float32
BF16 = mybir.dt.bfloat16
F32R = mybir.dt.float32r
AF = mybir.ActivationFunctionType
ALU = mybir.AluOpType
AX = mybir.AxisListType

ENGS = ("pe", "act", "dve", "pool", "sp")
EPS = 1e-6
NEG_BIG = -1.0e4


class Res:
    __slots__ = ("name", "lw", "rd", "excl")

    def __init__(self, name, excl=False):
        self.name = name
        self.lw = None
        self.rd = []
        self.excl = excl


class Op:
    __slots__ = ("eng", "fn", "waits", "idx", "milestone", "dma_sem", "dma_val", "ms_val")

    def __init__(self, eng, fn):
        self.eng = eng
        self.fn = fn
        self.waits = {}
        self.idx = None
        self.milestone = False
        self.dma_sem = None
        self.dma_val = None
        self.ms_val = None


class Prog:
    def __init__(self):
        self.ops = {e: [] for e in ENGS}
        self.known = {e: {} for e in ENGS}
        self.dma_count = {}
        self.dma_sems = []

    def _dep(self, op, key_val, same_ok=False):
        if key_val is None:
            return
        key, val = key_val
        if key == op.eng and (key == "pe" or same_ok):
            return
        if self.known[op.eng].get(key, -1) >= val:
            return
        if op.waits.get(key, -1) < val:
            op.waits[key] = val

    def add(self, eng, fn, reads=(), writes=(), dma_sem=None):
        op = Op(eng, fn)
        op.idx = len(self.ops[eng])
        for r in reads:
            self._dep(op, r.lw)
            if r.excl:
                for rd in r.rd:
                    self._dep(op, rd, same_ok=True)
        for w in writes:
            self._dep(op, w.lw)
            for rd in w.rd:
                self._dep(op, rd)
        kn = self.known[eng]
        for k, v in op.waits.items():
            if kn.get(k, -1) < v:
                kn[k] = v
        if dma_sem is not None:
            if dma_sem not in self.dma_count:
                self.dma_count[dma_sem] = 0
                self.dma_sems.append(dma_sem)
            self.dma_count[dma_sem] += 16
            op.dma_sem = dma_sem
            op.dma_val = self.dma_count[dma_sem]
            me = ("dma:" + dma_sem, op.dma_val)
        else:
            me = (eng, op.idx)
        for r in reads:
            r.rd.append(me)
        for w in writes:
            w.lw = me
            w.rd = []
        self.ops[eng].append(op)
        return op

    def emit(self, nc):
        for e in ENGS:
            for op in self.ops[e]:
                for k, v in op.waits.items():
                    if not k.startswith("dma:"):
                        self.ops[k][v].milestone = True
        for e in ENGS:
            c = 0
            for op in self.ops[e]:
                if op.milestone:
                    c += 1
                    op.ms_val = c
        with contextlib.ExitStack() as st:
            sems = {}
            for e in ENGS:
                sems[e] = st.enter_context(nc.semaphore("s_" + e))
            for d in self.dma_sems:
                sems["dma:" + d] = st.enter_context(nc.semaphore("d_" + d))
            block = st.enter_context(nc.Block())
            prog = self

            def run(eng_name, eng):
                embed_ok = eng_name in ("act", "dve")
                for op in prog.ops[eng_name]:
                    wl = [(sems[k], (v if k.startswith("dma:") else prog.ops[k][v].ms_val)) for k, v in op.waits.items()]
                    emb = None
                    if embed_ok and op.dma_sem is None and wl:
                        emb = wl.pop()
                    for (sm, vv) in wl:
                        eng.wait_ge(sm, vv)
                    ins = op.fn(eng)
                    if emb is not None:
                        ins._wait_ge(emb[0], emb[1])
                    if op.dma_sem is not None:
                        ins.then_inc(sems["dma:" + op.dma_sem], 16)
                    elif op.milestone:
                        ins.then_inc(sems[eng_name], 1)

            @block.tensor
            def _(eng):
                run("pe", eng)

            @block.scalar
            def _(eng):
                run("act", eng)

            @block.vector
            def _(eng):
                run("dve", eng)

            @block.gpsimd
            def _(eng):
                run("pool", eng)

            @block.sync
            def _(eng):
                run("sp", eng)
                for d in prog.dma_sems:
                    eng.wait_ge(sems["dma:" + d], prog.dma_count[d])


T = 1024
NT = 8
NK = 12
IN_W = 2848


def build(nlayers=4, taps=(), upto=99):
    nc = bass.Bass("TRN2", target_bir_lowering=False)
    P = Prog()
    taps = set(taps)
    tap_out = {}

    def din(name, shape):
        return nc.dram_tensor(name, list(shape), F32, kind="ExternalInput").ap()

    def dout(name, shape):
        return nc.dram_tensor(name, list(shape), F32, kind="ExternalOutput").ap()

    d_x = din("xin", [T, 1024])
    d_cond = din("cond", [8, 128])
    d_cgk = din("c_gk", [4, 512, 128])
    d_cgv = din("c_gv", [4, 512, 128])
    d_cckv = din("c_ckv", [4, 512, 128])
    d_ckpe = din("c_kpe", [4, 512, 32])
    d_st0 = din("state0", [4, 2, 4, 64, 64])
    d_wada = din("w_ada", [4, 1024, 3072])
    d_bada = din("b_ada", [96, 128])
    d_win = din("w_in", [4, 1024, IN_W])
    d_wuq = din("w_uq", [4, 256, 576])
    d_wukv = din("w_ukv", [4, 128, 768])
    d_wout = din("w_out", [4, 1024, 1024])
    d_ng = din("norm_g", [32, 128])
    d_fn = din("final_norm", [8, 128])
    d_logit = din("ret_logit", [32])
    d_gqg = din("gq_g", [256])
    d_gkg = din("gk_g", [256])
    d_mqg = din("mq_g", [1024])
    d_mkvg = din("mkv_g", [512])
    d_ropehd = din("rope_hd", [T, 2, 64])
    d_ropem = din("rope_m", [T, 2, 32])
    d_qmask = din("qmask", [5, T])
    d_kmask = din("kmask", [5, 1536])
    d_carry = din("carry", [8])
    d_nef = din("ne_f", [128, 128])
    d_neb = din("ne_b", [128, 128])
    d_neq = din("ne_q", [256])
    d_nek = din("ne_k", [128, 2])

    o_y = dout("y", [T, 1024])
    o_st = dout("st_out", [4, 4, 64, 512])
    o_gk = dout("gk_out", [4, T, 128])
    o_gv = dout("gv_out", [4, T, 128])
    o_ckv = dout("ckv_out", [4, T, 128])
    o_kpe = dout("kpe_out", [4, T, 32])

    with contextlib.ExitStack() as st:
        cnt = [0]

        def sbt(name, shape, dt):
            return st.enter_context(nc.sbuf_tensor(name, list(shape), dt))

        def R(name, excl=False):
            return Res(name, excl)

        def _flat(x):
            out = []
            for it in x:
                if isinstance(it, (list, tuple)):
                    out.extend(_flat(it))
                else:
                    out.append(it)
            return out

        def A(eng, fn, r=(), w=(), dma=None):
            return P.add(eng, fn, reads=_flat(r), writes=_flat(w), dma_sem=dma)

        def uniq(prefix):
            cnt[0] += 1
            return "%s%d" % (prefix, cnt[0])

        PP = [st.enter_context(nc.psum_tensor("pp%d" % i, [128, 1024], F32)) for i in range(4)]
        PR = [[R("pp%d_%d" % (i, j), excl=True) for j in range(2)] for i in range(4)]

        def bank(i):
            return PP[i // 2][:, (i % 2) * 512:(i % 2) * 512 + 512], PR[i // 2][i % 2]

        class Region:
            def __init__(self, name, nbytes, cell):
                self.t = sbt(name, [128, nbytes // 2], BF16)
                self.cell = cell
                self.cells = [R("%s_c%d" % (name, i)) for i in range((nbytes + cell - 1) // cell)]
                self.nbytes = nbytes

            def view(self, off, nbytes, dt, p0=0, p1=128):
                assert off % 4 == 0 and off + nbytes <= self.nbytes, (off, nbytes, self.nbytes)
                ap = self.t[p0:p1, off // 2:(off + nbytes) // 2]
                if dt != BF16:
                    ap = ap.bitcast(dt)
                res = self.cells[off // self.cell:(off + nbytes - 1) // self.cell + 1]
                return ap, res

        xT = sbt("xT", [128, 8, T], F32)
        r_xT = [R("xT%d" % c) for c in range(8)]
        hyT = sbt("hyT", [128, 8, T], BF16)
        r_hy = [R("hy%d" % c) for c in range(8)]
        szT = sbt("szT", [128, 8, T], BF16)
        r_sz = [R("sz%d" % c) for c in range(8)]

        cbias = sbt("cbias", [128, 8], F32); r_cb = R("cb")
        ident_f = sbt("ident_f", [128, 128], F32); r_idf = R("idf")
        ident_b = sbt("ident_b", [128, 128], BF16); r_idb = R("idb")
        ones_mean = sbt("ones_mean", [128, 128], BF16); r_om = R("om")
        ones64 = sbt("ones64", [128, 64], BF16); r_o64 = R("o64")
        ones_r = sbt("ones_r", [128, 64], F32); r_onr = R("onr")

        NSLOT = 3
        ws_t = [sbt("ws%d" % i, [128, 8, 512], BF16) for i in range(NSLOT)]
        ws_r = [R("ws%d" % i) for i in range(NSLOT)]

        wuq = sbt("wuq", [128, 2, 576], BF16); r_wuq = R("wuq")
        wukv = sbt("wukvV", [128, 384], BF16); r_wukv = R("wukvV")
        wukp = sbt("wukp", [128, 6, 96], BF16); r_wukp = R("wukp")

        vT = sbt("vT", [128, 144], F32); r_vT = R("vT")
        scond = sbt("scond", [128, 8], BF16); r_scond = R("scond")
        modl = [sbt("modl%d" % l, [128, 32], F32) for l in range(4)]
        r_modl = [R("modl%d" % l) for l in range(4)]

        Gq8 = sbt("Gq8", [128, 64], F32); Gk8 = sbt("Gk8", [128, 64], F32)
        Gmq = sbt("Gmq", [128, 256], F32); Gmkv = sbt("Gmkv", [128, 128], F32)
        r_G = R("G")
        RD = Region("RD", 4096, 4096)
        ropeG_all, r_ropeG = RD.view(0, 4096, F32); ropeG = ropeG_all.rearrange("p (t a d) -> p t a d", t=8, a=2)
        ropeM = sbt("ropeM", [128, 8, 2, 32], F32); r_ropeM = R("ropeM")
        nlg = sbt("nlg", [128, 32], F32); r_nlg = R("nlg")
        NEf = sbt("NEf", [128, 128], BF16); NEb = sbt("NEb", [128, 128], BF16)
        NEq = sbt("NEq", [128, 2, 128], F32); NEk = sbt("NEk", [128, 2], F32)
        mcar = sbt("mcar", [128, 8], F32)
        r_NE = R("NE")

        qdec = sbt("qdec", [128, 2, 4, 128], BF16); r_qdec = R("qdec")
        kdec = sbt("kdec", [128, 8], F32); r_kdec = R("kdec")
        mix1 = sbt("mix1", [128, 1024], F32)
        cdec = mix1[0:64, 0:512].rearrange("d (a e) -> d a e", a=8); r_cdec = R("cdec")
        S_out = mix1[0:64, 512:1024]; r_Sout = R("Sout")
        kpeT = mix1[64:96, 0:768].bitcast(BF16); r_kpeT = R("kpeT")
        isel = mix1[64:96, 768:816].bitcast(BF16); r_isel = R("isel")
        mix2 = sbt("mix2", [128, 1024], F32)
        S_aft = mix2[0:64, 0:512]; r_Saft = R("Saft")
        S_tmp = mix2[0:64, 512:1024]; r_Stmp = R("Stmp")
        rowr = sbt("rowr", [65, 512], F32); r_rowr = R("rowr")

        V_ret = sbt("V_ret", [128, 8, 256], BF16); r_vret = [R("vret%d" % t) for t in range(8)]
        ckvn = sbt("ckvn", [128, 12, 128], BF16); r_ckvn = [R("ckvn%d" % t) for t in range(12)]
        kpeb = sbt("kpeb", [128, 12, 32], BF16); r_kpeb = [R("kpeb%d" % t) for t in range(12)]

        Vg = sbt("Vg", [128, 12, 3, 64], BF16); r_Vg = [R("Vg%d" % k) for k in range(12)]
        QTg = sbt("QTg", [128, 2, T], BF16); r_QTg = [R("QTg%d" % s_) for s_ in range(2)]
        KTg = sbt("KTg", [128, 2, 1536], BF16); r_KTg = [R("KTg%d" % g) for g in range(2)]
        cqnT = sbt("cqnT", [128, 2, T], BF16); r_cqnT = R("cqnT")
        ckvT = sbt("ckvT", [128, 1536], BF16); r_ckvT = R("ckvT")
        small = [sbt("small%d" % i, [128, 16], F32) for i in range(4)]; r_small = [R("small%d" % i) for i in range(4)]
        stage0 = sbt("stage0", [128, 416], F32)
        r_stg = [R("stg_gk"), R("stg_gv"), R("stg_ckv"), R("stg_kpe")]

        RA = Region("RA", 8192, 1024)
        PT = []; r_PT = []
        for i in range(4):
            ap_, rs_ = RA.view(i * 2048, 2048, BF16); PT.append(ap_); r_PT.append(rs_)
        xst = []; r_xst = []
        for i in range(2):
            ap_, rs_ = RA.view(i * 4096, 4096, F32); xst.append(ap_); r_xst.append(rs_)
        qk_all, _ = RA.view(0, 8192, BF16)
        qk_tm = qk_all.rearrange("p (t c) -> p t c", t=8)
        r_qk = [[RA.cells[t]] for t in range(8)]
        gkc_all, r_gkc = RA.view(4096, 1024, BF16); gkc = gkc_all.rearrange("p (k d) -> p k d", k=4)
        cqn_all, _ = RA.view(0, 4096, BF16); cqn = cqn_all.rearrange("p (t c) -> p t c", t=8)
        r_cqn = [[RA.cells[t // 2]] for t in range(8)]
        intra_all, r_intra = RD.view(0, 4096, F32); intra = intra_all.rearrange("p (a h i) -> p a h i", a=2, h=4)

        RB = Region("RB", 40960, 1024)
        g_all, _ = RB.view(0, 8192, BF16); g_tm = g_all.rearrange("p (t c) -> p t c", t=8)
        r_g = [[RB.cells[t]] for t in range(8)]
        kd_all, _ = RB.view(8192, 8192, BF16); kd_tm = kd_all.rearrange("p (t a h d) -> p t a h d", t=8, a=2, h=4)
        r_kd = [[RB.cells[8 + t]] for t in range(8)]
        qkT_all, r_qkT_all = RB.view(16384, 8192, BF16); qkT = qkT_all.rearrange("p (r k t) -> p r k t", r=2, k=2)
        r_qkT = [RB.cells[16 + 4 * (h // 2):16 + 4 * (h // 2) + 4] for h in range(4)]
        Sin_all, _ = RB.view(24576, 8192, BF16); S_in = Sin_all.rearrange("p (t c) -> p t c", t=8)
        r_Sin = [[RB.cells[24 + t]] for t in range(8)]
        qd_all, r_qd_all = RB.view(32768, 4096, BF16); qd = qd_all.rearrange("p (a t) -> p a t", a=2)
        r_qd = r_qd_all
        AT = []; r_AT = []
        for i in range(2):
            ap_, rs_ = RB.view(36864 + i * 2048, 2048, BF16); AT.append(ap_.rearrange("p (a c) -> p a c", a=2)); r_AT.append(rs_)
        qm_all, _ = RB.view(0, 9216, BF16); qm_tm = qm_all.rearrange("p (t h x) -> p t h x", t=8, h=6)
        r_qm = [RB.cells[(t * 1152) // 1024:(t * 1152 + 1151) // 1024 + 1] for t in range(8)]
        Vm_all, r_Vm_all = RB.view(9216, 13824, BF16); Vm = Vm_all.rearrange("p (k r x d) -> p k r x d", k=12, r=3, x=3)
        r_Vm = [RB.cells[(9216 + k * 1152) // 1024:(9216 + k * 1152 + 1151) // 1024 + 1] for k in range(12)]
        KTm = []; r_KTm = []
        QTm = []; r_QTm = []
        for i in range(2):
            ap_, rs_ = RB.view(23040 + i * 3072, 3072, BF16, 0, 101); KTm.append(ap_); r_KTm.append(rs_)
        for i in range(2):
            ap_, rs_ = RB.view(29184 + i * 2048, 2048, BF16, 0, 101); QTm.append(ap_); r_QTm.append(rs_)

        RC = Region("RC", 12288, 2048)
        Xs = []; r_Xs = []; T1 = []; r_T1 = []; T2 = []; r_T2 = []; sqb = []; r_sqb = []
        for i in range(2):
            ap_, rs_ = RC.view(i * 2048, 2048, F32); Xs.append(ap_); r_Xs.append(rs_)
            ap_, rs_ = RC.view(4096 + i * 2048, 2048, F32); T1.append(ap_); r_T1.append(rs_)
            ap_, rs_ = RC.view(8192 + i * 2048, 2048, F32); T2.append(ap_); r_T2.append(rs_)
            ap_, rs_ = RC.view(8192 + i * 2048, 2048, BF16); sqb.append(ap_); r_sqb.append(rs_)
        vecA_all, r_vecA = RC.view(8192, 512, F32); vecA = vecA_all[0:48, :]
        vecB_all, r_vecB = RC.view(10240, 512, F32); vecB = vecB_all[0:96, :]
        big1, r_big1 = RC.view(0, 4096, F32)
        big2, r_big2 = RC.view(4096, 4096, F32)
        for i in range(2):
            ap_, rs_ = RB.view(24576 + i * 6144, 2048, F32); Xs.append(ap_); r_Xs.append(rs_)
            ap_, rs_ = RB.view(24576 + i * 6144 + 2048, 2048, F32); T1.append(ap_); r_T1.append(rs_)
            ap_, rs_ = RB.view(24576 + i * 6144 + 4096, 2048, F32); T2.append(ap_); r_T2.append(rs_)
        NTMP = 4

        def dma_in(eng, out_ap, in_ap, w, nonctg=False):
            name = uniq("su")
            if nonctg:
                return A(eng, lambda e: e.dma_start(out=out_ap, in_=in_ap, allow_slow_non_contiguous=True), w=w, dma=name)
            return A(eng, lambda e: e.dma_start(out=out_ap, in_=in_ap), w=w, dma=name)

        def x_dma(tt):
            s_ = tt % 2
            A("sp", lambda e: e.dma_start(out=xst[s_][:], in_=d_x[tt * 128:(tt + 1) * 128, :]), w=[r_xst[s_]], dma="xst%d" % s_)
        x_dma(0)
        x_dma(1)

        A("dve", lambda e: e.memset(cbias[:, 0:1], EPS), w=[r_cb])
        A("dve", lambda e: e.memset(cbias[:, 1:2], float(np.log(0.125))), w=[r_cb])
        A("dve", lambda e: e.memset(cbias[:, 2:3], 1.0), w=[r_cb])
        A("dve", lambda e: e.memset(cbias[:, 3:4], 64.0 * EPS), w=[r_cb])
        A("dve", lambda e: e.memset(cbias[:, 4:5], 128.0 * EPS), w=[r_cb])
        A("dve", lambda e: e.memset(cbias[:, 5:6], 256.0 * EPS), w=[r_cb])
        A("dve", lambda e: e.memset(cbias[:, 6:7], -1.0), w=[r_cb])
        A("dve", lambda e: e.memset(ident_f[:], 0.0), w=[r_idf])
        A("pool", lambda e: e.affine_select(out=ident_f[:], in_=ident_f[:], pattern=[[-1, 128]], compare_op=ALU.not_equal,
                                            fill=1.0, base=0, channel_multiplier=1), r=[r_idf], w=[r_idf])
        A("dve", lambda e: e.tensor_copy(ident_b[:], ident_f[:]), r=[r_idf], w=[r_idb])
        A("dve", lambda e: e.memset(ones_mean[:], 1.0 / 1024.0), w=[r_om])
        A("dve", lambda e: e.memset(ones64[:], 1.0 / 64.0), w=[r_o64])
        A("dve", lambda e: e.memset(big2[:, 0:64], 1.0), w=[r_big2])
        A("dve", lambda e: e.tensor_copy(ones_r[:].bitcast(F32R), big2[:, 0:64]), r=[r_big2], w=[r_onr])
        A("dve", lambda e: e.memset(isel, 0.0), w=[r_isel])
        A("dve", lambda e: e.tensor_copy(isel[:, 64:96], ident_f[0:32, 0:32]), r=[r_idf, r_isel], w=[r_isel])
        A("dve", lambda e: e.memset(wukp[:], 0.0), w=[r_wukp])
        A("dve", lambda e: e.memset(Vg[:, :, 1, :], 1.0), w=r_Vg)

        dma_in("sp", vecA[0:8, :], d_cond, [r_vecA])
        dma_in("sp", vecA[8:16, :], d_fn, [r_vecA])
        dma_in("sp", vecA[16:48, :], d_ng, [r_vecA])
        dma_in("sp", vecB[:], d_bada, [r_vecB])
        dma_in("sp", ropeM[:].rearrange("p t a d -> p t (a d)"), d_ropem.rearrange("(t p) a d -> p t (a d)", p=128), [r_ropeM])
        dma_in("sp", big1[:, 0:32], d_logit.partition_broadcast(128), [r_big1])
        dma_in("pool", NEf[:], d_nef, [r_NE])
        dma_in("pool", NEb[:], d_neb, [r_NE])
        dma_in("sp", NEq[:].rearrange("p a i -> p (a i)"), d_neq.partition_broadcast(128), [r_NE])
        dma_in("sp", NEk[:], d_nek, [r_NE])
        dma_in("sp", mcar[:], d_carry.partition_broadcast(128), [r_NE])
        A("dve", lambda e: e.memset(QTg[:], 0.0), w=r_QTg)
        A("dve", lambda e: e.memset(KTg[:], 0.0), w=r_KTg)
        for s_ in range(2):
            dma_in("pool", QTg[64:69, s_, :], d_qmask, [r_QTg[s_]])
        for g in range(2):
            dma_in("pool", KTg[64:69, g, :], d_kmask, [r_KTg[g]])

        A("act", lambda e: e.activation(big1[:, 32:64], big1[:, 0:32], AF.Exp, scale=-1.0), r=[r_big1], w=[r_big1])
        A("act", lambda e: e.activation(nlg[:], big1[:, 32:64], AF.Ln, bias=cbias[:, 2:3], scale=1.0), r=[r_big1, r_cb], w=[r_nlg])

        bA, rA = bank(6)
        A("pe", lambda e: e.transpose(bA[:, 0:48], vecA[0:48, :], ident_f[0:48, 0:48]), r=[r_vecA, r_idf], w=[rA])
        A("pe", lambda e: e.transpose(bA[:, 48:144], vecB[0:96, :], ident_f[0:96, 0:96]), r=[r_vecB, r_idf], w=[rA])
        A("dve", lambda e: e.tensor_copy(vT[:], bA[:, 0:144]), r=[rA], w=[r_vT])
        A("act", lambda e: e.activation(scond[:], vT[:, 0:8], AF.Silu), r=[r_vT], w=[r_scond])

        def x_load(cb):
            for tt in range(NT):
                s = tt % 2
                if tt >= 2:
                    x_dma(tt)
                for hf in range(2):
                    b, rb = bank(2 * s + hf)
                    for j in range(4):
                        c = hf * 4 + j
                        A("pe", lambda e, b=b, j=j, c=c, s=s: e.transpose(b[:, j * 128:(j + 1) * 128], xst[s][:, c * 128:(c + 1) * 128], ident_f[:]),
                          r=[r_xst[s], r_idf], w=[rb])
                    eng = "dve" if hf == 0 else "act"
                    if eng == "dve":
                        A("dve", lambda e, b=b, hf=hf, tt=tt: e.tensor_copy(xT[:, hf * 4:hf * 4 + 4, tt * 128:(tt + 1) * 128],
                                                                            b.rearrange("p (j t) -> p j t", j=4)),
                          r=[rb], w=r_xT[hf * 4:hf * 4 + 4])
                    else:
                        A("act", lambda e, b=b, hf=hf, tt=tt: e.copy(xT[:, hf * 4:hf * 4 + 4, tt * 128:(tt + 1) * 128],
                                                                      b.rearrange("p (j t) -> p j t", j=4)),
                          r=[rb], w=r_xT[hf * 4:hf * 4 + 4])
                cb()

        pieces = []

        def wview(d, l):
            return d[l].rearrange("(c p) n -> p c n", p=128)

        def add_ada(l):
            for pc in range(6):
                pieces.append([(0, 512, wview(d_wada, l)[:, :, pc * 512:(pc + 1) * 512])])

        def add_in(l):
            v = wview(d_win, l)
            pieces.append([(0, 512, v[:, :, 0:512])])
            pieces.append([(0, 512, v[:, :, 768:1280])])
            pieces.append([(0, 256, v[:, :, 512:768]), (256, 128, v[:, :, 1280:1408]), (384, 128, v[:, :, 1664:1792])])
            pieces.append([(0, 256, v[:, :, 1408:1664]), (256, 32, v[:, :, 1792:1824])])
            pieces.append([(0, 512, v[:, :, 1824:2336])])
            pieces.append([(0, 512, v[:, :, 2336:2848])])

        def add_out(l):
            for pc in range(2):
                pieces.append([(0, 512, wview(d_wout, l)[:, :, pc * 512:(pc + 1) * 512])])

        add_ada(0)
        for l in range(nlayers):
            add_in(l)
            if l + 1 < nlayers:
                add_ada(l + 1)
            add_out(l)
        ws_state = {"loaded": 0, "used": 0}

        def ws_issue():
            i = ws_state["loaded"]
            if i >= len(pieces):
                return
            s = i % NSLOT
            for (off, wd, src) in pieces[i]:
                A("pool", lambda e, s=s, off=off, wd=wd, src=src: e.dma_start(out=ws_t[s][:, :, off:off + wd], in_=src),
                  w=[ws_r[s]], dma="ws%d" % s)
            ws_state["loaded"] += 1

        def ws_get():
            i = ws_state["used"]
            ws_state["used"] += 1
            return ws_t[i % NSLOT], ws_r[i % NSLOT]

        for _ in range(NSLOT):
            ws_issue()

        tmp_i = [0]

        def nxt():
            tmp_i[0] = (tmp_i[0] + 1) % NTMP
            return tmp_i[0]

        bank_rr = [0]

        def rope_ops(src, r_src, cos, sin, r_tab, dst, r_dst, nh, hd, ti, c0=0):
            w = nh * hd
            t1 = T1[ti][:, c0:c0 + w]; t2 = T2[ti][:, c0:c0 + w]
            A("dve", lambda e: e.tensor_tensor(t1.rearrange("p (h d) -> p h d", h=nh), src.rearrange("p (h d) -> p h d", h=nh),
                                               cos.unsqueeze(1).broadcast_to([128, nh, hd]), ALU.mult), r=[r_src, r_tab], w=[r_T1[ti]])
            sv = src.rearrange("p (h a q s) -> p h a q s", h=nh, a=2, q=2)
            tv = t2.rearrange("p (h a q s) -> p h a q s", h=nh, a=2, q=2)
            sn = sin.rearrange("p (a q s) -> p a q s", a=2, q=2)
            for q in range(2):
                snq = sn[:, :, q, :].unsqueeze(1).broadcast_to([128, nh, 2, hd // 4])
                A("dve", lambda e, q=q, snq=snq: e.tensor_tensor(tv[:, :, :, q, :], sv[:, :, :, 1 - q, :], snq, ALU.mult),
                  r=[r_src, r_tab], w=[r_T2[ti]])
            yield
            A("dve", lambda e: e.tensor_tensor(dst, t1, t2, ALU.add), r=[r_T1[ti], r_T2[ti]], w=[r_dst])

        def transposes_gen(items, banks=(4, 5, 6, 7), evac=None):
            k = 0
            for (src_fn, rs_fn, w, dst_fn, r_dst, ntile, t0) in items:
                for half in range((ntile + 3) // 4):
                    bi = banks[bank_rr[0] % len(banks)]
                    bank_rr[0] += 1
                    b, rb = bank(bi)
                    bb = b.bitcast(BF16)
                    n = min(4, ntile - half * 4)
                    for j in range(n):
                        tt = half * 4 + j
                        A("pe", lambda e, bb=bb, j=j, tt=tt, src_fn=src_fn, w=w: e.transpose(bb[0:w, j * 128:(j + 1) * 128], src_fn(tt), ident_b[:]),
                          r=[rs_fn(tt), r_idb], w=[rb])
                    eng = evac if evac is not None else ("act" if (k % 2 == 0) else "dve")
                    k += 1
                    dst = dst_fn(half)
                    if eng == "act":
                        A("act", lambda e, bb=bb, w=w, n=n, dst=dst: e.copy(dst, bb[0:w, 0:n * 128]), r=[rb], w=[r_dst])
                    else:
                        A("dve", lambda e, bb=bb, w=w, n=n, dst=dst: e.tensor_copy(dst, bb[0:w, 0:n * 128]), r=[rb], w=[r_dst])
                    yield

        def transposes(items, banks=(4, 5, 6, 7), evac=None):
            for _ in transposes_gen(items, banks, evac):
                pass

        def tap(name, ap, shape, reads):
            if name not in taps:
                return
            d = dout("tap_" + name, shape)
            tap_out[name] = shape
            A("pool", lambda e: e.dma_start(out=d, in_=ap), r=reads, dma=uniq("tap"))

        def mod_gen(l):
            pm, rpm = bank(7)
            for pc in range(6):
                wt, wr = ws_get()
                for j in range(4):
                    cc = pc * 4 + j
                    for kc in range(8):
                        A("pe", lambda e, wt=wt, j=j, kc=kc, cc=cc: e.matmul(pm[:, cc:cc + 1], lhsT=wt[:, kc, j * 128:(j + 1) * 128],
                                                                              rhs=scond[:, kc:kc + 1], start=(kc == 0), stop=(kc == 7)),
                          r=[wr, r_scond], w=[rpm])
                    yield
                ws_issue()
            m = modl[l]
            A("dve", lambda e: e.tensor_tensor(m[:, 0:24], pm[:, 0:24], vT[:, 48 + 24 * l:72 + 24 * l], ALU.add), r=[rpm, r_vT], w=[r_modl[l]])
            A("dve", lambda e: e.scalar_tensor_tensor(m[:, 24:32], m[:, 8:16], 1.0, vT[:, 16 + 8 * l:24 + 8 * l], ALU.add, ALU.mult),
              r=[r_modl[l], r_vT], w=[r_modl[l]])
            yield

        def phase_mod(l):
            for _ in mod_gen(l):
                pass

        def sumsq_bc(dst_bc, r_dst, src_chunks, r_src, nchunk, ones, r_ones, np_, eps_n, PPi):
            for c in range(nchunk):
                s = c % 2
                A("act", lambda e, c=c, s=s: e.activation(sqb[s][0:np_, :], src_chunks(c), AF.Square), r=[r_src[c]], w=[r_sqb[s]])
                for hf in range(2):
                    A("pe", lambda e, c=c, s=s, hf=hf: e.matmul(PP[PPi][0:np_, hf * 512:(hf + 1) * 512], lhsT=ones[0:np_, 0:np_],
                                                                 rhs=sqb[s][0:np_, hf * 512:(hf + 1) * 512], start=(c == 0), stop=(c == nchunk - 1)),
                      r=[r_sqb[s], r_ones], w=[PR[PPi][hf]])
            A("act", lambda e: e.activation(dst_bc, PP[PPi][0:np_, :], AF.Ln, bias=cbias[0:np_, 0:1], scale=1.0), r=PR[PPi] + [r_cb], w=[r_dst])
            A("act", lambda e: e.activation(dst_bc, dst_bc, AF.Exp, scale=-0.5), r=[r_dst], w=[r_dst])

        def phase_norm(l):
            sumsq_bc(big1[:], r_big1, lambda c: xT[:, c, :], r_xT, 8, ones_mean, r_om, 128, EPS, 0)
            m = modl[l]
            big3, r_big3 = RC.view(8192, 4096, F32)
            for c in range(8):
                bb_, rbb_ = (big2, r_big2) if c % 2 == 0 else (big3, r_big3)
                A("dve", lambda e, c=c, bb_=bb_: e.tensor_tensor(bb_, xT[:, c, :], big1[:], ALU.mult), r=[r_xT[c], r_big1], w=[rbb_])
                A("act", lambda e, c=c, bb_=bb_: e.activation(hyT[:, c, :], bb_, AF.Identity, bias=m[:, c:c + 1], scale=m[:, 24 + c:25 + c]),
                  r=[rbb_, r_modl[l]], w=[r_hy[c]])

        def load_small(l, gkc_only=False, skip_gkc=False):
            if gkc_only:
                A("pool", lambda e: e.dma_start(out=gkc, in_=d_cgk[l].rearrange("(k p) d -> p k d", p=128)), w=[r_gkc], dma="gkc")
                return
            A("sp", lambda e: e.dma_start(out=ropeG.rearrange("p t a d -> p t (a d)"), in_=d_ropehd.rearrange("(t p) a d -> p t (a d)", p=128)),
              w=[r_ropeG], dma="ropeG")
            A("sp", lambda e: e.dma_start(out=Gq8[:], in_=d_gqg[l * 64:(l + 1) * 64].partition_broadcast(128)), w=[r_G], dma="g1")
            A("sp", lambda e: e.dma_start(out=Gk8[:], in_=d_gkg[l * 64:(l + 1) * 64].partition_broadcast(128)), w=[r_G], dma="g2")
            A("sp", lambda e: e.dma_start(out=Gmq[:], in_=d_mqg[l * 256:(l + 1) * 256].partition_broadcast(128)), w=[r_G], dma="g3")
            A("sp", lambda e: e.dma_start(out=Gmkv[:], in_=d_mkvg[l * 128:(l + 1) * 128].partition_broadcast(128)), w=[r_G], dma="g4")
            A("dve", lambda e: e.tensor_scalar(Gq8[:], Gq8[:], 8.0, None, ALU.mult), r=[r_G], w=[r_G])
            A("dve", lambda e: e.tensor_scalar(Gk8[:], Gk8[:], 8.0, None, ALU.mult), r=[r_G], w=[r_G])
            A("dve", lambda e: e.tensor_scalar(Gmq[:], Gmq[:], 16.0, None, ALU.mult), r=[r_G], w=[r_G])
            A("dve", lambda e: e.tensor_scalar(Gmkv[:], Gmkv[:], float(np.sqrt(128.0)), None, ALU.mult), r=[r_G], w=[r_G])
            A("pool", lambda e: e.dma_start(out=wuq[:], in_=d_wuq[l].rearrange("(c p) n -> p c n", p=128)), w=[r_wuq], dma="wuq")
            A("pool", lambda e: e.dma_start(out=wukv[:].rearrange("p (h d) -> p h d", h=6), in_=d_wukv[l].rearrange("p (h x) -> p h x", h=6)[:, :, 64:128]),
              w=[r_wukv], dma="wukv")
            A("pool", lambda e: e.dma_start(out=wukp[:, :, 0:64], in_=d_wukv[l].rearrange("p (h x) -> p h x", h=6)[:, :, 0:64]), w=[r_wukp], dma="wukp")
            if not skip_gkc:
                A("pool", lambda e: e.dma_start(out=gkc, in_=d_cgk[l].rearrange("(k p) d -> p k d", p=128)), w=[r_gkc], dma="gkc")
            for g in range(2):
                A("pool", lambda e, g=g: e.dma_start(out=Vg[:, 0:4, 2 * g, :], in_=d_cgv[l].rearrange("(k p) (g d) -> p k g d", p=128, g=2)[:, :, g, :]),
                  w=r_Vg[0:4], dma="vgc")
            A("pool", lambda e: e.dma_start(out=ckvn[:, 0:4, :], in_=d_cckv[l].rearrange("(k p) d -> p k d", p=128)), w=r_ckvn[0:4], dma="ckvc")
            A("pool", lambda e: e.dma_start(out=kpeb[:, 0:4, :], in_=d_ckpe[l].rearrange("(k p) d -> p k d", p=128)), w=r_kpeb[0:4], dma="kpec")
            A("sp", lambda e: e.dma_start(out=S_aft.rearrange("d (a e) -> d a e", a=8), in_=d_st0[l].rearrange("a h d e -> d (a h) e")),
              w=[r_Saft], dma="st0")

        def phase_consts(l):
            LN8 = float(np.log(0.125))
            for a in range(2):
                NE = NEf if a == 0 else NEb
                for h in range(4):
                    col = l * 8 + a * 4 + h
                    p0 = 64 * (h % 2)
                    A("act", lambda e, a=a, h=h, col=col, p0=p0: e.activation(qdec[p0:p0 + 64, a, h, :], NEq[p0:p0 + 64, a, :], AF.Exp,
                                                                              scale=nlg[p0:p0 + 64, col:col + 1]),
                      r=[r_NE, r_nlg], w=[r_qdec])
            sm = small[0]
            A("dve", lambda e: e.tensor_tensor(sm[:, 0:8].rearrange("p (a h) -> p a h", a=2), nlg[:, l * 8:l * 8 + 8].rearrange("p (a h) -> p a h", a=2),
                                               NEk[:].unsqueeze(2).broadcast_to([128, 2, 4]), ALU.mult), r=[r_nlg, r_NE], w=[r_small[0]])
            A("act", lambda e: e.activation(kdec[:], sm[:, 0:8], AF.Exp, bias=cbias[:, 1:2]), r=[r_small[0], r_cb], w=[r_kdec])
            A("act", lambda e: e.activation(sm[0:64, 8:16], nlg[0:64, l * 8:l * 8 + 8], AF.Exp, scale=-128.0), r=[r_nlg, r_small[0]], w=[r_small[0]])
            A("dve", lambda e: e.tensor_copy(cdec, sm[0:64, 8:16].unsqueeze(2).broadcast_to([64, 8, 64])), r=[r_small[0]], w=[r_cdec])

        def phase_consts_b(l):
            LN8 = float(np.log(0.125))
            for a in range(2):
                NE = NEf if a == 0 else NEb
                for h in range(4):
                    col = l * 8 + a * 4 + h
                    A("act", lambda e, a=a, h=h, col=col, NE=NE: e.activation(intra[:, a, h, :], NE[:], AF.Exp, bias=cbias[:, 1:2], scale=nlg[:, col:col + 1]),
                      r=[r_NE, r_nlg, r_cb], w=[r_intra])

        def phase_inproj(l):
            m = modl[l]
            transposes([
                (lambda tt: gkc[:, tt, 0:64], lambda tt: r_gkc, 64, lambda half: KTg[0:64, 0, 0:512], r_KTg[0], 4, 0),
                (lambda tt: gkc[:, tt, 64:128], lambda tt: r_gkc, 64, lambda half: KTg[0:64, 1, 0:512], r_KTg[1], 4, 0),
                (lambda tt: ckvn[:, tt, :], lambda tt: r_ckvn[tt], 128, lambda half: ckvT[:, 0:512], r_ckvT, 4, 0),
                (lambda tt: kpeb[:, tt, :], lambda tt: r_kpeb[tt], 32, lambda half: kpeT[:, 0:512], r_kpeT, 4, 0),
            ])
            widths = [512, 512, 512, 288]
            def mm_group(g):
                wt, wr = ws_get()
                wd = widths[g]
                def prep_gen(tt, b, rb, ti):
                    X = Xs[ti]
                    A("act", lambda e, X=X, b=b, wd=wd: e.copy(X[:, 0:wd], b[:, 0:wd]), r=[rb], w=[r_Xs[ti]])
                    yield
                    sg = stage0
                    if g == 0:
                        yield from rope_ops(X[:, 0:512], r_Xs[ti], ropeG[:, tt, 0, :], ropeG[:, tt, 1, :], r_ropeG,
                                 qk_tm[:, tt, :], r_qk[tt], 8, 64, ti, 0)
                        A("pool", lambda e, tt=tt: e.tensor_tensor(kd_tm[:, tt, :, :, :],
                                                                   qk_tm[:, tt, 256:512].rearrange("p (h d) -> p h d", h=4).unsqueeze(1).broadcast_to([128, 2, 4, 64]),
                                                                   kdec[:].rearrange("p (a h) -> p a h", a=2).unsqueeze(3).broadcast_to([128, 2, 4, 64]), ALU.mult),
                          r=[r_qk[tt], r_kdec], w=[r_kd[tt]])
                    elif g == 1:
                        sq = T1[ti]; sm = small[ti]
                        A("act", lambda e, X=X, sq=sq: e.activation(sq[:], X[:], AF.Square), r=[r_Xs[ti]], w=[r_T1[ti]])
                        yield
                        A("dve", lambda e, sq=sq, sm=sm: e.tensor_reduce(sm[:, 0:8], sq[:].rearrange("p (h d) -> p h d", h=8), AX.X, ALU.add),
                          r=[r_T1[ti]], w=[r_small[ti]])
                        yield
                        A("act", lambda e, sm=sm: e.activation(sm[:, 8:16], sm[:, 0:8], AF.Ln, bias=cbias[:, 3:4], scale=1.0), r=[r_small[ti], r_cb], w=[r_small[ti]])
                        yield
                        A("act", lambda e, sm=sm: e.activation(sm[:, 8:16], sm[:, 8:16], AF.Exp, scale=-0.5), r=[r_small[ti]], w=[r_small[ti]])
                        yield
                        Tn = T2[ti]
                        A("dve", lambda e, X=X, sm=sm, Tn=Tn: e.tensor_tensor(Tn[:].rearrange("p (h d) -> p h d", h=8), X[:].rearrange("p (h d) -> p h d", h=8),
                                                                              sm[:, 8:16].unsqueeze(2).broadcast_to([128, 8, 64]), ALU.mult),
                          r=[r_Xs[ti], r_small[ti]], w=[r_T2[ti]])
                        yield
                        A("dve", lambda e, X=X, Tn=Tn: e.tensor_tensor(X[:, 0:384].rearrange("p (h d) -> p h d", h=6), Tn[:, 0:384].rearrange("p (h d) -> p h d", h=6),
                                                                        Gq8[:].unsqueeze(1).broadcast_to([128, 6, 64]), ALU.mult),
                          r=[r_T2[ti], r_G], w=[r_Xs[ti]])
                        A("dve", lambda e, X=X, Tn=Tn: e.tensor_tensor(X[:, 384:512].rearrange("p (h d) -> p h d", h=2), Tn[:, 384:512].rearrange("p (h d) -> p h d", h=2),
                                                                        Gk8[:].unsqueeze(1).broadcast_to([128, 2, 64]), ALU.mult),
                          r=[r_T2[ti], r_G], w=[r_Xs[ti]])
                        yield
                        A("act", lambda e, X=X, sg=sg: e.copy(sg[:, 0:128], X[:, 384:512]), r=[r_Xs[ti]], w=[r_stg[0]])
                        A("sp", lambda e, sg=sg, tt=tt: e.dma_start(out=o_gk[l, tt * 128:(tt + 1) * 128, :], in_=sg[:, 0:128]), r=[r_stg[0]], dma="stg0")
                        yield from rope_ops(X[:, 0:512], r_Xs[ti], ropeG[:, tt, 0, :], ropeG[:, tt, 1, :], r_ropeG, g_tm[:, tt, :], r_g[tt], 8, 64, ti)
                    elif g == 2:
                        A("dve", lambda e, X=X, tt=tt: e.tensor_copy(V_ret[:, tt, :], X[:, 0:256]), r=[r_Xs[ti]], w=[r_vret[tt]])
                        A("dve", lambda e, X=X, tt=tt: e.tensor_copy(Vg[:, 4 + tt, 0:3:2, :], X[:, 256:384].rearrange("p (g d) -> p g d", g=2)),
                          r=[r_Xs[ti]], w=[r_Vg[4 + tt]])
                        A("act", lambda e, X=X, sg=sg: e.copy(sg[:, 128:256], X[:, 256:384]), r=[r_Xs[ti]], w=[r_stg[1]])
                        A("sp", lambda e, sg=sg, tt=tt: e.dma_start(out=o_gv[l, tt * 128:(tt + 1) * 128, :], in_=sg[:, 128:256]), r=[r_stg[1]], dma="stg1")
                        sm = small[ti]; junk = T1[ti]
                        A("dve", lambda e, X=X, sm=sm, junk=junk: e.scalar_tensor_tensor(junk[:, 0:128], X[:, 384:512], 1.0, X[:, 384:512], ALU.mult, ALU.mult,
                                                                                        accum_out=sm[:, 0:1]), r=[r_Xs[ti]], w=[r_T1[ti], r_small[ti]])
                        yield
                        A("act", lambda e, sm=sm: e.activation(sm[:, 1:2], sm[:, 0:1], AF.Ln, bias=cbias[:, 4:5], scale=1.0), r=[r_small[ti], r_cb], w=[r_small[ti]])
                        yield
                        A("act", lambda e, sm=sm: e.activation(sm[:, 1:2], sm[:, 1:2], AF.Exp, scale=-0.5), r=[r_small[ti]], w=[r_small[ti]])
                        yield
                        A("dve", lambda e, X=X, sm=sm, sg=sg: e.scalar_tensor_tensor(sg[:, 256:384], X[:, 384:512], sm[:, 1:2], Gmkv[:], ALU.mult, ALU.mult),
                          r=[r_Xs[ti], r_small[ti], r_G], w=[r_stg[2]])
                        A("dve", lambda e, sg=sg, tt=tt: e.tensor_copy(ckvn[:, 4 + tt, :], sg[:, 256:384]), r=[r_stg[2]], w=[r_ckvn[4 + tt]])
                        A("sp", lambda e, sg=sg, tt=tt: e.dma_start(out=o_ckv[l, tt * 128:(tt + 1) * 128, :], in_=sg[:, 256:384]), r=[r_stg[2]], dma="stg2")
                    else:
                        sm = small[ti]; junk = T1[ti]
                        A("dve", lambda e, X=X, sm=sm, junk=junk: e.scalar_tensor_tensor(junk[:, 0:256], X[:, 0:256], 1.0, X[:, 0:256], ALU.mult, ALU.mult,
                                                                                        accum_out=sm[:, 0:1]), r=[r_Xs[ti]], w=[r_T1[ti], r_small[ti]])
                        yield
                        A("act", lambda e, sm=sm: e.activation(sm[:, 1:2], sm[:, 0:1], AF.Ln, bias=cbias[:, 5:6], scale=1.0), r=[r_small[ti], r_cb], w=[r_small[ti]])
                        yield
                        A("act", lambda e, sm=sm: e.activation(sm[:, 1:2], sm[:, 1:2], AF.Exp, scale=-0.5), r=[r_small[ti]], w=[r_small[ti]])
                        yield
                        A("dve", lambda e, X=X, sm=sm, tt=tt: e.scalar_tensor_tensor(cqn[:, tt, :], X[:, 0:256], sm[:, 1:2], Gmq[:], ALU.mult, ALU.mult),
                          r=[r_Xs[ti], r_small[ti], r_G], w=[r_cqn[tt]])
                        yield from rope_ops(X[:, 256:288], r_Xs[ti], ropeM[:, tt, 0, :], ropeM[:, tt, 1, :], r_ropeM, sg[:, 384:416], r_stg[3], 1, 32, ti, 256)
                        A("dve", lambda e, sg=sg, tt=tt: e.tensor_copy(kpeb[:, 4 + tt, :], sg[:, 384:416]), r=[r_stg[3]], w=[r_kpeb[4 + tt]])
                        A("sp", lambda e, sg=sg, tt=tt: e.dma_start(out=o_kpe[l, tt * 128:(tt + 1) * 128, :], in_=sg[:, 384:416]), r=[r_stg[3]], dma="stg3")
                    yield
                for t0 in (0, 4):
                    gens = []
                    for tt in range(t0, t0 + 4):
                        bi = (g * NT + tt) % 4
                        b, rb = bank(bi)
                        for kc in range(8):
                            A("pe", lambda e, b=b, wd=wd, kc=kc, tt=tt, wt=wt: e.matmul(b[:, 0:wd], lhsT=hyT[:, kc, tt * 128:(tt + 1) * 128], rhs=wt[:, kc, 0:wd],
                                                                                         start=(kc == 0), stop=(kc == 7)),
                              r=[r_hy[kc], wr], w=[rb])
                        gens.append(prep_gen(tt, b, rb, nxt()))
                    while gens:
                        alive = []
                        for gn in gens:
                            try:
                                next(gn)
                                alive.append(gn)
                            except StopIteration:
                                pass
                        gens = alive
                ws_issue()
            def tr_group(g):
                if g == 0:
                    items = []
                    for h in range(4):
                        p0 = 64 * (h % 2); pr = h // 2
                        items.append((lambda tt, h=h: qk_tm[:, tt, h * 64:(h + 1) * 64], lambda tt: r_qk[tt], 64,
                                      lambda half, p0=p0, pr=pr: qkT[p0:p0 + 64, pr, 0, half * 512:(half + 1) * 512], r_qkT[h], 8, 0))
                        items.append((lambda tt, h=h: qk_tm[:, tt, 256 + h * 64:256 + (h + 1) * 64], lambda tt: r_qk[tt], 64,
                                      lambda half, p0=p0, pr=pr: qkT[p0:p0 + 64, pr, 1, half * 512:(half + 1) * 512], r_qkT[h], 8, 0))
                    transposes(items, evac="act")
                    tap("qk%d" % l, qk_tm, [128, 8, 512], r_qk)
                elif g == 1:
                    items = []
                    for gg in range(2):
                        items.append((lambda tt, gg=gg: g_tm[:, tt, 384 + gg * 64:384 + (gg + 1) * 64], lambda tt: r_g[tt], 64,
                                      lambda half, gg=gg: KTg[0:64, gg, 512 + half * 512:512 + (half + 1) * 512], r_KTg[gg], 8, 0))
                    transposes(items, evac="act")
                elif g == 2:
                    transposes([(lambda tt: ckvn[:, 4 + tt, :], lambda tt: r_ckvn[4 + tt], 128,
                                 lambda half: ckvT[:, 512 + half * 512:512 + (half + 1) * 512], r_ckvT, 8, 0)])
                else:
                    transposes([
                        (lambda tt: cqn[:, tt, 0:128], lambda tt: r_cqn[tt], 128, lambda half: cqnT[:, 0, half * 512:(half + 1) * 512], r_cqnT, 8, 0),
                        (lambda tt: cqn[:, tt, 128:256], lambda tt: r_cqn[tt], 128, lambda half: cqnT[:, 1, half * 512:(half + 1) * 512], r_cqnT, 8, 0),
                        (lambda tt: kpeb[:, 4 + tt, :], lambda tt: r_kpeb[4 + tt], 32, lambda half: kpeT[:, 512 + half * 512:512 + (half + 1) * 512], r_kpeT, 8, 0),
                    ])
            def z_group(zp):
                wt, wr = ws_get()
                for j in range(4):
                    cc = zp * 4 + j
                    for hf in range(2):
                        b, rb = bank((j * 2 + hf) % 4)
                        for kc in range(8):
                            A("pe", lambda e, b=b, wt=wt, j=j, kc=kc, hf=hf: e.matmul(b[:], lhsT=wt[:, kc, j * 128:(j + 1) * 128],
                                                                                        rhs=hyT[:, kc, hf * 512:(hf + 1) * 512], start=(kc == 0), stop=(kc == 7)),
                              r=[wr, r_hy[kc]], w=[rb])
                        A("act", lambda e, b=b, cc=cc, hf=hf: e.activation(szT[:, cc, hf * 512:(hf + 1) * 512], b[:], AF.Silu), r=[rb], w=[r_sz[cc]])
                ws_issue()
            mm_group(0)
            mm_group(1)
            tr_group(0)
            mm_group(2)
            tr_group(1)
            mm_group(3)
            tr_group(2)
            z_group(0)
            tr_group(3)
            z_group(1)

        pend = [None]
        bg = [None]

        def bg_step(n=1):
            for _ in range(n):
                if bg[0] is None:
                    return
                try:
                    next(bg[0])
                except StopIteration:
                    bg[0] = None

        def bg_flush():
            while bg[0] is not None:
                bg_step()

        prep = [None]

        def prep_step():
            if prep[0] is None:
                return
            try:
                next(prep[0])
            except StopIteration:
                prep[0] = None

        def prep_flush():
            while prep[0] is not None:
                prep_step()

        pass_ctr = [0]

        def attend(QT_fn, r_Q, KT_fn, r_K, V_fn, r_V_fn, krows, scale, ymix_row0, par):
            po = 64 * par
            pd = 64 - po
            c = ymix_row0 // 128
            p0 = ymix_row0 % 128
            units = [(hf, kc) for hf in range(2) for kc in range(NK)]
            NU = len(units)
            obank = {}
            for hf in range(2):
                obank[hf] = bank(4 + (pass_ctr[0] % 2))
                pass_ctr[0] += 1

            def QK(u):
                hf, kc = units[u]
                pS, rS = bank(u % 4)
                A("pe", lambda e, pS=pS, kc=kc, hf=hf: e.matmul(pS, lhsT=KT_fn(kc), rhs=QT_fn(hf), start=True, stop=True),
                  r=[r_K, r_Q], w=[rS])
                pi = u % 8
                ptv = PT[pi // 2][:, (pi % 2) * 512:(pi % 2) * 512 + 512]
                A("act", lambda e, pS=pS, ptv=ptv: e.activation(ptv, pS, AF.Exp, scale=scale), r=[rS], w=[RA.cells[pi]])

            def PV(u):
                hf, kc = units[u]
                pi = u % 8
                ptv = PT[pi // 2][:, (pi % 2) * 512:(pi % 2) * 512 + 512]
                pOb, rOb = obank[hf]
                A("pe", lambda e, kc=kc, ptv=ptv, pOb=pOb: e.matmul(pOb, lhsT=V_fn(kc), rhs=ptv, start=(kc == 0), stop=(kc == NK - 1)),
                  r=[r_V_fn(kc), RA.cells[pi]], w=[rOb])

            def finish(hf):
                pOb, rOb = obank[hf]
                cs = slice(hf * 512, (hf + 1) * 512)
                A("act", lambda e: e.activation(big1[po:po + 64, cs], pOb[pd:pd + 64, :], AF.Ln), r=[rOb], w=[r_big1[hf]])
                A("dve", lambda e: e.tensor_copy(big2[po:po + 64, cs], pOb[po:po + 64, :]), r=[rOb], w=[r_big2[hf]])
                A("act", lambda e: e.activation(big1[po:po + 64, cs], big1[po:po + 64, cs], AF.Exp, scale=-1.0), r=[r_big1[hf]], w=[r_big1[hf]])

                def norm():
                    A("dve", lambda e: e.tensor_tensor(big2[p0:p0 + 64, cs], big2[po:po + 64, cs], big1[po:po + 64, cs], ALU.mult),
                      r=[r_big2[hf], r_big1[hf]], w=[r_big2[hf]])
                    A("pool", lambda e: e.tensor_tensor(hyT[p0:p0 + 64, c, cs], big2[p0:p0 + 64, cs], szT[p0:p0 + 64, c, cs], ALU.mult),
                      r=[r_big2[hf], r_sz[c]], w=[r_hy[c]])
                pend.append(norm)

            for u in range(min(3, NU)):
                QK(u)
            for u in range(NU):
                if u + 3 < NU:
                    QK(u + 3)
                PV(u)
                hf, kc = units[u]
                if kc == 2 and len(pend) > 1:
                    pend.pop(1)()
                if u % 4 == 1:
                    prep_step()
                if u % 2 == 1:
                    bg_step()
                if kc == NK - 1:
                    if hf == 1:
                        prep_flush()
                    finish(hf)

        def attend_flush():
            while len(pend) > 1:
                pend.pop(1)()

        def gqa_head_prep(h):
            s_ = h % 2
            yield from transposes_gen([(lambda tt, h=h: g_tm[:, tt, h * 64:(h + 1) * 64], lambda tt: r_g[tt], 64,
                                        lambda half, s_=s_: QTg[0:64, s_, half * 512:(half + 1) * 512], r_QTg[s_], 8, 0)], banks=(6,), evac="dve")

        def phase_gqa(l):
            bg[0] = ret_gen(l)
            for _ in gqa_head_prep(0):
                pass
            for h in range(6):
                g = h // 3
                s_ = h % 2
                if h + 1 < 6:
                    prep[0] = gqa_head_prep(h + 1)
                attend(lambda hf, s_=s_: QTg[:, s_, hf * 512:(hf + 1) * 512], r_QTg[s_],
                       lambda kc, g=g: KTg[:, g, kc * 128:(kc + 1) * 128], r_KTg[g],
                       lambda kc, g=g: Vg[:, kc, g:g + 2, :].rearrange("p x d -> p (x d)"), lambda kc: r_Vg[kc], 69, 0.125, 256 + 64 * h, g)
            attend_flush()
            bg_flush()

        def phase_mla_proj(l):
            def tile_gen(tt):
                b0, rb0 = bank(0 + 2 * (tt % 2)); b1, rb1 = bank(1 + 2 * (tt % 2))
                for hh, (b, rb) in enumerate(((b0, rb0), (b1, rb1))):
                    for kc in range(2):
                        A("pe", lambda e, b=b, kc=kc, tt=tt, hh=hh: e.matmul(b[:, 0:288], lhsT=cqnT[:, kc, tt * 128:(tt + 1) * 128],
                                                                              rhs=wuq[:, kc, hh * 288:(hh + 1) * 288], start=(kc == 0), stop=(kc == 1)),
                          r=[r_cqnT, r_wuq], w=[rb])
                yield
                ti = nxt()
                Xa = Xs[ti]; Xb = T1[ti]
                A("act", lambda e, Xa=Xa, b0=b0: e.copy(Xa[:, 0:288], b0[:, 0:288]), r=[rb0], w=[r_Xs[ti]])
                A("act", lambda e, Xb=Xb, b1=b1: e.copy(Xb[:, 0:288], b1[:, 0:288]), r=[rb1], w=[r_T1[ti]])
                yield
                fin = []
                for hh, (Xh, rX) in enumerate(((Xa, r_Xs[ti]), (Xb, r_T1[ti]))):
                    xv = Xh[:, 0:288].rearrange("p (h x) -> p h x", h=3)
                    dv = qm_tm[:, tt, hh * 3:(hh + 1) * 3, :]
                    A("dve", lambda e, xv=xv, dv=dv: e.tensor_copy(dv[:, :, 0:64], xv[:, :, 0:64]), r=[rX], w=[r_qm[tt]])
                    cosb = ropeM[:, tt, 0, :].unsqueeze(1).broadcast_to([128, 3, 32])
                    sinb = ropeM[:, tt, 1, :].unsqueeze(1).broadcast_to([128, 3, 32])
                    t1 = T2[ti][:, hh * 96:(hh + 1) * 96].rearrange("p (h x) -> p h x", h=3)
                    t2 = T2[ti][:, 192 + hh * 96:192 + (hh + 1) * 96].rearrange("p (h x) -> p h x", h=3)
                    A("dve", lambda e, t1=t1, xv=xv, cosb=cosb: e.tensor_tensor(t1, xv[:, :, 64:96], cosb, ALU.mult), r=[rX, r_ropeM], w=[r_T2[ti]])
                    x5 = xv[:, :, 64:96].rearrange("p h (a q s) -> p h a q s", a=2, q=2)
                    t5 = t2.rearrange("p h (a q s) -> p h a q s", a=2, q=2)
                    s5 = sinb.rearrange("p h (a q s) -> p h a q s", a=2, q=2)
                    A("dve", lambda e, t5=t5, x5=x5, s5=s5: e.tensor_tensor(t5[:, :, :, 0, :], x5[:, :, :, 1, :], s5[:, :, :, 0, :], ALU.mult),
                      r=[rX, r_ropeM], w=[r_T2[ti]])
                    A("dve", lambda e, t5=t5, x5=x5, s5=s5: e.tensor_tensor(t5[:, :, :, 1, :], x5[:, :, :, 0, :], s5[:, :, :, 1, :], ALU.mult),
                      r=[rX, r_ropeM], w=[r_T2[ti]])
                    fin.append((dv, t1, t2))
                yield
                for (dv, t1, t2) in fin:
                    A("dve", lambda e, dv=dv, t1=t1, t2=t2: e.tensor_tensor(dv[:, :, 64:96], t1, t2, ALU.add), r=[r_T2[ti]], w=[r_qm[tt]])
                yield

            def v_gen():
                A("dve", lambda e: e.memset(Vm[:, :, :, 1, :], 1.0), w=[r_Vm_all])
                for kc in range(NK):
                    b, rb = bank(4 + kc % 4)
                    A("pe", lambda e, b=b, kc=kc: e.matmul(b[:, 0:384], lhsT=ckvT[:, kc * 128:(kc + 1) * 128], rhs=wukv[:], start=True, stop=True),
                      r=[r_ckvT, r_wukv], w=[rb])
                    yield
                    if kc % 2 == 0:
                        A("act", lambda e, b=b, kc=kc: e.copy(Vm[:, kc, :, 0:3:2, :], b[:, 0:384].rearrange("p (r x d) -> p r x d", r=3, x=2)), r=[rb], w=[r_Vm[kc]])
                    else:
                        A("dve", lambda e, b=b, kc=kc: e.tensor_copy(Vm[:, kc, :, 0:3:2, :], b[:, 0:384].rearrange("p (r x d) -> p r x d", r=3, x=2)), r=[rb], w=[r_Vm[kc]])
                    yield

            vg = [v_gen()]

            def step(gn):
                try:
                    next(gn)
                    return True
                except StopIteration:
                    return False

            for t0 in range(0, NT, 2):
                gens = [tile_gen(t0), tile_gen(t0 + 1)]
                while gens:
                    gens = [gn for gn in gens if step(gn)]
                    if vg[0] is not None and not step(vg[0]):
                        vg[0] = None
            while vg[0] is not None:
                if not step(vg[0]):
                    vg[0] = None

        def mla_head_prep(h):
            s = h % 2
            yield from transposes_gen([(lambda tt, h=h: qm_tm[:, tt, h, :], lambda tt: r_qm[tt], 96,
                                        lambda half, s=s: QTm[s][0:96, half * 512:(half + 1) * 512], r_QTm[s], 8, 0)], banks=(6,), evac="dve")
            for kb in range(3):
                b, rb = bank(6)
                A("pe", lambda e, b=b, kb=kb, h=h: e.matmul(b[0:96, :], lhsT=wukp[:, h, :], rhs=ckvT[:, kb * 512:(kb + 1) * 512], start=True, stop=False),
                  r=[r_wukp, r_ckvT], w=[rb])
                A("pe", lambda e, b=b, kb=kb: e.matmul(b[0:96, :], lhsT=isel, rhs=kpeT[:, kb * 512:(kb + 1) * 512], start=False, stop=True),
                  r=[r_isel, r_kpeT], w=[rb])
                A("dve", lambda e, b=b, kb=kb, s=s: e.tensor_copy(KTm[s][0:96, kb * 512:(kb + 1) * 512], b[0:96, :]), r=[rb], w=[r_KTm[s]])
                yield

        def phase_mla(l):
            for s_ in range(2):
                A("pool", lambda e, s_=s_: e.dma_start(out=QTm[s_][96:101, :], in_=d_qmask), w=[r_QTm[s_]], dma="mq%d" % s_)
                A("pool", lambda e, s_=s_: e.dma_start(out=KTm[s_][96:101, :], in_=d_kmask), w=[r_KTm[s_]], dma="mk%d" % s_)
            if l + 1 < nlayers:
                bg[0] = mod_gen(l + 1)
            for _ in mla_head_prep(0):
                pass
            for h in range(6):
                s = h % 2
                if h + 1 < 6:
                    prep[0] = mla_head_prep(h + 1)
                attend(lambda hf, s=s: QTm[s][:, hf * 512:(hf + 1) * 512], r_QTm[s],
                       lambda kc, s=s: KTm[s][:, kc * 128:(kc + 1) * 128], r_KTm[s],
                       lambda kc, h=h: Vm[:, kc, h // 2, (h % 2):(h % 2) + 2, :].rearrange("p x d -> p (x d)"), lambda kc: r_Vm[kc], 101, float(96.0 ** -0.5), 640 + 64 * h, h % 2)
            attend_flush()
            bg_flush()

        def ret_gen(l):
            for a in range(2):
                NE = NEf if a == 0 else NEb
                for h in range(4):
                    col = l * 8 + a * 4 + h
                    A("act", lambda e, a=a, h=h, col=col, NE=NE: e.activation(intra[:, a, h, :], NE[:], AF.Exp, bias=cbias[:, 1:2], scale=nlg[:, col:col + 1]),
                      r=[r_NE, r_nlg, r_cb], w=[r_intra])
                yield
            A("dve", lambda e: e.tensor_tensor(intra[:, 0, :, :], intra[:, 0, :, :], intra[:, 1, :, :], ALU.add), r=[r_intra], w=[r_intra])
            yield
            cdf = cdec.rearrange("d a e -> d (a e)")
            for t in range(8):
                pk, rk = bank(7)
                for a in range(2):
                    ch = t if a == 0 else 7 - t
                    for h in range(4):
                        A("pe", lambda e, pk=pk, a=a, h=h, ch=ch: e.matmul(pk[0:64, (a * 4 + h) * 64:(a * 4 + h + 1) * 64], lhsT=kd_tm[:, ch, a, h, :],
                                                                           rhs=V_ret[:, ch, h * 64:(h + 1) * 64], start=True, stop=True),
                          r=[r_kd[ch], r_vret[ch]], w=[rk])
                if t == 0:
                    A("dve", lambda e: e.tensor_copy(S_in[0:64, 0, :], S_aft), r=[r_Saft], w=[r_Sin[0]])
                    A("dve", lambda e: e.tensor_copy(S_in[64:128, 0, :], S_aft), r=[r_Saft], w=[r_Sin[0]])
                    A("dve", lambda e: e.tensor_tensor(S_tmp, S_aft, cdf, ALU.mult), r=[r_Saft, r_cdec], w=[r_Stmp])
                else:
                    A("dve", lambda e, t=t: e.tensor_scalar(S_in[0:64, t, :], S_aft, mcar[0:64, t:t + 1], None, ALU.mult), r=[r_Saft, r_NE], w=[r_Sin[t]])
                    A("dve", lambda e, t=t: e.tensor_scalar(S_in[64:128, t, :], S_aft, mcar[0:64, t:t + 1], None, ALU.mult), r=[r_Saft, r_NE], w=[r_Sin[t]])
                    A("dve", lambda e, t=t: e.scalar_tensor_tensor(S_tmp, S_aft, mcar[0:64, t:t + 1], cdf, ALU.mult, ALU.mult),
                      r=[r_Saft, r_cdec, r_NE], w=[r_Stmp])
                yield
                A("dve", lambda e, pk=pk: e.tensor_tensor(S_aft, S_tmp, pk[0:64, :], ALU.add), r=[r_Stmp, rk], w=[r_Saft])
                if t % 2 == 1:
                    A("dve", lambda e: e.tensor_copy(S_out, S_aft), r=[r_Saft], w=[r_Sout])
                    A("sp", lambda e, t=t: e.dma_start(out=o_st[l, t // 2], in_=S_out), r=[r_Sout], dma="sto")
                yield
            for h in range(4):
                p0 = 64 * (h % 2); pr = h // 2
                qTh = qkT[p0:p0 + 64, pr, 0, :]; kTh = qkT[p0:p0 + 64, pr, 1, :]
                for a in range(2):
                    A("pool", lambda e, a=a, h=h, p0=p0, qTh=qTh: e.tensor_tensor(qd[p0:p0 + 64, a, :].rearrange("d (c i) -> d c i", c=8), qTh.rearrange("d (c i) -> d c i", c=8),
                                                                                 qdec[p0:p0 + 64, a, h, :].unsqueeze(1).broadcast_to([64, 8, 128]), ALU.mult),
                      r=[r_qkT[h], r_qdec], w=[r_qd])
                yield
                c = (64 * h) // 128
                Oh = T2[0][p0:p0 + 64, :]; rOh = r_T2[0]
                W2 = T2[1][p0:p0 + 64, :]; rW2 = r_T2[1]
                W2b = sqb[1][p0:p0 + 64, 0:512]
                for half in range(2):
                    pq, rq = bank(7)
                    for j in range(4):
                        ch = half * 4 + j
                        A("pe", lambda e, pq=pq, j=j, ch=ch, kTh=kTh, qTh=qTh: e.matmul(pq[:, j * 128:(j + 1) * 128], lhsT=kTh[:, ch * 128:(ch + 1) * 128],
                                                                                         rhs=qTh[:, ch * 128:(ch + 1) * 128], start=True, stop=True),
                          r=[r_qkT[h]], w=[rq])
                    at = AT[half]; rat = r_AT[half]
                    A("dve", lambda e, at=at, pq=pq, h=h: e.tensor_tensor(at[:, 0, :].rearrange("j (c i) -> j c i", c=4), pq.rearrange("j (c i) -> j c i", c=4),
                                                                          intra[:, 0, h, :].unsqueeze(1).broadcast_to([128, 4, 128]), ALU.mult),
                      r=[rq, r_intra], w=[rat])
                    yield
                    pO, rO = bank(7)
                    for j in range(4):
                        ch = half * 4 + j
                        cols = slice(j * 128, (j + 1) * 128)
                        A("pe", lambda e, cols=cols, ch=ch, h=h, at=at, j=j, pO=pO: e.matmul(pO[0:64, cols], lhsT=V_ret[:, ch, h * 64:(h + 1) * 64],
                                                                                              rhs=at[:, 0, j * 128:(j + 1) * 128], start=True, stop=False),
                          r=[r_vret[ch], rat], w=[rO])
                        A("pe", lambda e, cols=cols, ch=ch, h=h, p0=p0, pO=pO: e.matmul(pO[0:64, cols], lhsT=S_in[p0:p0 + 64, ch, h * 64:(h + 1) * 64],
                                                                                         rhs=qd[p0:p0 + 64, 0, ch * 128:(ch + 1) * 128], start=False, stop=False),
                          r=[r_Sin[ch], r_qd], w=[rO])
                        A("pe", lambda e, cols=cols, ch=ch, h=h, p0=p0, pO=pO: e.matmul(pO[0:64, cols], lhsT=S_in[p0:p0 + 64, 7 - ch, (4 + h) * 64:(5 + h) * 64],
                                                                                         rhs=qd[p0:p0 + 64, 1, ch * 128:(ch + 1) * 128], start=False, stop=True),
                          r=[r_Sin[7 - ch], r_qd], w=[rO])
                    yield
                    A("dve", lambda e, Oh=Oh, pO=pO: e.tensor_copy(Oh, pO[0:64, :]), r=[rO], w=[rOh])
                    A("pool", lambda e, Oh=Oh, W2b=W2b: e.tensor_tensor(W2b, Oh, Oh, ALU.mult), r=[rOh], w=[rW2])
                    yield
                    pss, rss = bank(7)
                    A("pe", lambda e, pss=pss, W2b=W2b, p0=p0: e.matmul(pss[0:64, :], lhsT=ones64[p0:p0 + 64, :], rhs=W2b, start=True, stop=True),
                      r=[rW2, r_o64], w=[rss])
                    A("act", lambda e, W2=W2, pss=pss, p0=p0: e.activation(W2, pss[0:64, :], AF.Ln, bias=cbias[p0:p0 + 64, 0:1], scale=1.0), r=[rss, r_cb], w=[rW2])
                    A("act", lambda e, W2=W2: e.activation(W2, W2, AF.Exp, scale=-0.5), r=[rW2], w=[rW2])
                    yield
                    A("dve", lambda e, Oh=Oh, W2=W2: e.tensor_tensor(Oh, Oh, W2, ALU.mult), r=[rOh, rW2], w=[rOh])
                    A("pool", lambda e, Oh=Oh, p0=p0, c=c, half=half: e.tensor_tensor(hyT[p0:p0 + 64, c, half * 512:(half + 1) * 512], Oh,
                                                                                    szT[p0:p0 + 64, c, half * 512:(half + 1) * 512], ALU.mult),
                      r=[rOh, r_sz[c]], w=[r_hy[c]])
                    yield

        def phase_ret(l):
            for _ in ret_gen(l):
                pass

        def phase_outproj(l):
            m = modl[l]
            for pc in range(2):
                wt, wr = ws_get()
                for j in range(4):
                    cc = pc * 4 + j
                    for hf in range(2):
                        b, rb = bank((j * 2 + hf) % 8)
                        for kc in range(8):
                            A("pe", lambda e, b=b, wt=wt, j=j, kc=kc, hf=hf: e.matmul(b[:], lhsT=wt[:, kc, j * 128:(j + 1) * 128],
                                                                                        rhs=hyT[:, kc, hf * 512:(hf + 1) * 512], start=(kc == 0), stop=(kc == 7)),
                              r=[wr, r_hy[kc]], w=[rb])
                        A("dve", lambda e, b=b, cc=cc, hf=hf: e.scalar_tensor_tensor(xT[:, cc, hf * 512:(hf + 1) * 512], b[:], m[:, 16 + cc:17 + cc],
                                                                                       xT[:, cc, hf * 512:(hf + 1) * 512], ALU.mult, ALU.add),
                          r=[rb, r_modl[l], r_xT[cc]], w=[r_xT[cc]])
                ws_issue()

        def phase_final():
            sumsq_bc(big1[:], r_big1, lambda c: xT[:, c, :], r_xT, 8, ones_mean, r_om, 128, EPS, 0)
            tap("rstdF", big1, [128, T], [r_big1])
            for c in range(8):
                A("dve", lambda e, c=c: e.scalar_tensor_tensor(xT[:, c, :], xT[:, c, :], vT[:, 8 + c:9 + c], big1[:], ALU.mult, ALU.mult),
                  r=[r_xT[c], r_vT, r_big1], w=[r_xT[c]])
            for tt in range(NT):
                s = tt % 2
                for hf in range(2):
                    b, rb = bank(2 * s + hf + 4 * 0)
                    for j in range(4):
                        c = hf * 4 + j
                        A("pe", lambda e, b=b, j=j, c=c, tt=tt: e.transpose(b[:, j * 128:(j + 1) * 128], xT[:, c, tt * 128:(tt + 1) * 128], ident_f[:]),
                          r=[r_xT[c], r_idf], w=[rb])
                    if hf == 0:
                        A("dve", lambda e, b=b, s=s, hf=hf: e.tensor_copy(xst[s][:, hf * 512:(hf + 1) * 512], b[:]), r=[rb], w=[r_xst[s]])
                    else:
                        A("act", lambda e, b=b, s=s, hf=hf: e.copy(xst[s][:, hf * 512:(hf + 1) * 512], b[:]), r=[rb], w=[r_xst[s]])
                A("sp", lambda e, s=s, tt=tt: e.dma_start(out=o_y[tt * 128:(tt + 1) * 128, :], in_=xst[s][:]), r=[r_xst[s]], dma="xst%d" % s)

        tap("xin", xT[:], [128, 8, T], r_xT)
        tap("vT", vT[:], [128, 144], [r_vT])
        if upto >= 1:
            load_small(0, skip_gkc=True)
            mg0 = mod_gen(0)

            def _cb():
                for _ in range(3):
                    try:
                        next(mg0)
                    except StopIteration:
                        pass
            x_load(_cb)
            load_small(0, gkc_only=True)
            for _ in mg0:
                pass
        else:
            x_load(lambda: None)
        for l in range(nlayers):
            if upto < 1:
                break
            if l > 0:
                load_small(l)
            phase_consts(l)
            phase_norm(l)
            tap("hT%d" % l, hyT[:], [128, 8, T], r_hy)
            if upto < 2:
                break
            phase_inproj(l)
            tap("g%d" % l, g_tm, [128, 8, 512], r_g)
            tap("sz%d" % l, szT[:], [128, 8, T], r_sz)
            if upto < 3:
                break
            if upto < 4:
                phase_ret(l)
                tap("yT%d" % l, hyT[:], [128, 8, T], r_hy)
                break
            phase_gqa(l)
            if upto < 5:
                tap("yT%d" % l, hyT[:], [128, 8, T], r_hy)
                break
            phase_mla_proj(l)
            tap("qm%d" % l, qm_all, [128, 4608], r_qm)
            tap("vm%d" % l, Vm_all, [128, 6912], r_Vm)
            tap("ckvT%d" % l, ckvT[:], [128, 1536], [r_ckvT])
            tap("kpeT%d" % l, kpeT, [32, 1536], [r_kpeT])
            phase_mla(l)
            tap("ktm%d" % l, KTm[1], [101, 1536], r_KTm[1])
            tap("qtm%d" % l, QTm[1], [101, 1024], r_QTm[1])
            tap("yT%d" % l, hyT[:], [128, 8, T], r_hy)
            if upto < 6:
                break
            phase_outproj(l)
            tap("xT%d" % l, xT[:], [128, 8, T], r_xT)
        phase_final()
        P.emit(nc)
    return nc, P, tap_out


def _rope_tables(n_tok, dim, grid_w=64, theta=10000.0):
    rows = n_tok // grid_w
    row = np.repeat(np.arange(rows), grid_w).astype(np.float32)
    col = (np.arange(n_tok) % grid_w).astype(np.float32)
    half = dim // 2
    inv = (theta ** (-np.arange(0, half, 2, dtype=np.float32) / half)).astype(np.float32)
    ar = row[:, None] * inv[None]
    ac = col[:, None] * inv[None]
    ang = np.concatenate([ar, ar, ac, ac], -1)
    cos = np.cos(ang).astype(np.float32)
    sin = np.sin(ang).astype(np.float32)
    qd = half // 2
    sgn = np.concatenate([-np.ones(qd), np.ones(qd), -np.ones(qd), np.ones(qd)]).astype(np.float32)
    return np.stack([cos, sin * sgn[None]], 1)


def _const_tables():
    i = np.arange(128, dtype=np.float32)
    rel = i[None, :] - i[:, None]
    ne_f = np.where(rel >= 0, -rel, -1.0e7).astype(np.float32)
    ne_b = np.where(rel <= 0, rel, -1.0e7).astype(np.float32)
    ne_q = np.concatenate([-(i + 1.0), -(128.0 - i)]).astype(np.float32)
    ne_k = np.stack([-(127.0 - i), -i], 1).astype(np.float32)
    return ne_f, ne_b, ne_q, ne_k


def make_in_maps(inputs):
    f = lambda a: np.ascontiguousarray(np.asarray(a, dtype=np.float32))
    xp = f(inputs["x_prompt"]); xs = f(inputs["x_sample"])
    ne_f, ne_b, ne_q, ne_k = _const_tables()
    shared = {
        "w_ada": f(inputs["w_ada"]), "b_ada": f(inputs["b_ada"]).reshape(96, 128), "w_in": f(inputs["w_in"]),
        "w_uq": f(inputs["w_uq"]), "w_ukv": f(inputs["w_ukv"]), "w_out": f(inputs["w_out"]),
        "norm_g": f(inputs["norm_g"]).reshape(32, 128), "final_norm": f(inputs["final_norm"]).reshape(8, 128),
        "ret_logit": f(inputs["ret_decay_logit"]).reshape(32), "gq_g": f(inputs["gqa_q_norm"]).reshape(256),
        "gk_g": f(inputs["gqa_k_norm"]).reshape(256), "mq_g": f(inputs["mla_q_norm"]).reshape(1024),
        "mkv_g": f(inputs["mla_kv_norm"]).reshape(512),
        "ne_f": ne_f, "ne_b": ne_b, "ne_q": ne_q, "ne_k": ne_k,
    }
    kmask = np.zeros((5, 1536), np.float32)
    kmask[4, 0:512] = 1.0
    for s in range(4):
        kmask[s, 512 + 256 * s:512 + 256 * (s + 1)] = 1.0
    rope_s_hd = _rope_tables(1024, 64); rope_s_m = _rope_tables(1024, 32)
    rope_p_hd = np.zeros((1024, 2, 64), np.float32); rope_p_hd[:, 0] = 1.0
    rope_p_m = np.zeros((1024, 2, 32), np.float32); rope_p_m[:, 0] = 1.0
    maps = []
    for core in range(8):
        m = dict(shared)
        m["kmask"] = kmask
        if core < 4:
            b = core
            m["xin"] = xs[b]
            m["cond"] = f(inputs["c"])[b].reshape(8, 128)
            m["c_gk"] = f(inputs["cache_gqa_k"])[b].reshape(4, 512, 128)
            m["c_gv"] = f(inputs["cache_gqa_v"])[b].reshape(4, 512, 128)
            m["c_ckv"] = f(inputs["cache_mla_ckv"])[b]
            m["c_kpe"] = f(inputs["cache_mla_kpe"])[b]
            m["state0"] = f(inputs["state_ret"])[b]
            m["rope_hd"] = rope_s_hd; m["rope_m"] = rope_s_m
            m["qmask"] = np.zeros((5, 1024), np.float32)
            m["carry"] = np.ones(8, np.float32)
        else:
            b0 = 4 * (core - 4)
            m["xin"] = xp[b0:b0 + 4].reshape(1024, 1024)
            m["cond"] = f(inputs["c_ctx"]).reshape(8, 128)
            m["c_gk"] = np.zeros((4, 512, 128), np.float32)
            m["c_gv"] = np.zeros((4, 512, 128), np.float32)
            m["c_ckv"] = np.zeros((4, 512, 128), np.float32)
            m["c_kpe"] = np.zeros((4, 512, 32), np.float32)
            m["state0"] = np.zeros((4, 2, 4, 64, 64), np.float32)
            m["rope_hd"] = rope_p_hd; m["rope_m"] = rope_p_m
            qm = np.full((5, 1024), NEG_BIG, np.float32)
            for s in range(4):
                qm[s, 256 * s:256 * (s + 1)] = 0.0
            m["qmask"] = qm
            m["carry"] = np.array([1, 1, 0, 1, 0, 1, 0, 1], np.float32)
        maps.append({k: np.ascontiguousarray(v) for k, v in m.items()})
    return maps


def assemble(results):
    y_prompt = np.zeros((16, 256, 1024), np.float32)
    y_sample = np.zeros((4, 1024, 1024), np.float32)
    st = np.zeros((16, 4, 2, 4, 64, 64), np.float32)
    gk = np.zeros((16, 4, 256, 2, 64), np.float32)
    gv = np.zeros((16, 4, 256, 2, 64), np.float32)
    ckv = np.zeros((16, 4, 256, 128), np.float32)
    kpe = np.zeros((16, 4, 256, 32), np.float32)
    for core in range(8):
        r = results[core]
        if core < 4:
            y_sample[core] = r["y"]
            continue
        b0 = 4 * (core - 4)
        y_prompt[b0:b0 + 4] = r["y"].reshape(4, 256, 1024)
        so = r["st_out"].reshape(4, 4, 64, 2, 4, 64)
        for s in range(4):
            st[b0 + s, :, 0] = so[:, s, :, 0].transpose(0, 2, 1, 3)
            st[b0 + s, :, 1] = so[:, 3 - s, :, 1].transpose(0, 2, 1, 3)
        gk[b0:b0 + 4] = r["gk_out"].reshape(4, 4, 256, 2, 64).transpose(1, 0, 2, 3, 4)
        gv[b0:b0 + 4] = r["gv_out"].reshape(4, 4, 256, 2, 64).transpose(1, 0, 2, 3, 4)
        ckv[b0:b0 + 4] = r["ckv_out"].reshape(4, 4, 256, 128).transpose(1, 0, 2, 3)
        kpe[b0:b0 + 4] = r["kpe_out"].reshape(4, 4, 256, 32).transpose(1, 0, 2, 3)
    return (y_prompt, y_sample, st, gk, gv, ckv, kpe)


_CACHE = {}


def kernel(**inputs):
    if "nc" not in _CACHE:
        _CACHE["nc"] = build()[0]
    nc = _CACHE["nc"]
    maps = make_in_maps(inputs)
    res = run_bass_kernel_spmd(nc, maps, core_ids=list(range(8)))
    return assemble(res.results)
```

```python
import contextlib
import numpy as np
import concourse.bass as bass
import concourse.mybir as mybir
from concourse.bass_utils import run_bass_kernel_spmd

F32 = mybir.dt.float32
BF16 = mybir.dt.bfloat16
F32R = mybir.dt.float32r
AF = mybir.ActivationFunctionType
ALU = mybir.AluOpType
AX = mybir.AxisListType

ENGS = ("pe", "act", "dve", "pool", "sp")
EPS = 1e-6
NEG_BIG = -1.0e4


class Res:
    __slots__ = ("name", "lw", "rd", "excl")

    def __init__(self, name, excl=False):
        self.name = name
        self.lw = None
        self.rd = []
        self.excl = excl


class Op:
    __slots__ = ("eng", "fn", "waits", "idx", "milestone", "dma_sem", "dma_val", "ms_val")

    def __init__(self, eng, fn):
        self.eng = eng
        self.fn = fn
        self.waits = {}
        self.idx = None
        self.milestone = False
        self.dma_sem = None
        self.dma_val = None
        self.ms_val = None


class Prog:
    def __init__(self):
        self.ops = {e: [] for e in ENGS}
        self.known = {e: {} for e in ENGS}
        self.dma_count = {}
        self.dma_sems = []

    def _dep(self, op, key_val, same_ok=False):
        if key_val is None:
            return
        key, val = key_val
        if key == op.eng and (key == "pe" or same_ok):
            return
        if self.known[op.eng].get(key, -1) >= val:
            return
        if op.waits.get(key, -1) < val:
            op.waits[key] = val

    def add(self, eng, fn, reads=(), writes=(), dma_sem=None):
        op = Op(eng, fn)
        op.idx = len(self.ops[eng])
        for r in reads:
            self._dep(op, r.lw)
            if r.excl:
                for rd in r.rd:
                    self._dep(op, rd, same_ok=True)
        for w in writes:
            self._dep(op, w.lw)
            for rd in w.rd:
                self._dep(op, rd)
        kn = self.known[eng]
        for k, v in op.waits.items():
            if kn.get(k, -1) < v:
                kn[k] = v
        if dma_sem is not None:
            if dma_sem not in self.dma_count:
                self.dma_count[dma_sem] = 0
                self.dma_sems.append(dma_sem)
            self.dma_count[dma_sem] += 16
            op.dma_sem = dma_sem
            op.dma_val = self.dma_count[dma_sem]
            me = ("dma:" + dma_sem, op.dma_val)
        else:
            me = (eng, op.idx)
        for r in reads:
            r.rd.append(me)
        for w in writes:
            w.lw = me
            w.rd = []
        self.ops[eng].append(op)
        return op

    def emit(self, nc):
        for e in ENGS:
            for op in self.ops[e]:
                for k, v in op.waits.items():
                    if not k.startswith("dma:"):
                        self.ops[k][v].milestone = True
        for e in ENGS:
            c = 0
            for op in self.ops[e]:
                if op.milestone:
                    c += 1
                    op.ms_val = c
        with contextlib.ExitStack() as st:
            sems = {}
            for e in ENGS:
                sems[e] = st.enter_context(nc.semaphore("s_" + e))
            for d in self.dma_sems:
                sems["dma:" + d] = st.enter_context(nc.semaphore("d_" + d))
            block = st.enter_context(nc.Block())
            prog = self

            def run(eng_name, eng):
                embed_ok = eng_name in ("act", "dve", "pe")
                for op in prog.ops[eng_name]:
                    wl = [(sems[k], (v if k.startswith("dma:") else prog.ops[k][v].ms_val)) for k, v in op.waits.items()]
                    emb = None
                    if embed_ok and op.dma_sem is None and wl:
                        emb = wl.pop()
                    for (sm, vv) in wl:
                        eng.wait_ge(sm, vv)
                    ins = op.fn(eng)
                    if emb is not None:
                        ins._wait_ge(emb[0], emb[1])
                    if op.dma_sem is not None:
                        ins.then_inc(sems["dma:" + op.dma_sem], 16)
                    elif op.milestone:
                        ins.then_inc(sems[eng_name], 1)

            @block.tensor
            def _(eng):
                run("pe", eng)

            @block.scalar
            def _(eng):
                run("act", eng)

            @block.vector
            def _(eng):
                run("dve", eng)

            @block.gpsimd
            def _(eng):
                run("pool", eng)

            @block.sync
            def _(eng):
                run("sp", eng)
                for d in prog.dma_sems:
                    eng.wait_ge(sems["dma:" + d], prog.dma_count[d])


T = 1024
NT = 8
NK = 12
IN_W = 2848


def build(nlayers=4, taps=(), upto=99):
    nc = bass.Bass("TRN2", target_bir_lowering=False)
    P = Prog()
    taps = set(taps)
    tap_out = {}

    def din(name, shape):
        return nc.dram_tensor(name, list(shape), F32, kind="ExternalInput").ap()

    def dout(name, shape):
        return nc.dram_tensor(name, list(shape), F32, kind="ExternalOutput").ap()

    d_x = din("xin", [T, 1024])
    d_cond = din("cond", [8, 128])
    d_cgk = din("c_gk", [4, 512, 128])
    d_cgv = din("c_gv", [4, 512, 128])
    d_cckv = din("c_ckv", [4, 512, 128])
    d_ckpe = din("c_kpe", [4, 512, 32])
    d_st0 = din("state0", [4, 2, 4, 64, 64])
    d_wada = din("w_ada", [4, 1024, 3072])
    d_bada = din("b_ada", [96, 128])
    d_win = din("w_in", [4, 1024, IN_W])
    d_wuq = din("w_uq", [4, 256, 576])
    d_wukv = din("w_ukv", [4, 128, 768])
    d_wout = din("w_out", [4, 1024, 1024])
    d_ng = din("norm_g", [32, 128])
    d_fn = din("final_norm", [8, 128])
    d_logit = din("ret_logit", [32])
    d_gqg = din("gq_g", [256])
    d_gkg = din("gk_g", [256])
    d_mqg = din("mq_g", [1024])
    d_mkvg = din("mkv_g", [512])
    d_ropehd = din("rope_hd", [T, 2, 64])
    d_ropem = din("rope_m", [T, 2, 32])
    d_qmask = din("qmask", [5, T])
    d_kmask = din("kmask", [5, 1536])
    d_carry = din("carry", [8])
    d_nef = din("ne_f", [128, 128])
    d_neb = din("ne_b", [128, 128])
    d_neq = din("ne_q", [256])
    d_nek = din("ne_k", [128, 2])

    o_y = dout("y", [T, 1024])
    o_st = dout("st_out", [4, 4, 64, 512])
    o_gk = dout("gk_out", [4, T, 128])
    o_gv = dout("gv_out", [4, T, 128])
    o_ckv = dout("ckv_out", [4, T, 128])
    o_kpe = dout("kpe_out", [4, T, 32])

    with contextlib.ExitStack() as st:
        cnt = [0]

        def sbt(name, shape, dt):
            return st.enter_context(nc.sbuf_tensor(name, list(shape), dt))

        def R(name, excl=False):
            return Res(name, excl)

        def _flat(x):
            out = []
            for it in x:
                if isinstance(it, (list, tuple)):
                    out.extend(_flat(it))
                else:
                    out.append(it)
            return out

        def A(eng, fn, r=(), w=(), dma=None):
            return P.add(eng, fn, reads=_flat(r), writes=_flat(w), dma_sem=dma)

        def uniq(prefix):
            cnt[0] += 1
            return "%s%d" % (prefix, cnt[0])

        PP = [st.enter_context(nc.psum_tensor("pp%d" % i, [128, 1024], F32)) for i in range(4)]
        PR = [[R("pp%d_%d" % (i, j), excl=True) for j in range(2)] for i in range(4)]

        def bank(i):
            return PP[i // 2][:, (i % 2) * 512:(i % 2) * 512 + 512], PR[i // 2][i % 2]

        class Region:
            def __init__(self, name, nbytes, cell):
                self.t = sbt(name, [128, nbytes // 2], BF16)
                self.cell = cell
                self.cells = [R("%s_c%d" % (name, i)) for i in range((nbytes + cell - 1) // cell)]
                self.nbytes = nbytes

            def view(self, off, nbytes, dt, p0=0, p1=128):
                assert off % 4 == 0 and off + nbytes <= self.nbytes, (off, nbytes, self.nbytes)
                ap = self.t[p0:p1, off // 2:(off + nbytes) // 2]
                if dt != BF16:
                    ap = ap.bitcast(dt)
                res = self.cells[off // self.cell:(off + nbytes - 1) // self.cell + 1]
                return ap, res

        xT = sbt("xT", [128, 8, T], F32)
        r_xT = [R("xT%d" % c) for c in range(8)]
        hyT = sbt("hyT", [128, 8, T], BF16)
        r_hy = [R("hy%d" % c) for c in range(8)]
        szT = sbt("szT", [128, 8, T], BF16)
        r_sz = [R("sz%d" % c) for c in range(8)]

        cbias = sbt("cbias", [128, 8], F32); r_cb = R("cb")
        ident_f = sbt("ident_f", [128, 128], F32); r_idf = R("idf")
        ident_b = sbt("ident_b", [128, 128], BF16); r_idb = R("idb")
        ones_mean = sbt("ones_mean", [128, 128], BF16); r_om = R("om")
        ones64 = sbt("ones64", [128, 64], BF16); r_o64 = R("o64")
        ones_r = sbt("ones_r", [128, 64], F32); r_onr = R("onr")

        NSLOT = 3
        ws_t = [sbt("ws%d" % i, [128, 8, 512], BF16) for i in range(NSLOT)]
        ws_r = [R("ws%d" % i) for i in range(NSLOT)]

        wuq = sbt("wuq", [128, 2, 576], BF16); r_wuq = R("wuq")
        wukv = sbt("wukvV", [128, 384], BF16); r_wukv = R("wukvV")
        wukp = sbt("wukp", [128, 6, 96], BF16); r_wukp = R("wukp")

        vT = sbt("vT", [128, 144], F32); r_vT = R("vT")
        scond = sbt("scond", [128, 8], BF16); r_scond = R("scond")
        modl = [sbt("modl%d" % l, [128, 32], F32) for l in range(4)]
        r_modl = [R("modl%d" % l) for l in range(4)]

        Gq8 = sbt("Gq8", [128, 64], F32); Gk8 = sbt("Gk8", [128, 64], F32)
        Gmq = sbt("Gmq", [128, 256], F32); Gmkv = sbt("Gmkv", [128, 128], F32)
        r_G = R("G")
        RD = Region("RD", 4096, 4096)
        ropeG_all, r_ropeG = RD.view(0, 4096, F32); ropeG = ropeG_all.rearrange("p (t a d) -> p t a d", t=8, a=2)
        ropeM = sbt("ropeM", [128, 8, 2, 32], F32); r_ropeM = R("ropeM")
        nlg = sbt("nlg", [128, 32], F32); r_nlg = R("nlg")
        NEf = sbt("NEf", [128, 128], BF16); NEb = sbt("NEb", [128, 128], BF16)
        NEq = sbt("NEq", [128, 2, 128], F32); NEk = sbt("NEk", [128, 2], F32)
        mcar = sbt("mcar", [128, 8], F32)
        r_NE = R("NE")

        qdec = sbt("qdec", [128, 2, 4, 128], BF16); r_qdec = R("qdec")
        kdec = sbt("kdec", [128, 8], F32); r_kdec = R("kdec")
        mix1 = sbt("mix1", [128, 1024], F32)
        cdec = mix1[0:64, 0:512].rearrange("d (a e) -> d a e", a=8); r_cdec = R("cdec")
        S_out = mix1[0:64, 512:1024]; r_Sout = R("Sout")
        kpeT = mix1[64:96, 0:768].bitcast(BF16); r_kpeT = R("kpeT")
        isel = mix1[64:96, 768:816].bitcast(BF16); r_isel = R("isel")
        mix2 = sbt("mix2", [128, 1024], F32)
        S_aft = mix2[0:64, 0:512]; r_Saft = R("Saft")
        S_tmp = mix2[0:64, 512:1024]; r_Stmp = R("Stmp")
        rowr = sbt("rowr", [65, 512], F32); r_rowr = R("rowr")

        V_ret = sbt("V_ret", [128, 8, 256], BF16); r_vret = [R("vret%d" % t) for t in range(8)]
        ckvn = sbt("ckvn", [128, 12, 128], BF16); r_ckvn = [R("ckvn%d" % t) for t in range(12)]
        kpeb = sbt("kpeb", [128, 12, 32], BF16); r_kpeb = [R("kpeb%d" % t) for t in range(12)]

        Vg = sbt("Vg", [128, 12, 3, 64], BF16); r_Vg = [R("Vg%d" % k) for k in range(12)]
        QTg = sbt("QTg", [128, 2, T], BF16); r_QTg = [R("QTg%d" % s_) for s_ in range(2)]
        KTg = sbt("KTg", [128, 2, 1536], BF16); r_KTg = [R("KTg%d" % g) for g in range(2)]
        cqnT = sbt("cqnT", [128, 2, T], BF16); r_cqnT = R("cqnT")
        ckvT = sbt("ckvT", [128, 1536], BF16); r_ckvT = R("ckvT")
        small = [sbt("small%d" % i, [128, 16], F32) for i in range(4)]; r_small = [R("small%d" % i) for i in range(4)]
        stage0 = sbt("stage0", [128, 416], F32)
        r_stg = [R("stg_gk"), R("stg_gv"), R("stg_ckv"), R("stg_kpe")]

        RA = Region("RA", 8192, 1024)
        PT = []; r_PT = []
        for i in range(4):
            ap_, rs_ = RA.view(i * 2048, 2048, BF16); PT.append(ap_); r_PT.append(rs_)
        xst = []; r_xst = []
        for i in range(2):
            ap_, rs_ = RA.view(i * 4096, 4096, F32); xst.append(ap_); r_xst.append(rs_)
        qk_all, _ = RA.view(0, 8192, BF16)
        qk_tm = qk_all.rearrange("p (t c) -> p t c", t=8)
        r_qk = [[RA.cells[t]] for t in range(8)]
        gkc_all, r_gkc = RA.view(4096, 1024, BF16); gkc = gkc_all.rearrange("p (k d) -> p k d", k=4)
        cqn_all, _ = RA.view(0, 4096, BF16); cqn = cqn_all.rearrange("p (t c) -> p t c", t=8)
        r_cqn = [[RA.cells[t // 2]] for t in range(8)]
        intra_all, r_intra = RD.view(0, 4096, F32); intra = intra_all.rearrange("p (a h i) -> p a h i", a=2, h=4)

        RB = Region("RB", 40960, 1024)
        g_all, _ = RB.view(0, 8192, BF16); g_tm = g_all.rearrange("p (t c) -> p t c", t=8)
        r_g = [[RB.cells[t]] for t in range(8)]
        kd_all, _ = RB.view(8192, 8192, BF16); kd_tm = kd_all.rearrange("p (t a h d) -> p t a h d", t=8, a=2, h=4)
        r_kd = [[RB.cells[8 + t]] for t in range(8)]
        qkT_all, r_qkT_all = RB.view(16384, 8192, BF16); qkT = qkT_all.rearrange("p (r k t) -> p r k t", r=2, k=2)
        r_qkT = [RB.cells[16 + 4 * (h // 2):16 + 4 * (h // 2) + 4] for h in range(4)]
        Sin_all, _ = RB.view(24576, 8192, BF16); S_in = Sin_all.rearrange("p (t c) -> p t c", t=8)
        r_Sin = [[RB.cells[24 + t]] for t in range(8)]
        qd_all, r_qd_all = RB.view(32768, 4096, BF16); qd = qd_all.rearrange("p (a t) -> p a t", a=2)
        r_qd = r_qd_all
        AT = []; r_AT = []
        for i in range(2):
            ap_, rs_ = RB.view(36864 + i * 2048, 2048, BF16); AT.append(ap_.rearrange("p (a c) -> p a c", a=2)); r_AT.append(rs_)
        qm_all, _ = RB.view(0, 9216, BF16); qm_tm = qm_all.rearrange("p (t h x) -> p t h x", t=8, h=6)
        r_qm = [RB.cells[(t * 1152) // 1024:(t * 1152 + 1151) // 1024 + 1] for t in range(8)]
        Vm_all, r_Vm_all = RB.view(9216, 13824, BF16); Vm = Vm_all.rearrange("p (k r x d) -> p k r x d", k=12, r=3, x=3)
        r_Vm = [RB.cells[(9216 + k * 1152) // 1024:(9216 + k * 1152 + 1151) // 1024 + 1] for k in range(12)]
        KTm = []; r_KTm = []
        QTm = []; r_QTm = []
        for i in range(2):
            ap_, rs_ = RB.view(23040 + i * 3072, 3072, BF16, 0, 101); KTm.append(ap_); r_KTm.append(rs_)
        for i in range(2):
            ap_, rs_ = RB.view(29184 + i * 2048, 2048, BF16, 0, 101); QTm.append(ap_); r_QTm.append(rs_)

        RC = Region("RC", 12288, 2048)
        Xs = []; r_Xs = []; T1 = []; r_T1 = []; T2 = []; r_T2 = []; sqb = []; r_sqb = []
        for i in range(2):
            ap_, rs_ = RC.view(i * 2048, 2048, F32); Xs.append(ap_); r_Xs.append(rs_)
            ap_, rs_ = RC.view(4096 + i * 2048, 2048, F32); T1.append(ap_); r_T1.append(rs_)
            ap_, rs_ = RC.view(8192 + i * 2048, 2048, F32); T2.append(ap_); r_T2.append(rs_)
            ap_, rs_ = RC.view(8192 + i * 2048, 2048, BF16); sqb.append(ap_); r_sqb.append(rs_)
        vecA_all, r_vecA = RC.view(8192, 512, F32); vecA = vecA_all[0:48, :]
        vecB_all, r_vecB = RC.view(10240, 512, F32); vecB = vecB_all[0:96, :]
        big1, r_big1 = RC.view(0, 4096, F32)
        big2, r_big2 = RC.view(4096, 4096, F32)
        for i in range(2):
            ap_, rs_ = RB.view(24576 + i * 6144, 2048, F32); Xs.append(ap_); r_Xs.append(rs_)
            ap_, rs_ = RB.view(24576 + i * 6144 + 2048, 2048, F32); T1.append(ap_); r_T1.append(rs_)
            ap_, rs_ = RB.view(24576 + i * 6144 + 4096, 2048, F32); T2.append(ap_); r_T2.append(rs_)
        NTMP = 4

        def dma_in(eng, out_ap, in_ap, w, nonctg=False):
            name = uniq("su")
            if nonctg:
                return A(eng, lambda e: e.dma_start(out=out_ap, in_=in_ap, allow_slow_non_contiguous=True), w=w, dma=name)
            return A(eng, lambda e: e.dma_start(out=out_ap, in_=in_ap), w=w, dma=name)

        def x_dma(tt):
            s_ = tt % 2
            A("sp", lambda e: e.dma_start(out=xst[s_][:], in_=d_x[tt * 128:(tt + 1) * 128, :]), w=[r_xst[s_]], dma="xst%d" % s_)
        x_dma(0)
        x_dma(1)

        A("dve", lambda e: e.memset(cbias[:, 0:1], EPS), w=[r_cb])
        A("dve", lambda e: e.memset(cbias[:, 1:2], float(np.log(0.125))), w=[r_cb])
        A("dve", lambda e: e.memset(cbias[:, 2:3], 1.0), w=[r_cb])
        A("dve", lambda e: e.memset(cbias[:, 3:4], 64.0 * EPS), w=[r_cb])
        A("dve", lambda e: e.memset(cbias[:, 4:5], 128.0 * EPS), w=[r_cb])
        A("dve", lambda e: e.memset(cbias[:, 5:6], 256.0 * EPS), w=[r_cb])
        A("dve", lambda e: e.memset(cbias[:, 6:7], -1.0), w=[r_cb])
        A("dve", lambda e: e.memset(ident_f[:], 0.0), w=[r_idf])
        A("pool", lambda e: e.affine_select(out=ident_f[:], in_=ident_f[:], pattern=[[-1, 128]], compare_op=ALU.not_equal,
                                            fill=1.0, base=0, channel_multiplier=1), r=[r_idf], w=[r_idf])
        A("dve", lambda e: e.tensor_copy(ident_b[:], ident_f[:]), r=[r_idf], w=[r_idb])
        A("dve", lambda e: e.memset(ones_mean[:], 1.0 / 1024.0), w=[r_om])
        A("dve", lambda e: e.memset(ones64[:], 1.0 / 64.0), w=[r_o64])
        A("dve", lambda e: e.memset(big2[:, 0:64], 1.0), w=[r_big2])
        A("dve", lambda e: e.tensor_copy(ones_r[:].bitcast(F32R), big2[:, 0:64]), r=[r_big2], w=[r_onr])
        A("dve", lambda e: e.memset(isel, 0.0), w=[r_isel])
        A("dve", lambda e: e.tensor_copy(isel[:, 64:96], ident_f[0:32, 0:32]), r=[r_idf, r_isel], w=[r_isel])
        A("dve", lambda e: e.memset(wukp[:], 0.0), w=[r_wukp])
        A("dve", lambda e: e.memset(Vg[:, :, 1, :], 1.0), w=r_Vg)

        dma_in("sp", vecA[0:8, :], d_cond, [r_vecA])
        dma_in("sp", vecA[8:16, :], d_fn, [r_vecA])
        dma_in("sp", vecA[16:48, :], d_ng, [r_vecA])
        dma_in("sp", vecB[:], d_bada, [r_vecB])
        dma_in("sp", ropeM[:].rearrange("p t a d -> p t (a d)"), d_ropem.rearrange("(t p) a d -> p t (a d)", p=128), [r_ropeM])
        dma_in("sp", big1[:, 0:32], d_logit.partition_broadcast(128), [r_big1])
        dma_in("pool", NEf[:], d_nef, [r_NE])
        dma_in("pool", NEb[:], d_neb, [r_NE])
        dma_in("sp", NEq[:].rearrange("p a i -> p (a i)"), d_neq.partition_broadcast(128), [r_NE])
        dma_in("sp", NEk[:], d_nek, [r_NE])
        dma_in("sp", mcar[:], d_carry.partition_broadcast(128), [r_NE])
        A("dve", lambda e: e.memset(QTg[:], 0.0), w=r_QTg)
        A("dve", lambda e: e.memset(KTg[:], 0.0), w=r_KTg)
        for s_ in range(2):
            dma_in("pool", QTg[64:69, s_, :], d_qmask, [r_QTg[s_]])
        for g in range(2):
            dma_in("pool", KTg[64:69, g, :], d_kmask, [r_KTg[g]])

        A("act", lambda e: e.activation(big1[:, 32:64], big1[:, 0:32], AF.Exp, scale=-1.0), r=[r_big1], w=[r_big1])
        A("act", lambda e: e.activation(nlg[:], big1[:, 32:64], AF.Ln, bias=cbias[:, 2:3], scale=1.0), r=[r_big1, r_cb], w=[r_nlg])

        bA, rA = bank(6)
        A("pe", lambda e: e.transpose(bA[:, 0:48], vecA[0:48, :], ident_f[0:48, 0:48]), r=[r_vecA, r_idf], w=[rA])
        A("pe", lambda e: e.transpose(bA[:, 48:144], vecB[0:96, :], ident_f[0:96, 0:96]), r=[r_vecB, r_idf], w=[rA])
        A("dve", lambda e: e.tensor_copy(vT[:], bA[:, 0:144]), r=[rA], w=[r_vT])
        A("act", lambda e: e.activation(scond[:], vT[:, 0:8], AF.Silu), r=[r_vT], w=[r_scond])

        def x_load(cb):
            for tt in range(NT):
                s = tt % 2
                if tt >= 2:
                    x_dma(tt)
                for hf in range(2):
                    b, rb = bank(2 * s + hf)
                    for j in range(4):
                        c = hf * 4 + j
                        A("pe", lambda e, b=b, j=j, c=c, s=s: e.transpose(b[:, j * 128:(j + 1) * 128], xst[s][:, c * 128:(c + 1) * 128], ident_f[:]),
                          r=[r_xst[s], r_idf], w=[rb])
                    eng = "dve" if hf == 0 else "act"
                    if eng == "dve":
                        A("dve", lambda e, b=b, hf=hf, tt=tt: e.tensor_copy(xT[:, hf * 4:hf * 4 + 4, tt * 128:(tt + 1) * 128],
                                                                            b.rearrange("p (j t) -> p j t", j=4)),
                          r=[rb], w=r_xT[hf * 4:hf * 4 + 4])
                    else:
                        A("act", lambda e, b=b, hf=hf, tt=tt: e.copy(xT[:, hf * 4:hf * 4 + 4, tt * 128:(tt + 1) * 128],
                                                                      b.rearrange("p (j t) -> p j t", j=4)),
                          r=[rb], w=r_xT[hf * 4:hf * 4 + 4])
                cb()

        pieces = []

        def wview(d, l):
            return d[l].rearrange("(c p) n -> p c n", p=128)

        def add_ada(l):
            for pc in range(6):
                pieces.append([(0, 512, wview(d_wada, l)[:, :, pc * 512:(pc + 1) * 512])])

        def add_in(l):
            v = wview(d_win, l)
            pieces.append([(0, 512, v[:, :, 0:512])])
            pieces.append([(0, 512, v[:, :, 768:1280])])
            pieces.append([(0, 256, v[:, :, 512:768]), (256, 128, v[:, :, 1280:1408]), (384, 128, v[:, :, 1664:1792])])
            pieces.append([(0, 256, v[:, :, 1408:1664]), (256, 32, v[:, :, 1792:1824])])
            pieces.append([(0, 512, v[:, :, 1824:2336])])
            pieces.append([(0, 512, v[:, :, 2336:2848])])

        def add_out(l):
            for pc in range(2):
                pieces.append([(0, 512, wview(d_wout, l)[:, :, pc * 512:(pc + 1) * 512])])

        add_ada(0)
        for l in range(nlayers):
            add_in(l)
            if l + 1 < nlayers:
                add_ada(l + 1)
            add_out(l)
        ws_state = {"loaded": 0, "used": 0}

        def ws_issue():
            i = ws_state["loaded"]
            if i >= len(pieces):
                return
            s = i % NSLOT
            for (off, wd, src) in pieces[i]:
                A("pool", lambda e, s=s, off=off, wd=wd, src=src: e.dma_start(out=ws_t[s][:, :, off:off + wd], in_=src),
                  w=[ws_r[s]], dma="ws%d" % s)
            ws_state["loaded"] += 1

        def ws_get():
            i = ws_state["used"]
            ws_state["used"] += 1
            return ws_t[i % NSLOT], ws_r[i % NSLOT]

        for _ in range(NSLOT):
            ws_issue()

        tmp_i = [0]

        def nxt():
            tmp_i[0] = (tmp_i[0] + 1) % NTMP
            return tmp_i[0]

        bank_rr = [0]

        def rope_ops(src, r_src, cos, sin, r_tab, dst, r_dst, nh, hd, ti, c0=0):
            w = nh * hd
            t1 = T1[ti][:, c0:c0 + w]; t2 = T2[ti][:, c0:c0 + w]
            A("dve", lambda e: e.tensor_tensor(t1.rearrange("p (h d) -> p h d", h=nh), src.rearrange("p (h d) -> p h d", h=nh),
                                               cos.unsqueeze(1).broadcast_to([128, nh, hd]), ALU.mult), r=[r_src, r_tab], w=[r_T1[ti]])
            sv = src.rearrange("p (h a q s) -> p h a q s", h=nh, a=2, q=2)
            tv = t2.rearrange("p (h a q s) -> p h a q s", h=nh, a=2, q=2)
            sn = sin.rearrange("p (a q s) -> p a q s", a=2, q=2)
            for q in range(2):
                snq = sn[:, :, q, :].unsqueeze(1).broadcast_to([128, nh, 2, hd // 4])
                A("dve", lambda e, q=q, snq=snq: e.tensor_tensor(tv[:, :, :, q, :], sv[:, :, :, 1 - q, :], snq, ALU.mult),
                  r=[r_src, r_tab], w=[r_T2[ti]])
            yield
            A("dve", lambda e: e.tensor_tensor(dst, t1, t2, ALU.add), r=[r_T1[ti], r_T2[ti]], w=[r_dst])

        def transposes_gen(items, banks=(4, 5, 6, 7), evac=None):
            k = 0
            for (src_fn, rs_fn, w, dst_fn, r_dst, ntile, t0) in items:
                for half in range((ntile + 3) // 4):
                    bi = banks[bank_rr[0] % len(banks)]
                    bank_rr[0] += 1
                    b, rb = bank(bi)
                    bb = b.bitcast(BF16)
                    n = min(4, ntile - half * 4)
                    for j in range(n):
                        tt = half * 4 + j
                        A("pe", lambda e, bb=bb, j=j, tt=tt, src_fn=src_fn, w=w: e.transpose(bb[0:w, j * 128:(j + 1) * 128], src_fn(tt), ident_b[:]),
                          r=[rs_fn(tt), r_idb], w=[rb])
                    eng = evac if evac is not None else ("act" if (k % 2 == 0) else "dve")
                    k += 1
                    dst = dst_fn(half)
                    if eng == "act":
                        A("act", lambda e, bb=bb, w=w, n=n, dst=dst: e.copy(dst, bb[0:w, 0:n * 128]), r=[rb], w=[r_dst])
                    else:
                        A("dve", lambda e, bb=bb, w=w, n=n, dst=dst: e.tensor_copy(dst, bb[0:w, 0:n * 128]), r=[rb], w=[r_dst])
                    yield

        def transposes(items, banks=(4, 5, 6, 7), evac=None):
            for _ in transposes_gen(items, banks, evac):
                pass

        def tap(name, ap, shape, reads):
            if name not in taps:
                return
            d = dout("tap_" + name, shape)
            tap_out[name] = shape
            A("pool", lambda e: e.dma_start(out=d, in_=ap), r=reads, dma=uniq("tap"))

        def mod_gen(l):
            pm, rpm = bank(7)
            for pc in range(6):
                wt, wr = ws_get()
                for j in range(4):
                    cc = pc * 4 + j
                    for kc in range(8):
                        A("pe", lambda e, wt=wt, j=j, kc=kc, cc=cc: e.matmul(pm[:, cc:cc + 1], lhsT=wt[:, kc, j * 128:(j + 1) * 128],
                                                                              rhs=scond[:, kc:kc + 1], start=(kc == 0), stop=(kc == 7)),
                          r=[wr, r_scond], w=[rpm])
                    yield
                ws_issue()
            m = modl[l]
            A("dve", lambda e: e.tensor_tensor(m[:, 0:24], pm[:, 0:24], vT[:, 48 + 24 * l:72 + 24 * l], ALU.add), r=[rpm, r_vT], w=[r_modl[l]])
            A("dve", lambda e: e.scalar_tensor_tensor(m[:, 24:32], m[:, 8:16], 1.0, vT[:, 16 + 8 * l:24 + 8 * l], ALU.add, ALU.mult),
              r=[r_modl[l], r_vT], w=[r_modl[l]])
            yield

        def phase_mod(l):
            for _ in mod_gen(l):
                pass

        def sumsq_bc(dst_bc, r_dst, src_chunks, r_src, nchunk, ones, r_ones, np_, eps_n, PPi):
            for c in range(nchunk):
                s = c % 2
                A("act", lambda e, c=c, s=s: e.activation(sqb[s][0:np_, :], src_chunks(c), AF.Square), r=[r_src[c]], w=[r_sqb[s]])
                for hf in range(2):
                    A("pe", lambda e, c=c, s=s, hf=hf: e.matmul(PP[PPi][0:np_, hf * 512:(hf + 1) * 512], lhsT=ones[0:np_, 0:np_],
                                                                 rhs=sqb[s][0:np_, hf * 512:(hf + 1) * 512], start=(c == 0), stop=(c == nchunk - 1)),
                      r=[r_sqb[s], r_ones], w=[PR[PPi][hf]])
            A("act", lambda e: e.activation(dst_bc, PP[PPi][0:np_, :], AF.Ln, bias=cbias[0:np_, 0:1], scale=1.0), r=PR[PPi] + [r_cb], w=[r_dst])
            A("act", lambda e: e.activation(dst_bc, dst_bc, AF.Exp, scale=-0.5), r=[r_dst], w=[r_dst])

        def phase_norm(l):
            sumsq_bc(big1[:], r_big1, lambda c: xT[:, c, :], r_xT, 8, ones_mean, r_om, 128, EPS, 0)
            m = modl[l]
            big3, r_big3 = RC.view(8192, 4096, F32)
            for c in range(8):
                bb_, rbb_ = (big2, r_big2) if c % 2 == 0 else (big3, r_big3)
                A("dve", lambda e, c=c, bb_=bb_: e.tensor_tensor(bb_, xT[:, c, :], big1[:], ALU.mult), r=[r_xT[c], r_big1], w=[rbb_])
                A("act", lambda e, c=c, bb_=bb_: e.activation(hyT[:, c, :], bb_, AF.Identity, bias=m[:, c:c + 1], scale=m[:, 24 + c:25 + c]),
                  r=[rbb_, r_modl[l]], w=[r_hy[c]])

        def load_small(l, gkc_only=False, skip_gkc=False):
            if gkc_only:
                A("pool", lambda e: e.dma_start(out=gkc, in_=d_cgk[l].rearrange("(k p) d -> p k d", p=128)), w=[r_gkc], dma="gkc")
                return
            A("sp", lambda e: e.dma_start(out=ropeG.rearrange("p t a d -> p t (a d)"), in_=d_ropehd.rearrange("(t p) a d -> p t (a d)", p=128)),
              w=[r_ropeG], dma="ropeG")
            A("sp", lambda e: e.dma_start(out=Gq8[:], in_=d_gqg[l * 64:(l + 1) * 64].partition_broadcast(128)), w=[r_G], dma="g1")
            A("sp", lambda e: e.dma_start(out=Gk8[:], in_=d_gkg[l * 64:(l + 1) * 64].partition_broadcast(128)), w=[r_G], dma="g2")
            A("sp", lambda e: e.dma_start(out=Gmq[:], in_=d_mqg[l * 256:(l + 1) * 256].partition_broadcast(128)), w=[r_G], dma="g3")
            A("sp", lambda e: e.dma_start(out=Gmkv[:], in_=d_mkvg[l * 128:(l + 1) * 128].partition_broadcast(128)), w=[r_G], dma="g4")
            A("dve", lambda e: e.tensor_scalar(Gq8[:], Gq8[:], 8.0, None, ALU.mult), r=[r_G], w=[r_G])
            A("dve", lambda e: e.tensor_scalar(Gk8[:], Gk8[:], 8.0, None, ALU.mult), r=[r_G], w=[r_G])
            A("dve", lambda e: e.tensor_scalar(Gmq[:], Gmq[:], 16.0, None, ALU.mult), r=[r_G], w=[r_G])
            A("dve", lambda e: e.tensor_scalar(Gmkv[:], Gmkv[:], float(np.sqrt(128.0)), None, ALU.mult), r=[r_G], w=[r_G])
            A("pool", lambda e: e.dma_start(out=wuq[:], in_=d_wuq[l].rearrange("(c p) n -> p c n", p=128)), w=[r_wuq], dma="wuq")
            A("pool", lambda e: e.dma_start(out=wukv[:].rearrange("p (h d) -> p h d", h=6), in_=d_wukv[l].rearrange("p (h x) -> p h x", h=6)[:, :, 64:128]),
              w=[r_wukv], dma="wukv")
            A("pool", lambda e: e.dma_start(out=wukp[:, :, 0:64], in_=d_wukv[l].rearrange("p (h x) -> p h x", h=6)[:, :, 0:64]), w=[r_wukp], dma="wukp")
            if not skip_gkc:
                A("pool", lambda e: e.dma_start(out=gkc, in_=d_cgk[l].rearrange("(k p) d -> p k d", p=128)), w=[r_gkc], dma="gkc")
            for g in range(2):
                A("pool", lambda e, g=g: e.dma_start(out=Vg[:, 0:4, 2 * g, :], in_=d_cgv[l].rearrange("(k p) (g d) -> p k g d", p=128, g=2)[:, :, g, :]),
                  w=r_Vg[0:4], dma="vgc")
            A("pool", lambda e: e.dma_start(out=ckvn[:, 0:4, :], in_=d_cckv[l].rearrange("(k p) d -> p k d", p=128)), w=r_ckvn[0:4], dma="ckvc")
            A("pool", lambda e: e.dma_start(out=kpeb[:, 0:4, :], in_=d_ckpe[l].rearrange("(k p) d -> p k d", p=128)), w=r_kpeb[0:4], dma="kpec")
            A("sp", lambda e: e.dma_start(out=S_aft.rearrange("d (a e) -> d a e", a=8), in_=d_st0[l].rearrange("a h d e -> d (a h) e")),
              w=[r_Saft], dma="st0")

        def phase_consts(l):
            LN8 = float(np.log(0.125))
            for a in range(2):
                NE = NEf if a == 0 else NEb
                for h in range(4):
                    col = l * 8 + a * 4 + h
                    p0 = 64 * (h % 2)
                    A("act", lambda e, a=a, h=h, col=col, p0=p0: e.activation(qdec[p0:p0 + 64, a, h, :], NEq[p0:p0 + 64, a, :], AF.Exp,
                                                                              scale=nlg[p0:p0 + 64, col:col + 1]),
                      r=[r_NE, r_nlg], w=[r_qdec])
            sm = small[0]
            A("dve", lambda e: e.tensor_tensor(sm[:, 0:8].rearrange("p (a h) -> p a h", a=2), nlg[:, l * 8:l * 8 + 8].rearrange("p (a h) -> p a h", a=2),
                                               NEk[:].unsqueeze(2).broadcast_to([128, 2, 4]), ALU.mult), r=[r_nlg, r_NE], w=[r_small[0]])
            A("act", lambda e: e.activation(kdec[:], sm[:, 0:8], AF.Exp, bias=cbias[:, 1:2]), r=[r_small[0], r_cb], w=[r_kdec])
            A("act", lambda e: e.activation(sm[0:64, 8:16], nlg[0:64, l * 8:l * 8 + 8], AF.Exp, scale=-128.0), r=[r_nlg, r_small[0]], w=[r_small[0]])
            A("dve", lambda e: e.tensor_copy(cdec, sm[0:64, 8:16].unsqueeze(2).broadcast_to([64, 8, 64])), r=[r_small[0]], w=[r_cdec])

        def phase_consts_b(l):
            LN8 = float(np.log(0.125))
            for a in range(2):
                NE = NEf if a == 0 else NEb
                for h in range(4):
                    col = l * 8 + a * 4 + h
                    A("act", lambda e, a=a, h=h, col=col, NE=NE: e.activation(intra[:, a, h, :], NE[:], AF.Exp, bias=cbias[:, 1:2], scale=nlg[:, col:col + 1]),
                      r=[r_NE, r_nlg, r_cb], w=[r_intra])

        def phase_inproj(l):
            m = modl[l]
            transposes([
                (lambda tt: gkc[:, tt, 0:64], lambda tt: r_gkc, 64, lambda half: KTg[0:64, 0, 0:512], r_KTg[0], 4, 0),
                (lambda tt: gkc[:, tt, 64:128], lambda tt: r_gkc, 64, lambda half: KTg[0:64, 1, 0:512], r_KTg[1], 4, 0),
                (lambda tt: ckvn[:, tt, :], lambda tt: r_ckvn[tt], 128, lambda half: ckvT[:, 0:512], r_ckvT, 4, 0),
                (lambda tt: kpeb[:, tt, :], lambda tt: r_kpeb[tt], 32, lambda half: kpeT[:, 0:512], r_kpeT, 4, 0),
            ])
            widths = [512, 512, 512, 288]
            def mm_group(g):
                wt, wr = ws_get()
                wd = widths[g]
                def prep_gen(tt, b, rb, ti):
                    X = Xs[ti]
                    A("act", lambda e, X=X, b=b, wd=wd: e.copy(X[:, 0:wd], b[:, 0:wd]), r=[rb], w=[r_Xs[ti]])
                    yield
                    sg = stage0
                    if g == 0:
                        yield from rope_ops(X[:, 0:512], r_Xs[ti], ropeG[:, tt, 0, :], ropeG[:, tt, 1, :], r_ropeG,
                                 qk_tm[:, tt, :], r_qk[tt], 8, 64, ti, 0)
                        A("pool", lambda e, tt=tt: e.tensor_tensor(kd_tm[:, tt, :, :, :],
                                                                   qk_tm[:, tt, 256:512].rearrange("p (h d) -> p h d", h=4).unsqueeze(1).broadcast_to([128, 2, 4, 64]),
                                                                   kdec[:].rearrange("p (a h) -> p a h", a=2).unsqueeze(3).broadcast_to([128, 2, 4, 64]), ALU.mult),
                          r=[r_qk[tt], r_kdec], w=[r_kd[tt]])
                    elif g == 1:
                        sq = T1[ti]; sm = small[ti]
                        A("act", lambda e, X=X, sq=sq: e.activation(sq[:], X[:], AF.Square), r=[r_Xs[ti]], w=[r_T1[ti]])
                        yield
                        A("dve", lambda e, sq=sq, sm=sm: e.tensor_reduce(sm[:, 0:8], sq[:].rearrange("p (h d) -> p h d", h=8), AX.X, ALU.add),
                          r=[r_T1[ti]], w=[r_small[ti]])
                        yield
                        A("act", lambda e, sm=sm: e.activation(sm[:, 8:16], sm[:, 0:8], AF.Ln, bias=cbias[:, 3:4], scale=1.0), r=[r_small[ti], r_cb], w=[r_small[ti]])
                        yield
                        A("act", lambda e, sm=sm: e.activation(sm[:, 8:16], sm[:, 8:16], AF.Exp, scale=-0.5), r=[r_small[ti]], w=[r_small[ti]])
                        yield
                        Tn = T2[ti]
                        A("dve", lambda e, X=X, sm=sm, Tn=Tn: e.tensor_tensor(Tn[:].rearrange("p (h d) -> p h d", h=8), X[:].rearrange("p (h d) -> p h d", h=8),
                                                                              sm[:, 8:16].unsqueeze(2).broadcast_to([128, 8, 64]), ALU.mult),
                          r=[r_Xs[ti], r_small[ti]], w=[r_T2[ti]])
                        yield
                        A("dve", lambda e, X=X, Tn=Tn: e.tensor_tensor(X[:, 0:384].rearrange("p (h d) -> p h d", h=6), Tn[:, 0:384].rearrange("p (h d) -> p h d", h=6),
                                                                        Gq8[:].unsqueeze(1).broadcast_to([128, 6, 64]), ALU.mult),
                          r=[r_T2[ti], r_G], w=[r_Xs[ti]])
                        A("dve", lambda e, X=X, Tn=Tn: e.tensor_tensor(X[:, 384:512].rearrange("p (h d) -> p h d", h=2), Tn[:, 384:512].rearrange("p (h d) -> p h d", h=2),
                                                                        Gk8[:].unsqueeze(1).broadcast_to([128, 2, 64]), ALU.mult),
                          r=[r_T2[ti], r_G], w=[r_Xs[ti]])
                        yield
                        A("act", lambda e, X=X, sg=sg: e.copy(sg[:, 0:128], X[:, 384:512]), r=[r_Xs[ti]], w=[r_stg[0]])
                        A("sp", lambda e, sg=sg, tt=tt: e.dma_start(out=o_gk[l, tt * 128:(tt + 1) * 128, :], in_=sg[:, 0:128]), r=[r_stg[0]], dma="stg0")
                        yield from rope_ops(X[:, 0:512], r_Xs[ti], ropeG[:, tt, 0, :], ropeG[:, tt, 1, :], r_ropeG, g_tm[:, tt, :], r_g[tt], 8, 64, ti)
                    elif g == 2:
                        A("dve", lambda e, X=X, tt=tt: e.tensor_copy(V_ret[:, tt, :], X[:, 0:256]), r=[r_Xs[ti]], w=[r_vret[tt]])
                        A("dve", lambda e, X=X, tt=tt: e.tensor_copy(Vg[:, 4 + tt, 0:3:2, :], X[:, 256:384].rearrange("p (g d) -> p g d", g=2)),
                          r=[r_Xs[ti]], w=[r_Vg[4 + tt]])
                        A("act", lambda e, X=X, sg=sg: e.copy(sg[:, 128:256], X[:, 256:384]), r=[r_Xs[ti]], w=[r_stg[1]])
                        A("sp", lambda e, sg=sg, tt=tt: e.dma_start(out=o_gv[l, tt * 128:(tt + 1) * 128, :], in_=sg[:, 128:256]), r=[r_stg[1]], dma="stg1")
                        sm = small[ti]; junk = T1[ti]
                        A("dve", lambda e, X=X, sm=sm, junk=junk: e.scalar_tensor_tensor(junk[:, 0:128], X[:, 384:512], 1.0, X[:, 384:512], ALU.mult, ALU.mult,
                                                                                        accum_out=sm[:, 0:1]), r=[r_Xs[ti]], w=[r_T1[ti], r_small[ti]])
                        yield
                        A("act", lambda e, sm=sm: e.activation(sm[:, 1:2], sm[:, 0:1], AF.Ln, bias=cbias[:, 4:5], scale=1.0), r=[r_small[ti], r_cb], w=[r_small[ti]])
                        yield
                        A("act", lambda e, sm=sm: e.activation(sm[:, 1:2], sm[:, 1:2], AF.Exp, scale=-0.5), r=[r_small[ti]], w=[r_small[ti]])
                        yield
                        A("dve", lambda e, X=X, sm=sm, sg=sg: e.scalar_tensor_tensor(sg[:, 256:384], X[:, 384:512], sm[:, 1:2], Gmkv[:], ALU.mult, ALU.mult),
                          r=[r_Xs[ti], r_small[ti], r_G], w=[r_stg[2]])
                        A("dve", lambda e, sg=sg, tt=tt: e.tensor_copy(ckvn[:, 4 + tt, :], sg[:, 256:384]), r=[r_stg[2]], w=[r_ckvn[4 + tt]])
                        A("sp", lambda e, sg=sg, tt=tt: e.dma_start(out=o_ckv[l, tt * 128:(tt + 1) * 128, :], in_=sg[:, 256:384]), r=[r_stg[2]], dma="stg2")
                    else:
                        sm = small[ti]; junk = T1[ti]
                        A("dve", lambda e, X=X, sm=sm, junk=junk: e.scalar_tensor_tensor(junk[:, 0:256], X[:, 0:256], 1.0, X[:, 0:256], ALU.mult, ALU.mult,
                                                                                        accum_out=sm[:, 0:1]), r=[r_Xs[ti]], w=[r_T1[ti], r_small[ti]])
                        yield
                        A("act", lambda e, sm=sm: e.activation(sm[:, 1:2], sm[:, 0:1], AF.Ln, bias=cbias[:, 5:6], scale=1.0), r=[r_small[ti], r_cb], w=[r_small[ti]])
                        yield
                        A("act", lambda e, sm=sm: e.activation(sm[:, 1:2], sm[:, 1:2], AF.Exp, scale=-0.5), r=[r_small[ti]], w=[r_small[ti]])
                        yield
                        A("dve", lambda e, X=X, sm=sm, tt=tt: e.scalar_tensor_tensor(cqn[:, tt, :], X[:, 0:256], sm[:, 1:2], Gmq[:], ALU.mult, ALU.mult),
                          r=[r_Xs[ti], r_small[ti], r_G], w=[r_cqn[tt]])
                        yield from rope_ops(X[:, 256:288], r_Xs[ti], ropeM[:, tt, 0, :], ropeM[:, tt, 1, :], r_ropeM, sg[:, 384:416], r_stg[3], 1, 32, ti, 256)
                        A("dve", lambda e, sg=sg, tt=tt: e.tensor_copy(kpeb[:, 4 + tt, :], sg[:, 384:416]), r=[r_stg[3]], w=[r_kpeb[4 + tt]])
                        A("sp", lambda e, sg=sg, tt=tt: e.dma_start(out=o_kpe[l, tt * 128:(tt + 1) * 128, :], in_=sg[:, 384:416]), r=[r_stg[3]], dma="stg3")
                    yield
                for t0 in (0, 4):
                    gens = []
                    for tt in range(t0, t0 + 4):
                        bi = (g * NT + tt) % 4
                        b, rb = bank(bi)
                        for kc in range(8):
                            A("pe", lambda e, b=b, wd=wd, kc=kc, tt=tt, wt=wt: e.matmul(b[:, 0:wd], lhsT=hyT[:, kc, tt * 128:(tt + 1) * 128], rhs=wt[:, kc, 0:wd],
                                                                                         start=(kc == 0), stop=(kc == 7)),
                              r=[r_hy[kc], wr], w=[rb])
                        gens.append(prep_gen(tt, b, rb, nxt()))
                    while gens:
                        alive = []
                        for gn in gens:
                            try:
                                next(gn)
                                alive.append(gn)
                            except StopIteration:
                                pass
                        gens = alive
                ws_issue()
            def tr_group(g):
                if g == 0:
                    items = []
                    for h in range(4):
                        p0 = 64 * (h % 2); pr = h // 2
                        items.append((lambda tt, h=h: qk_tm[:, tt, h * 64:(h + 1) * 64], lambda tt: r_qk[tt], 64,
                                      lambda half, p0=p0, pr=pr: qkT[p0:p0 + 64, pr, 0, half * 512:(half + 1) * 512], r_qkT[h], 8, 0))
                        items.append((lambda tt, h=h: qk_tm[:, tt, 256 + h * 64:256 + (h + 1) * 64], lambda tt: r_qk[tt], 64,
                                      lambda half, p0=p0, pr=pr: qkT[p0:p0 + 64, pr, 1, half * 512:(half + 1) * 512], r_qkT[h], 8, 0))
                    transposes(items, evac="act")
                    tap("qk%d" % l, qk_tm, [128, 8, 512], r_qk)
                elif g == 1:
                    items = []
                    for gg in range(2):
                        items.append((lambda tt, gg=gg: g_tm[:, tt, 384 + gg * 64:384 + (gg + 1) * 64], lambda tt: r_g[tt], 64,
                                      lambda half, gg=gg: KTg[0:64, gg, 512 + half * 512:512 + (half + 1) * 512], r_KTg[gg], 8, 0))
                    transposes(items, evac="act")
                elif g == 2:
                    transposes([(lambda tt: ckvn[:, 4 + tt, :], lambda tt: r_ckvn[4 + tt], 128,
                                 lambda half: ckvT[:, 512 + half * 512:512 + (half + 1) * 512], r_ckvT, 8, 0)])
                else:
                    transposes([
                        (lambda tt: cqn[:, tt, 0:128], lambda tt: r_cqn[tt], 128, lambda half: cqnT[:, 0, half * 512:(half + 1) * 512], r_cqnT, 8, 0),
                        (lambda tt: cqn[:, tt, 128:256], lambda tt: r_cqn[tt], 128, lambda half: cqnT[:, 1, half * 512:(half + 1) * 512], r_cqnT, 8, 0),
                        (lambda tt: kpeb[:, 4 + tt, :], lambda tt: r_kpeb[4 + tt], 32, lambda half: kpeT[:, 512 + half * 512:512 + (half + 1) * 512], r_kpeT, 8, 0),
                    ])
            def z_group(zp):
                wt, wr = ws_get()
                for j in range(4):
                    cc = zp * 4 + j
                    for hf in range(2):
                        b, rb = bank((j * 2 + hf) % 4)
                        for kc in range(8):
                            A("pe", lambda e, b=b, wt=wt, j=j, kc=kc, hf=hf: e.matmul(b[:], lhsT=wt[:, kc, j * 128:(j + 1) * 128],
                                                                                        rhs=hyT[:, kc, hf * 512:(hf + 1) * 512], start=(kc == 0), stop=(kc == 7)),
                              r=[wr, r_hy[kc]], w=[rb])
                        A("act", lambda e, b=b, cc=cc, hf=hf: e.activation(szT[:, cc, hf * 512:(hf + 1) * 512], b[:], AF.Silu), r=[rb], w=[r_sz[cc]])
                ws_issue()
            mm_group(0)
            mm_group(1)
            tr_group(0)
            mm_group(2)
            tr_group(1)
            mm_group(3)
            tr_group(2)
            z_group(0)
            tr_group(3)
            z_group(1)

        pend = [None]
        bg = [None]

        def bg_step(n=1):
            for _ in range(n):
                if bg[0] is None:
                    return
                try:
                    next(bg[0])
                except StopIteration:
                    bg[0] = None

        def bg_flush():
            while bg[0] is not None:
                bg_step()

        prep = [None]

        def prep_step():
            if prep[0] is None:
                return
            try:
                next(prep[0])
            except StopIteration:
                prep[0] = None

        def prep_flush():
            while prep[0] is not None:
                prep_step()

        pass_ctr = [0]

        def attend(QT_fn, r_Q, KT_fn, r_K, V_fn, r_V_fn, krows, scale, ymix_row0, par):
            po = 64 * par
            pd = 64 - po
            c = ymix_row0 // 128
            p0 = ymix_row0 % 128
            units = [(hf, kc) for hf in range(2) for kc in range(NK)]
            NU = len(units)
            obank = {}
            for hf in range(2):
                obank[hf] = bank(4 + (pass_ctr[0] % 2))
                pass_ctr[0] += 1

            def QK(u):
                hf, kc = units[u]
                pS, rS = bank(u % 4)
                A("pe", lambda e, pS=pS, kc=kc, hf=hf: e.matmul(pS, lhsT=KT_fn(kc), rhs=QT_fn(hf), start=True, stop=True),
                  r=[r_K, r_Q], w=[rS])
                pi = u % 8
                ptv = PT[pi // 2][:, (pi % 2) * 512:(pi % 2) * 512 + 512]
                A("act", lambda e, pS=pS, ptv=ptv: e.activation(ptv, pS, AF.Exp, scale=scale), r=[rS], w=[RA.cells[pi]])

            def PV(u):
                hf, kc = units[u]
                pi = u % 8
                ptv = PT[pi // 2][:, (pi % 2) * 512:(pi % 2) * 512 + 512]
                pOb, rOb = obank[hf]
                A("pe", lambda e, kc=kc, ptv=ptv, pOb=pOb: e.matmul(pOb, lhsT=V_fn(kc), rhs=ptv, start=(kc == 0), stop=(kc == NK - 1)),
                  r=[r_V_fn(kc), RA.cells[pi]], w=[rOb])

            def finish(hf):
                pOb, rOb = obank[hf]
                cs = slice(hf * 512, (hf + 1) * 512)
                A("act", lambda e: e.activation(big1[po:po + 64, cs], pOb[pd:pd + 64, :], AF.Ln), r=[rOb], w=[r_big1[hf]])
                A("dve", lambda e: e.tensor_copy(big2[po:po + 64, cs], pOb[po:po + 64, :]), r=[rOb], w=[r_big2[hf]])
                A("act", lambda e: e.activation(big1[po:po + 64, cs], big1[po:po + 64, cs], AF.Exp, scale=-1.0), r=[r_big1[hf]], w=[r_big1[hf]])

                def norm():
                    A("dve", lambda e: e.tensor_tensor(big2[p0:p0 + 64, cs], big2[po:po + 64, cs], big1[po:po + 64, cs], ALU.mult),
                      r=[r_big2[hf], r_big1[hf]], w=[r_big2[hf]])
                    A("pool", lambda e: e.tensor_tensor(hyT[p0:p0 + 64, c, cs], big2[p0:p0 + 64, cs], szT[p0:p0 + 64, c, cs], ALU.mult),
                      r=[r_big2[hf], r_sz[c]], w=[r_hy[c]])
                pend.append(norm)

            for u in range(min(3, NU)):
                QK(u)
            for u in range(NU):
                if u + 3 < NU:
                    QK(u + 3)
                PV(u)
                hf, kc = units[u]
                if kc == 2 and len(pend) > 1:
                    pend.pop(1)()
                if u % 4 == 1:
                    prep_step()
                if u % 2 == 1:
                    bg_step()
                if kc == NK - 1:
                    if hf == 1:
                        prep_flush()
                    finish(hf)

        def attend_flush():
            while len(pend) > 1:
                pend.pop(1)()

        def gqa_head_prep(h):
            s_ = h % 2
            yield from transposes_gen([(lambda tt, h=h: g_tm[:, tt, h * 64:(h + 1) * 64], lambda tt: r_g[tt], 64,
                                        lambda half, s_=s_: QTg[0:64, s_, half * 512:(half + 1) * 512], r_QTg[s_], 8, 0)], banks=(6,), evac="dve")

        def phase_gqa(l):
            bg[0] = ret_gen(l)
            for _ in gqa_head_prep(0):
                pass
            for h in range(6):
                g = h // 3
                s_ = h % 2
                if h + 1 < 6:
                    prep[0] = gqa_head_prep(h + 1)
                attend(lambda hf, s_=s_: QTg[:, s_, hf * 512:(hf + 1) * 512], r_QTg[s_],
                       lambda kc, g=g: KTg[:, g, kc * 128:(kc + 1) * 128], r_KTg[g],
                       lambda kc, g=g: Vg[:, kc, g:g + 2, :].rearrange("p x d -> p (x d)"), lambda kc: r_Vg[kc], 69, 0.125, 256 + 64 * h, g)
            attend_flush()
            bg_flush()

        def phase_mla_proj(l):
            def tile_gen(tt):
                b0, rb0 = bank(0 + 2 * (tt % 2)); b1, rb1 = bank(1 + 2 * (tt % 2))
                for hh, (b, rb) in enumerate(((b0, rb0), (b1, rb1))):
                    for kc in range(2):
                        A("pe", lambda e, b=b, kc=kc, tt=tt, hh=hh: e.matmul(b[:, 0:288], lhsT=cqnT[:, kc, tt * 128:(tt + 1) * 128],
                                                                              rhs=wuq[:, kc, hh * 288:(hh + 1) * 288], start=(kc == 0), stop=(kc == 1)),
                          r=[r_cqnT, r_wuq], w=[rb])
                yield
                ti = nxt()
                Xa = Xs[ti]; Xb = T1[ti]
                A("act", lambda e, Xa=Xa, b0=b0: e.copy(Xa[:, 0:288], b0[:, 0:288]), r=[rb0], w=[r_Xs[ti]])
                A("act", lambda e, Xb=Xb, b1=b1: e.copy(Xb[:, 0:288], b1[:, 0:288]), r=[rb1], w=[r_T1[ti]])
                yield
                fin = []
                for hh, (Xh, rX) in enumerate(((Xa, r_Xs[ti]), (Xb, r_T1[ti]))):
                    xv = Xh[:, 0:288].rearrange("p (h x) -> p h x", h=3)
                    dv = qm_tm[:, tt, hh * 3:(hh + 1) * 3, :]
                    A("dve", lambda e, xv=xv, dv=dv: e.tensor_copy(dv[:, :, 0:64], xv[:, :, 0:64]), r=[rX], w=[r_qm[tt]])
                    cosb = ropeM[:, tt, 0, :].unsqueeze(1).broadcast_to([128, 3, 32])
                    sinb = ropeM[:, tt, 1, :].unsqueeze(1).broadcast_to([128, 3, 32])
                    t1 = T2[ti][:, hh * 96:(hh + 1) * 96].rearrange("p (h x) -> p h x", h=3)
                    t2 = T2[ti][:, 192 + hh * 96:192 + (hh + 1) * 96].rearrange("p (h x) -> p h x", h=3)
                    A("dve", lambda e, t1=t1, xv=xv, cosb=cosb: e.tensor_tensor(t1, xv[:, :, 64:96], cosb, ALU.mult), r=[rX, r_ropeM], w=[r_T2[ti]])
                    x5 = xv[:, :, 64:96].rearrange("p h (a q s) -> p h a q s", a=2, q=2)
                    t5 = t2.rearrange("p h (a q s) -> p h a q s", a=2, q=2)
                    s5 = sinb.rearrange("p h (a q s) -> p h a q s", a=2, q=2)
                    A("dve", lambda e, t5=t5, x5=x5, s5=s5: e.tensor_tensor(t5[:, :, :, 0, :], x5[:, :, :, 1, :], s5[:, :, :, 0, :], ALU.mult),
                      r=[rX, r_ropeM], w=[r_T2[ti]])
                    A("dve", lambda e, t5=t5, x5=x5, s5=s5: e.tensor_tensor(t5[:, :, :, 1, :], x5[:, :, :, 0, :], s5[:, :, :, 1, :], ALU.mult),
                      r=[rX, r_ropeM], w=[r_T2[ti]])
                    fin.append((dv, t1, t2))
                yield
                for (dv, t1, t2) in fin:
                    A("dve", lambda e, dv=dv, t1=t1, t2=t2: e.tensor_tensor(dv[:, :, 64:96], t1, t2, ALU.add), r=[r_T2[ti]], w=[r_qm[tt]])
                yield

            def v_gen():
                A("dve", lambda e: e.memset(Vm[:, :, :, 1, :], 1.0), w=[r_Vm_all])
                for kc in range(NK):
                    b, rb = bank(4 + kc % 4)
                    A("pe", lambda e, b=b, kc=kc: e.matmul(b[:, 0:384], lhsT=ckvT[:, kc * 128:(kc + 1) * 128], rhs=wukv[:], start=True, stop=True),
                      r=[r_ckvT, r_wukv], w=[rb])
                    yield
                    if kc % 2 == 0:
                        A("act", lambda e, b=b, kc=kc: e.copy(Vm[:, kc, :, 0:3:2, :], b[:, 0:384].rearrange("p (r x d) -> p r x d", r=3, x=2)), r=[rb], w=[r_Vm[kc]])
                    else:
                        A("dve", lambda e, b=b, kc=kc: e.tensor_copy(Vm[:, kc, :, 0:3:2, :], b[:, 0:384].rearrange("p (r x d) -> p r x d", r=3, x=2)), r=[rb], w=[r_Vm[kc]])
                    yield

            vg = [v_gen()]

            def step(gn):
                try:
                    next(gn)
                    return True
                except StopIteration:
                    return False

            for t0 in range(0, NT, 2):
                gens = [tile_gen(t0), tile_gen(t0 + 1)]
                while gens:
                    gens = [gn for gn in gens if step(gn)]
                    if vg[0] is not None and not step(vg[0]):
                        vg[0] = None
            while vg[0] is not None:
                if not step(vg[0]):
                    vg[0] = None

        def mla_head_prep(h):
            s = h % 2
            yield from transposes_gen([(lambda tt, h=h: qm_tm[:, tt, h, :], lambda tt: r_qm[tt], 96,
                                        lambda half, s=s: QTm[s][0:96, half * 512:(half + 1) * 512], r_QTm[s], 8, 0)], banks=(6,), evac="dve")
            for kb in range(3):
                b, rb = bank(6)
                A("pe", lambda e, b=b, kb=kb, h=h: e.matmul(b[0:96, :], lhsT=wukp[:, h, :], rhs=ckvT[:, kb * 512:(kb + 1) * 512], start=True, stop=False),
                  r=[r_wukp, r_ckvT], w=[rb])
                A("pe", lambda e, b=b, kb=kb: e.matmul(b[0:96, :], lhsT=isel, rhs=kpeT[:, kb * 512:(kb + 1) * 512], start=False, stop=True),
                  r=[r_isel, r_kpeT], w=[rb])
                A("dve", lambda e, b=b, kb=kb, s=s: e.tensor_copy(KTm[s][0:96, kb * 512:(kb + 1) * 512], b[0:96, :]), r=[rb], w=[r_KTm[s]])
                yield

        def phase_mla(l):
            for s_ in range(2):
                A("pool", lambda e, s_=s_: e.dma_start(out=QTm[s_][96:101, :], in_=d_qmask), w=[r_QTm[s_]], dma="mq%d" % s_)
                A("pool", lambda e, s_=s_: e.dma_start(out=KTm[s_][96:101, :], in_=d_kmask), w=[r_KTm[s_]], dma="mk%d" % s_)
            if l + 1 < nlayers:
                bg[0] = mod_gen(l + 1)
            for _ in mla_head_prep(0):
                pass
            for h in range(6):
                s = h % 2
                if h + 1 < 6:
                    prep[0] = mla_head_prep(h + 1)
                attend(lambda hf, s=s: QTm[s][:, hf * 512:(hf + 1) * 512], r_QTm[s],
                       lambda kc, s=s: KTm[s][:, kc * 128:(kc + 1) * 128], r_KTm[s],
                       lambda kc, h=h: Vm[:, kc, h // 2, (h % 2):(h % 2) + 2, :].rearrange("p x d -> p (x d)"), lambda kc: r_Vm[kc], 101, float(96.0 ** -0.5), 640 + 64 * h, h % 2)
            attend_flush()
            bg_flush()

        def ret_gen(l):
            for a in range(2):
                NE = NEf if a == 0 else NEb
                for h in range(4):
                    col = l * 8 + a * 4 + h
                    A("act", lambda e, a=a, h=h, col=col, NE=NE: e.activation(intra[:, a, h, :], NE[:], AF.Exp, bias=cbias[:, 1:2], scale=nlg[:, col:col + 1]),
                      r=[r_NE, r_nlg, r_cb], w=[r_intra])
                yield
            A("dve", lambda e: e.tensor_tensor(intra[:, 0, :, :], intra[:, 0, :, :], intra[:, 1, :, :], ALU.add), r=[r_intra], w=[r_intra])
            yield
            cdf = cdec.rearrange("d a e -> d (a e)")
            for t in range(8):
                pk, rk = bank(7)
                for a in range(2):
                    ch = t if a == 0 else 7 - t
                    for h in range(4):
                        A("pe", lambda e, pk=pk, a=a, h=h, ch=ch: e.matmul(pk[0:64, (a * 4 + h) * 64:(a * 4 + h + 1) * 64], lhsT=kd_tm[:, ch, a, h, :],
                                                                           rhs=V_ret[:, ch, h * 64:(h + 1) * 64], start=True, stop=True),
                          r=[r_kd[ch], r_vret[ch]], w=[rk])
                if t == 0:
                    A("dve", lambda e: e.tensor_copy(S_in[0:64, 0, :], S_aft), r=[r_Saft], w=[r_Sin[0]])
                    A("dve", lambda e: e.tensor_copy(S_in[64:128, 0, :], S_aft), r=[r_Saft], w=[r_Sin[0]])
                    A("dve", lambda e: e.tensor_tensor(S_tmp, S_aft, cdf, ALU.mult), r=[r_Saft, r_cdec], w=[r_Stmp])
                else:
                    A("dve", lambda e, t=t: e.tensor_scalar(S_in[0:64, t, :], S_aft, mcar[0:64, t:t + 1], None, ALU.mult), r=[r_Saft, r_NE], w=[r_Sin[t]])
                    A("dve", lambda e, t=t: e.tensor_scalar(S_in[64:128, t, :], S_aft, mcar[0:64, t:t + 1], None, ALU.mult), r=[r_Saft, r_NE], w=[r_Sin[t]])
                    A("dve", lambda e, t=t: e.scalar_tensor_tensor(S_tmp, S_aft, mcar[0:64, t:t + 1], cdf, ALU.mult, ALU.mult),
                      r=[r_Saft, r_cdec, r_NE], w=[r_Stmp])
                yield
                A("dve", lambda e, pk=pk: e.tensor_tensor(S_aft, S_tmp, pk[0:64, :], ALU.add), r=[r_Stmp, rk], w=[r_Saft])
                if t % 2 == 1:
                    A("dve", lambda e: e.tensor_copy(S_out, S_aft), r=[r_Saft], w=[r_Sout])
                    A("sp", lambda e, t=t: e.dma_start(out=o_st[l, t // 2], in_=S_out), r=[r_Sout], dma="sto")
                yield
            for h in range(4):
                p0 = 64 * (h % 2); pr = h // 2
                qTh = qkT[p0:p0 + 64, pr, 0, :]; kTh = qkT[p0:p0 + 64, pr, 1, :]
                for a in range(2):
                    A("pool", lambda e, a=a, h=h, p0=p0, qTh=qTh: e.tensor_tensor(qd[p0:p0 + 64, a, :].rearrange("d (c i) -> d c i", c=8), qTh.rearrange("d (c i) -> d c i", c=8),
                                                                                 qdec[p0:p0 + 64, a, h, :].unsqueeze(1).broadcast_to([64, 8, 128]), ALU.mult),
                      r=[r_qkT[h], r_qdec], w=[r_qd])
                yield
                c = (64 * h) // 128
                Oh = T2[0][p0:p0 + 64, :]; rOh = r_T2[0]
                W2 = T2[1][p0:p0 + 64, :]; rW2 = r_T2[1]
                W2b = sqb[1][p0:p0 + 64, 0:512]
                for half in range(2):
                    pq, rq = bank(7)
                    for j in range(4):
                        ch = half * 4 + j
                        A("pe", lambda e, pq=pq, j=j, ch=ch, kTh=kTh, qTh=qTh: e.matmul(pq[:, j * 128:(j + 1) * 128], lhsT=kTh[:, ch * 128:(ch + 1) * 128],
                                                                                         rhs=qTh[:, ch * 128:(ch + 1) * 128], start=True, stop=True),
                          r=[r_qkT[h]], w=[rq])
                    at = AT[half]; rat = r_AT[half]
                    A("dve", lambda e, at=at, pq=pq, h=h: e.tensor_tensor(at[:, 0, :].rearrange("j (c i) -> j c i", c=4), pq.rearrange("j (c i) -> j c i", c=4),
                                                                          intra[:, 0, h, :].unsqueeze(1).broadcast_to([128, 4, 128]), ALU.mult),
                      r=[rq, r_intra], w=[rat])
                    yield
                    pO, rO = bank(7)
                    for j in range(4):
                        ch = half * 4 + j
                        cols = slice(j * 128, (j + 1) * 128)
                        A("pe", lambda e, cols=cols, ch=ch, h=h, at=at, j=j, pO=pO: e.matmul(pO[0:64, cols], lhsT=V_ret[:, ch, h * 64:(h + 1) * 64],
                                                                                              rhs=at[:, 0, j * 128:(j + 1) * 128], start=True, stop=False),
                          r=[r_vret[ch], rat], w=[rO])
                        A("pe", lambda e, cols=cols, ch=ch, h=h, p0=p0, pO=pO: e.matmul(pO[0:64, cols], lhsT=S_in[p0:p0 + 64, ch, h * 64:(h + 1) * 64],
                                                                                         rhs=qd[p0:p0 + 64, 0, ch * 128:(ch + 1) * 128], start=False, stop=False),
                          r=[r_Sin[ch], r_qd], w=[rO])
                        A("pe", lambda e, cols=cols, ch=ch, h=h, p0=p0, pO=pO: e.matmul(pO[0:64, cols], lhsT=S_in[p0:p0 + 64, 7 - ch, (4 + h) * 64:(5 + h) * 64],
                                                                                         rhs=qd[p0:p0 + 64, 1, ch * 128:(ch + 1) * 128], start=False, stop=True),
                          r=[r_Sin[7 - ch], r_qd], w=[rO])
                    yield
                    A("dve", lambda e, Oh=Oh, pO=pO: e.tensor_copy(Oh, pO[0:64, :]), r=[rO], w=[rOh])
                    A("pool", lambda e, Oh=Oh, W2b=W2b: e.tensor_tensor(W2b, Oh, Oh, ALU.mult), r=[rOh], w=[rW2])
                    yield
                    pss, rss = bank(7)
                    A("pe", lambda e, pss=pss, W2b=W2b, p0=p0: e.matmul(pss[0:64, :], lhsT=ones64[p0:p0 + 64, :], rhs=W2b, start=True, stop=True),
                      r=[rW2, r_o64], w=[rss])
                    A("act", lambda e, W2=W2, pss=pss, p0=p0: e.activation(W2, pss[0:64, :], AF.Ln, bias=cbias[p0:p0 + 64, 0:1], scale=1.0), r=[rss, r_cb], w=[rW2])
                    A("act", lambda e, W2=W2: e.activation(W2, W2, AF.Exp, scale=-0.5), r=[rW2], w=[rW2])
                    yield
                    A("dve", lambda e, Oh=Oh, W2=W2: e.tensor_tensor(Oh, Oh, W2, ALU.mult), r=[rOh, rW2], w=[rOh])
                    A("pool", lambda e, Oh=Oh, p0=p0, c=c, half=half: e.tensor_tensor(hyT[p0:p0 + 64, c, half * 512:(half + 1) * 512], Oh,
                                                                                    szT[p0:p0 + 64, c, half * 512:(half + 1) * 512], ALU.mult),
                      r=[rOh, r_sz[c]], w=[r_hy[c]])
                    yield

        def phase_ret(l):
            for _ in ret_gen(l):
                pass

        def phase_outproj(l):
            m = modl[l]
            for pc in range(2):
                wt, wr = ws_get()
                for j in range(4):
                    cc = pc * 4 + j
                    for hf in range(2):
                        b, rb = bank((j * 2 + hf) % 8)
                        for kc in range(8):
                            A("pe", lambda e, b=b, wt=wt, j=j, kc=kc, hf=hf: e.matmul(b[:], lhsT=wt[:, kc, j * 128:(j + 1) * 128],
                                                                                        rhs=hyT[:, kc, hf * 512:(hf + 1) * 512], start=(kc == 0), stop=(kc == 7)),
                              r=[wr, r_hy[kc]], w=[rb])
                        A("dve", lambda e, b=b, cc=cc, hf=hf: e.scalar_tensor_tensor(xT[:, cc, hf * 512:(hf + 1) * 512], b[:], m[:, 16 + cc:17 + cc],
                                                                                       xT[:, cc, hf * 512:(hf + 1) * 512], ALU.mult, ALU.add),
                          r=[rb, r_modl[l], r_xT[cc]], w=[r_xT[cc]])
                ws_issue()

        def phase_final():
            sumsq_bc(big1[:], r_big1, lambda c: xT[:, c, :], r_xT, 8, ones_mean, r_om, 128, EPS, 0)
            tap("rstdF", big1, [128, T], [r_big1])
            for c in range(8):
                A("dve", lambda e, c=c: e.scalar_tensor_tensor(xT[:, c, :], xT[:, c, :], vT[:, 8 + c:9 + c], big1[:], ALU.mult, ALU.mult),
                  r=[r_xT[c], r_vT, r_big1], w=[r_xT[c]])
            for tt in range(NT):
                s = tt % 2
                for hf in range(2):
                    b, rb = bank(2 * s + hf + 4 * 0)
                    for j in range(4):
                        c = hf * 4 + j
                        A("pe", lambda e, b=b, j=j, c=c, tt=tt: e.transpose(b[:, j * 128:(j + 1) * 128], xT[:, c, tt * 128:(tt + 1) * 128], ident_f[:]),
                          r=[r_xT[c], r_idf], w=[rb])
                    if hf == 0:
                        A("dve", lambda e, b=b, s=s, hf=hf: e.tensor_copy(xst[s][:, hf * 512:(hf + 1) * 512], b[:]), r=[rb], w=[r_xst[s]])
                    else:
                        A("act", lambda e, b=b, s=s, hf=hf: e.copy(xst[s][:, hf * 512:(hf + 1) * 512], b[:]), r=[rb], w=[r_xst[s]])
                A("sp", lambda e, s=s, tt=tt: e.dma_start(out=o_y[tt * 128:(tt + 1) * 128, :], in_=xst[s][:]), r=[r_xst[s]], dma="xst%d" % s)

        tap("xin", xT[:], [128, 8, T], r_xT)
        tap("vT", vT[:], [128, 144], [r_vT])
        if upto >= 1:
            load_small(0, skip_gkc=True)
            mg0 = mod_gen(0)

            def _cb():
                for _ in range(3):
                    try:
                        next(mg0)
                    except StopIteration:
                        pass
            x_load(_cb)
            load_small(0, gkc_only=True)
            for _ in mg0:
                pass
        else:
            x_load(lambda: None)
        for l in range(nlayers):
            if upto < 1:
                break
            if l > 0:
                load_small(l)
            phase_consts(l)
            phase_norm(l)
            tap("hT%d" % l, hyT[:], [128, 8, T], r_hy)
            if upto < 2:
                break
            phase_inproj(l)
            tap("g%d" % l, g_tm, [128, 8, 512], r_g)
            tap("sz%d" % l, szT[:], [128, 8, T], r_sz)
            if upto < 3:
                break
            if upto < 4:
                phase_ret(l)
                tap("yT%d" % l, hyT[:], [128, 8, T], r_hy)
                break
            phase_gqa(l)
            if upto < 5:
                tap("yT%d" % l, hyT[:], [128, 8, T], r_hy)
                break
            phase_mla_proj(l)
            tap("qm%d" % l, qm_all, [128, 4608], r_qm)
            tap("vm%d" % l, Vm_all, [128, 6912], r_Vm)
            tap("ckvT%d" % l, ckvT[:], [128, 1536], [r_ckvT])
            tap("kpeT%d" % l, kpeT, [32, 1536], [r_kpeT])
            phase_mla(l)
            tap("ktm%d" % l, KTm[1], [101, 1536], r_KTm[1])
            tap("qtm%d" % l, QTm[1], [101, 1024], r_QTm[1])
            tap("yT%d" % l, hyT[:], [128, 8, T], r_hy)
            if upto < 6:
                break
            phase_outproj(l)
            tap("xT%d" % l, xT[:], [128, 8, T], r_xT)
        phase_final()
        P.emit(nc)
    return nc, P, tap_out


def _rope_tables(n_tok, dim, grid_w=64, theta=10000.0):
    rows = n_tok // grid_w
    row = np.repeat(np.arange(rows), grid_w).astype(np.float32)
    col = (np.arange(n_tok) % grid_w).astype(np.float32)
    half = dim // 2
    inv = (theta ** (-np.arange(0, half, 2, dtype=np.float32) / half)).astype(np.float32)
    ar = row[:, None] * inv[None]
    ac = col[:, None] * inv[None]
    ang = np.concatenate([ar, ar, ac, ac], -1)
    cos = np.cos(ang).astype(np.float32)
    sin = np.sin(ang).astype(np.float32)
    qd = half // 2
    sgn = np.concatenate([-np.ones(qd), np.ones(qd), -np.ones(qd), np.ones(qd)]).astype(np.float32)
    return np.stack([cos, sin * sgn[None]], 1)


def _const_tables():
    i = np.arange(128, dtype=np.float32)
    rel = i[None, :] - i[:, None]
    ne_f = np.where(rel >= 0, -rel, -1.0e7).astype(np.float32)
    ne_b = np.where(rel <= 0, rel, -1.0e7).astype(np.float32)
    ne_q = np.concatenate([-(i + 1.0), -(128.0 - i)]).astype(np.float32)
    ne_k = np.stack([-(127.0 - i), -i], 1).astype(np.float32)
    return ne_f, ne_b, ne_q, ne_k


def make_in_maps(inputs):
    f = lambda a: np.ascontiguousarray(np.asarray(a, dtype=np.float32))
    xp = f(inputs["x_prompt"]); xs = f(inputs["x_sample"])
    ne_f, ne_b, ne_q, ne_k = _const_tables()
    shared = {
        "w_ada": f(inputs["w_ada"]), "b_ada": f(inputs["b_ada"]).reshape(96, 128), "w_in": f(inputs["w_in"]),
        "w_uq": f(inputs["w_uq"]), "w_ukv": f(inputs["w_ukv"]), "w_out": f(inputs["w_out"]),
        "norm_g": f(inputs["norm_g"]).reshape(32, 128), "final_norm": f(inputs["final_norm"]).reshape(8, 128),
        "ret_logit": f(inputs["ret_decay_logit"]).reshape(32), "gq_g": f(inputs["gqa_q_norm"]).reshape(256),
        "gk_g": f(inputs["gqa_k_norm"]).reshape(256), "mq_g": f(inputs["mla_q_norm"]).reshape(1024),
        "mkv_g": f(inputs["mla_kv_norm"]).reshape(512),
        "ne_f": ne_f, "ne_b": ne_b, "ne_q": ne_q, "ne_k": ne_k,
    }
    kmask = np.zeros((5, 1536), np.float32)
    kmask[4, 0:512] = 1.0
    for s in range(4):
        kmask[s, 512 + 256 * s:512 + 256 * (s + 1)] = 1.0
    rope_s_hd = _rope_tables(1024, 64); rope_s_m = _rope_tables(1024, 32)
    rope_p_hd = np.zeros((1024, 2, 64), np.float32); rope_p_hd[:, 0] = 1.0
    rope_p_m = np.zeros((1024, 2, 32), np.float32); rope_p_m[:, 0] = 1.0
    maps = []
    for core in range(8):
        m = dict(shared)
        m["kmask"] = kmask
        if core < 4:
            b = core
            m["xin"] = xs[b]
            m["cond"] = f(inputs["c"])[b].reshape(8, 128)
            m["c_gk"] = f(inputs["cache_gqa_k"])[b].reshape(4, 512, 128)
            m["c_gv"] = f(inputs["cache_gqa_v"])[b].reshape(4, 512, 128)
            m["c_ckv"] = f(inputs["cache_mla_ckv"])[b]
            m["c_kpe"] = f(inputs["cache_mla_kpe"])[b]
            m["state0"] = f(inputs["state_ret"])[b]
            m["rope_hd"] = rope_s_hd; m["rope_m"] = rope_s_m
            m["qmask"] = np.zeros((5, 1024), np.float32)
            m["carry"] = np.ones(8, np.float32)
        else:
            b0 = 4 * (core - 4)
            m["xin"] = xp[b0:b0 + 4].reshape(1024, 1024)
            m["cond"] = f(inputs["c_ctx"]).reshape(8, 128)
            m["c_gk"] = np.zeros((4, 512, 128), np.float32)
            m["c_gv"] = np.zeros((4, 512, 128), np.float32)
            m["c_ckv"] = np.zeros((4, 512, 128), np.float32)
            m["c_kpe"] = np.zeros((4, 512, 32), np.float32)
            m["state0"] = np.zeros((4, 2, 4, 64, 64), np.float32)
            m["rope_hd"] = rope_p_hd; m["rope_m"] = rope_p_m
            qm = np.full((5, 1024), NEG_BIG, np.float32)
            for s in range(4):
                qm[s, 256 * s:256 * (s + 1)] = 0.0
            m["qmask"] = qm
            m["carry"] = np.array([1, 1, 0, 1, 0, 1, 0, 1], np.float32)
        maps.append({k: np.ascontiguousarray(v) for k, v in m.items()})
    return maps


def assemble(results):
    y_prompt = np.zeros((16, 256, 1024), np.float32)
    y_sample = np.zeros((4, 1024, 1024), np.float32)
    st = np.zeros((16, 4, 2, 4, 64, 64), np.float32)
    gk = np.zeros((16, 4, 256, 2, 64), np.float32)
    gv = np.zeros((16, 4, 256, 2, 64), np.float32)
    ckv = np.zeros((16, 4, 256, 128), np.float32)
    kpe = np.zeros((16, 4, 256, 32), np.float32)
    for core in range(8):
        r = results[core]
        if core < 4:
            y_sample[core] = r["y"]
            continue
        b0 = 4 * (core - 4)
        y_prompt[b0:b0 + 4] = r["y"].reshape(4, 256, 1024)
        so = r["st_out"].reshape(4, 4, 64, 2, 4, 64)
        for s in range(4):
            st[b0 + s, :, 0] = so[:, s, :, 0].transpose(0, 2, 1, 3)
            st[b0 + s, :, 1] = so[:, 3 - s, :, 1].transpose(0, 2, 1, 3)
        gk[b0:b0 + 4] = r["gk_out"].reshape(4, 4, 256, 2, 64).transpose(1, 0, 2, 3, 4)
        gv[b0:b0 + 4] = r["gv_out"].reshape(4, 4, 256, 2, 64).transpose(1, 0, 2, 3, 4)
        ckv[b0:b0 + 4] = r["ckv_out"].reshape(4, 4, 256, 128).transpose(1, 0, 2, 3)
        kpe[b0:b0 + 4] = r["kpe_out"].reshape(4, 4, 256, 32).transpose(1, 0, 2, 3)
    return (y_prompt, y_sample, st, gk, gv, ckv, kpe)


_CACHE = {}


def kernel(**inputs):
    if "nc" not in _CACHE:
        _CACHE["nc"] = build()[0]
    nc = _CACHE["nc"]
    maps = make_in_maps(inputs)
    res = run_bass_kernel_spmd(nc, maps, core_ids=list(range(8)))
    return assemble(res.results)
```
